# Optimizing a Trainium2 kernel written in Bass

```python
import math
import jax, jax.numpy as jnp
from jax import lax
import numpy as np

D_MODEL = 1024
BATCH = 8
SEQ = 4096
DEPTH = 2

MIX_WIDTH = D_MODEL
LRU_WIDTH = MIX_WIDTH // 2
LRU_BLOCKS = 8
LRU_BLOCK_DIM = LRU_WIDTH // LRU_BLOCKS
CONV_WIDTH = 4
LRU_C = 8.0
N_HEADS = 8
HEAD_DIM = (MIX_WIDTH - LRU_WIDTH) // N_HEADS
N_KV = 2
Q_PER_KV = N_HEADS // N_KV
NSA_WIDTH = N_HEADS * HEAD_DIM
KV_WIDTH = N_KV * HEAD_DIM
N_BRANCHES = 3
CMP_BLOCK = 32
CMP_STRIDE = 16
SEL_BLOCK = 64
SEL_TOPN = 16
WINDOW = 512
Q_BLOCK = 128
FORCE_BONUS = 1e4
NEG = -1e30
PEER_HEADS = 8
PEER_KEYS = 128
PEER_TOPK = 16
PEER_KEY_DIM = 128
N_EXPERTS = PEER_KEYS * PEER_KEYS
PEER_CHUNK = 512
IN_SPLITS = (LRU_WIDTH, LRU_WIDTH, NSA_WIDTH, KV_WIDTH, KV_WIDTH, KV_WIDTH, KV_WIDTH, KV_WIDTH, KV_WIDTH, N_HEADS * N_BRANCHES)
IN_COLS = sum(IN_SPLITS)
ALPHA = (2 * DEPTH) ** 0.25
BETA = (8 * DEPTH) ** -0.25
LN_EPS = 1e-5

kernel_name = 'hybrid_rglru_nsa_peer_deepnorm'


def _layer_norm(x, g, b):
    xf = x.astype(jnp.float32)
    mu = jnp.mean(xf, -1, keepdims=True)
    var = jnp.mean(jnp.square(xf - mu), -1, keepdims=True)
    return ((xf - mu) * lax.rsqrt(var + LN_EPS) * g.astype(jnp.float32) + b.astype(jnp.float32)).astype(x.dtype)


def _rms_norm(x, g):
    xf = x.astype(jnp.float32)
    return (xf * lax.rsqrt(jnp.mean(jnp.square(xf), -1, keepdims=True) + LN_EPS) * g.astype(jnp.float32)).astype(x.dtype)


def _alibi_slopes(n):
    return jnp.asarray(np.array([2.0 ** (-8.0 * (h + 1) / n) for h in range(n)], np.float32))


def _rg_lru(xb, gate, conv_w, conv_b, wa, ba, wx, bx, lam):
    B, S, C = xb.shape
    xc = lax.conv_general_dilated(xb, conv_w[:, None, :], window_strides=(1,), padding=[(CONV_WIDTH - 1, 0)],
                                  dimension_numbers=('NWC', 'WIO', 'NWC'), feature_group_count=C) + conv_b
    xr = xc.reshape(B, S, LRU_BLOCKS, LRU_BLOCK_DIM)
    r = jax.nn.sigmoid(jnp.einsum('bsnd,nde->bsne', xr, wa) + ba.reshape(LRU_BLOCKS, LRU_BLOCK_DIM)).reshape(B, S, C)
    i = jax.nn.sigmoid(jnp.einsum('bsnd,nde->bsne', xr, wx) + bx.reshape(LRU_BLOCKS, LRU_BLOCK_DIM)).reshape(B, S, C)
    log_a = -LRU_C * r.astype(jnp.float32) * jax.nn.softplus(-lam.astype(jnp.float32))
    a = jnp.exp(log_a)
    u = jnp.sqrt(-jnp.expm1(2.0 * log_a)) * (i * xc).astype(jnp.float32)

    def combine(left, right):
        a1, b1 = left
        a2, b2 = right
        return a1 * a2, a2 * b1 + b2

    _, h = lax.associative_scan(combine, (a, u), axis=1)
    return h.astype(xb.dtype) * jax.nn.gelu(gate)


def _compress(k, pos, w1, b1, w2, b2):
    B, S, G, hd = k.shape
    nc = (S - CMP_BLOCK) // CMP_STRIDE + 1
    idx = np.arange(nc)[:, None] * CMP_STRIDE + np.arange(CMP_BLOCK)[None, :]
    blk = k[:, idx] + pos[None, None, :, None, :]
    flat = blk.transpose(0, 1, 3, 2, 4).reshape(B, nc, G, CMP_BLOCK * hd)
    return jax.nn.gelu(flat @ w1 + b1) @ w2 + b2


def _selection_matrix(n_cmp, n_sel):
    cs = np.arange(n_cmp)[:, None] * CMP_STRIDE
    ss = np.arange(n_sel)[None, :] * SEL_BLOCK
    ov = np.clip(np.minimum(cs + CMP_BLOCK, ss + SEL_BLOCK) - np.maximum(cs, ss), 0, None)
    return (ov / CMP_STRIDE).astype(np.float32)


def _nsa(q, kc, vc, ks, vs, kw, vw, gates):
    B, S = q.shape[:2]
    G, R, hd = N_KV, Q_PER_KV, HEAD_DIM
    nb = S // Q_BLOCK
    qb_all = (q * (hd ** -0.5)).reshape(B, nb, Q_BLOCK, G, R, hd).transpose(1, 0, 2, 3, 4, 5)
    g_all = jax.nn.sigmoid(gates).reshape(B, nb, Q_BLOCK, G, R, N_BRANCHES).transpose(1, 0, 2, 3, 4, 5)
    slopes = _alibi_slopes(N_HEADS).reshape(G, R)
    nc = kc.shape[1]
    cmp_end = jnp.arange(nc) * CMP_STRIDE + CMP_BLOCK - 1
    n_sel = S // SEL_BLOCK
    n_top = min(SEL_TOPN, n_sel)
    sel_mat = jnp.asarray(_selection_matrix(nc, n_sel))
    ks_blk = ks.transpose(0, 2, 1, 3).reshape(B, G, n_sel, SEL_BLOCK, hd)
    vs_blk = vs.transpose(0, 2, 1, 3).reshape(B, G, n_sel, SEL_BLOCK, hd)
    kw_pad = jnp.pad(kw, ((0, 0), (WINDOW, 0), (0, 0), (0, 0)))
    vw_pad = jnp.pad(vw, ((0, 0), (WINDOW, 0), (0, 0), (0, 0)))
    gather = jax.vmap(jax.vmap(lambda blk, ix: blk[ix]))
    blk_id = jnp.arange(n_sel)

    def one_block(args):
        qb, q_blk, g_blk = args
        t = qb * Q_BLOCK + jnp.arange(Q_BLOCK)
        s = jnp.einsum('bqgrd,bcgd->bgrqc', q_blk, kc).astype(jnp.float32)
        dist_c = (t[:, None] - cmp_end[None, :]).astype(jnp.float32)
        s = s - slopes[:, :, None, None] * jnp.abs(dist_c)
        valid_c = cmp_end[None, :] <= t[:, None]
        p_c = jax.nn.softmax(jnp.where(valid_c, s, NEG), axis=-1) * valid_c.any(-1)[:, None].astype(jnp.float32)
        o_c = jnp.einsum('bgrqc,bcgd->bqgrd', p_c.astype(vc.dtype), vc)
        imp = jnp.einsum('bgrqc,cj->bgqj', p_c, sel_mat)
        cur = t // SEL_BLOCK
        vblk = blk_id[None, :] <= cur[:, None]
        forced = (blk_id[None, :] == 0) | (blk_id[None, :] == cur[:, None]) | (blk_id[None, :] == cur[:, None] - 1)
        imp = jnp.where(vblk, imp + jnp.where(forced, FORCE_BONUS, 0.0), NEG)
        _, idx = lax.top_k(imp, n_top)
        k_sel = gather(ks_blk, idx)
        v_sel = gather(vs_blk, idx)
        tok = idx[..., None] * SEL_BLOCK + jnp.arange(SEL_BLOCK)
        dist_s = (t[None, None, :, None, None] - tok)[:, :, None]
        s = jnp.einsum('bqgrd,bgqnld->bgrqnl', q_blk, k_sel).astype(jnp.float32)
        s = s - slopes[None, :, :, None, None, None] * jnp.abs(dist_s).astype(jnp.float32)
        s = jnp.where(dist_s >= 0, s, NEG)
        p_s = jax.nn.softmax(s.reshape(s.shape[:4] + (n_top * SEL_BLOCK,)), axis=-1).reshape(s.shape)
        o_s = jnp.einsum('bgrqnl,bgqnld->bqgrd', p_s.astype(v_sel.dtype), v_sel)
        k_w = lax.dynamic_slice_in_dim(kw_pad, qb * Q_BLOCK, WINDOW + Q_BLOCK, axis=1)
        v_w = lax.dynamic_slice_in_dim(vw_pad, qb * Q_BLOCK, WINDOW + Q_BLOCK, axis=1)
        pos = qb * Q_BLOCK - WINDOW + jnp.arange(WINDOW + Q_BLOCK)
        dist_w = t[:, None] - pos[None, :]
        valid_w = (dist_w >= 0) & (dist_w < WINDOW) & (pos >= 0)[None, :]
        s = jnp.einsum('bqgrd,bkgd->bgrqk', q_blk, k_w).astype(jnp.float32)
        s = s - slopes[:, :, None, None] * dist_w.astype(jnp.float32)
        p_w = jax.nn.softmax(jnp.where(valid_w, s, NEG), axis=-1)
        o_w = jnp.einsum('bgrqk,bkgd->bqgrd', p_w.astype(v_w.dtype), v_w)
        return g_blk[..., 0:1] * o_c + g_blk[..., 1:2] * o_s + g_blk[..., 2:3] * o_w

    out = lax.map(one_block, (jnp.arange(nb), qb_all, g_all))
    return out.transpose(1, 0, 2, 3, 4, 5).reshape(B, S, N_HEADS * hd)


def _peer(x, wq, subkeys, u_tab, v_tab):
    B, S, D = x.shape
    T = B * S
    xf = x.reshape(T, D)
    q = (xf @ wq).reshape(T, PEER_HEADS, 2, PEER_KEY_DIM)
    s = jnp.einsum('thcd,ckd->thck', q, subkeys).astype(jnp.float32)
    sv, si = lax.top_k(s, PEER_TOPK)
    cand = (sv[:, :, 0, :, None] + sv[:, :, 1, None, :]).reshape(T, PEER_HEADS, PEER_TOPK * PEER_TOPK)
    cv, ci = lax.top_k(cand, PEER_TOPK)
    ea = jnp.take_along_axis(si[:, :, 0], ci // PEER_TOPK, axis=-1)
    eb = jnp.take_along_axis(si[:, :, 1], ci % PEER_TOPK, axis=-1)
    experts = (ea * PEER_KEYS + eb).reshape(T, PEER_HEADS * PEER_TOPK)
    gate = jax.nn.softmax(cv, axis=-1).reshape(T, PEER_HEADS * PEER_TOPK).astype(x.dtype)
    chunk = math.gcd(T, PEER_CHUNK)
    nch = T // chunk

    def one_chunk(args):
        xc, ec, gc = args
        act = jax.nn.gelu(jnp.einsum('cd,ced->ce', xc, u_tab[ec]))
        return jnp.einsum('ce,ced->cd', act * gc, v_tab[ec])

    y = lax.map(one_chunk, (xf.reshape(nch, chunk, D), experts.reshape(nch, chunk, -1), gate.reshape(nch, chunk, -1)))
    return y.reshape(B, S, D)


def setup_inputs(seed: int = 0) -> dict:
    key = jax.random.key(seed)
    ks = jax.random.split(key, 40)
    L = DEPTH

    def nrm(k, shape, scale):
        return jax.random.normal(k, shape, jnp.float32) * scale

    col_scale = np.concatenate([np.full(n, s, np.float32) for n, s in zip(
        IN_SPLITS, (BETA, 1.0, 1.0, 1.0, BETA, 1.0, BETA, 1.0, BETA, 1.0))])
    u = jax.random.uniform(ks[9], (L, LRU_WIDTH), jnp.float32, 0.9, 0.999)
    a0 = u ** (1.0 / LRU_C)
    fan_c = CMP_BLOCK * HEAD_DIM
    return {
        'x': nrm(ks[0], (BATCH, SEQ, D_MODEL), 1.0),
        'w_in': nrm(ks[1], (L, D_MODEL, IN_COLS), D_MODEL ** -0.5) * jnp.asarray(col_scale),
        'b_in': nrm(ks[2], (L, IN_COLS), 0.01),
        'conv_w': nrm(ks[3], (L, CONV_WIDTH, LRU_WIDTH), CONV_WIDTH ** -0.5),
        'conv_b': nrm(ks[4], (L, LRU_WIDTH), 0.01),
        'lru_wa': nrm(ks[5], (L, LRU_BLOCKS, LRU_BLOCK_DIM, LRU_BLOCK_DIM), LRU_BLOCK_DIM ** -0.5),
        'lru_ba': nrm(ks[6], (L, LRU_WIDTH), 0.01),
        'lru_wx': nrm(ks[7], (L, LRU_BLOCKS, LRU_BLOCK_DIM, LRU_BLOCK_DIM), LRU_BLOCK_DIM ** -0.5),
        'lru_bx': nrm(ks[8], (L, LRU_WIDTH), 0.01),
        'lru_lambda': jnp.log(a0) - jnp.log1p(-a0),
        'cmp_pos_k': nrm(ks[10], (L, CMP_BLOCK, HEAD_DIM), 0.02),
        'cmpk_w1': nrm(ks[11], (L, fan_c, HEAD_DIM), fan_c ** -0.5),
        'cmpk_b1': nrm(ks[12], (L, HEAD_DIM), 0.01),
        'cmpk_w2': nrm(ks[13], (L, HEAD_DIM, HEAD_DIM), HEAD_DIM ** -0.5),
        'cmpk_b2': nrm(ks[14], (L, HEAD_DIM), 0.01),
        'cmp_pos_v': nrm(ks[15], (L, CMP_BLOCK, HEAD_DIM), 0.02),
        'cmpv_w1': nrm(ks[16], (L, fan_c, HEAD_DIM), fan_c ** -0.5),
        'cmpv_b1': nrm(ks[17], (L, HEAD_DIM), 0.01),
        'cmpv_w2': nrm(ks[18], (L, HEAD_DIM, HEAD_DIM), HEAD_DIM ** -0.5),
        'cmpv_b2': nrm(ks[19], (L, HEAD_DIM), 0.01),
        'gn_lru_g': 1.0 + nrm(ks[20], (L, LRU_WIDTH), 0.02),
        'gn_nsa_g': 1.0 + nrm(ks[21], (L, NSA_WIDTH), 0.02),
        'w_out': nrm(ks[22], (L, MIX_WIDTH, D_MODEL), BETA * MIX_WIDTH ** -0.5),
        'ln1_g': 1.0 + nrm(ks[23], (L, D_MODEL), 0.02),
        'ln1_b': nrm(ks[24], (L, D_MODEL), 0.02),
        'peer_wq': nrm(ks[25], (L, D_MODEL, PEER_HEADS * 2 * PEER_KEY_DIM), D_MODEL ** -0.5),
        'peer_subkeys': nrm(ks[26], (L, 2, PEER_KEYS, PEER_KEY_DIM), PEER_KEY_DIM ** -0.5),
        'peer_u': nrm(ks[27], (L, N_EXPERTS, D_MODEL), D_MODEL ** -0.5),
        'peer_v': nrm(ks[28], (L, N_EXPERTS, D_MODEL), BETA * PEER_HEADS ** -0.5),
        'ln2_g': 1.0 + nrm(ks[29], (L, D_MODEL), 0.02),
        'ln2_b': nrm(ks[30], (L, D_MODEL), 0.02),
    }


def reference(x, w_in, b_in, conv_w, conv_b, lru_wa, lru_ba, lru_wx, lru_bx, lru_lambda,
              cmp_pos_k, cmpk_w1, cmpk_b1, cmpk_w2, cmpk_b2,
              cmp_pos_v, cmpv_w1, cmpv_b1, cmpv_w2, cmpv_b2,
              gn_lru_g, gn_nsa_g, w_out, ln1_g, ln1_b,
              peer_wq, peer_subkeys, peer_u, peer_v, ln2_g, ln2_b):
    B, S, _ = x.shape
    offsets = np.cumsum(IN_SPLITS)[:-1].tolist()
    for l in range(DEPTH):
        h = x @ w_in[l] + b_in[l]
        lru_x, lru_gate, q, kc_raw, vc_raw, ks_, vs_, kw_, vw_, gts = jnp.split(h, offsets, axis=-1)
        y_lru = _rg_lru(lru_x, lru_gate, conv_w[l], conv_b[l], lru_wa[l], lru_ba[l],
                        lru_wx[l], lru_bx[l], lru_lambda[l])
        kv = lambda t: t.reshape(B, S, N_KV, HEAD_DIM)
        kc = _compress(kv(kc_raw), cmp_pos_k[l], cmpk_w1[l], cmpk_b1[l], cmpk_w2[l], cmpk_b2[l])
        vc = _compress(kv(vc_raw), cmp_pos_v[l], cmpv_w1[l], cmpv_b1[l], cmpv_w2[l], cmpv_b2[l])
        y_nsa = _nsa(q.reshape(B, S, N_HEADS, HEAD_DIM), kc, vc, kv(ks_), kv(vs_), kv(kw_), kv(vw_),
                     gts.reshape(B, S, N_HEADS, N_BRANCHES))
        mix = jnp.concatenate([_rms_norm(y_lru, gn_lru_g[l]), _rms_norm(y_nsa, gn_nsa_g[l])], axis=-1) @ w_out[l]
        x = _layer_norm(ALPHA * x + mix, ln1_g[l], ln1_b[l])
        x = _layer_norm(ALPHA * x + _peer(x, peer_wq[l], peer_subkeys[l], peer_u[l], peer_v[l]), ln2_g[l], ln2_b[l])
    return x
```

```python
import numpy as np
from contextlib import ExitStack
import concourse.bass as bass
import concourse.mybir as mybir
from concourse.bass_utils import run_bass_kernel_spmd

F32 = mybir.dt.float32
BF16 = mybir.dt.bfloat16
AF = mybir.ActivationFunctionType
ALU = mybir.AluOpType

S_LEN = 4096
DM = 1024
NL = 2
ALPHA = (2 * NL) ** 0.25
EPS = 1e-5
NEG = -1e30

PC_SPEC = [("bx", 4), ("bg", 4), ("cw", 16), ("cb", 4), ("ba", 4), ("bxg", 4), ("lam", 4), ("gl", 4),
           ("bq", 8), ("bkc", 2), ("bvc", 2), ("bks", 2), ("bkw", 2), ("kb1", 1), ("kb2", 1), ("vb1", 1)]
PC_OFF = {}
_o = 0
for _n, _c in PC_SPEC:
    PC_OFF[_n] = _o
    _o += _c
NPC = _o
PR_SPEC = [("bvs", 128), ("bvw", 128), ("bgt", 24), ("gn", 512), ("l1g", 1024), ("l1b", 1024), ("l2g", 1024), ("l2b", 1024)]
PR_OFF = {}
_o = 0
for _n, _c in PR_SPEC:
    PR_OFF[_n] = _o
    _o += _c
NPR = _o


class Buf:
    __slots__ = ("w", "r")

    def __init__(self):
        self.w = {}
        self.r = {}


class T:
    def __init__(self, t):
        self.t = t
        self.b = Buf()

    def __getitem__(self, k):
        return self.t[k]


class V:
    def __init__(self, ap):
        self.ap = ap
        self.b = Buf()


class Sched:
    def __init__(self, nc, stack, ndma=32):
        self.nc = nc
        self.eng = {"pe": nc.tensor, "dve": nc.vector, "act": nc.scalar, "pool": nc.gpsimd, "sp": nc.sync}
        self.sem = {}
        self.cnt = {}
        for k in self.eng:
            self.sem[k] = stack.enter_context(nc.semaphore("s_" + k))
            self.cnt[k] = 0
        self.ndma = ndma
        for i in range(ndma):
            k = "d%d" % i
            self.sem[k] = stack.enter_context(nc.semaphore("s_" + k))
            self.cnt[k] = 0
        self.seen = {e: {} for e in self.eng}
        self.rr = 0

    def _deps(self, reads, writes, join):
        deps = {}

        def add(d):
            for s, v in d.items():
                if deps.get(s, 0) < v:
                    deps[s] = v
        for t in reads:
            add(t.b.w)
        for t in writes:
            if not join:
                add(t.b.w)
            add(t.b.r)
        return deps

    def _wait(self, e, deps):
        for s, v in deps.items():
            if s == "pe" and e == "pe":
                continue
            if self.seen[e].get(s, 0) >= v:
                continue
            self.eng[e].wait_ge(self.sem[s], v)
            self.seen[e][s] = v

    def _mark(self, tok, reads, writes, join):
        s, v = tok
        for t in reads:
            t.b.r[s] = v
        for t in writes:
            if join:
                t.b.w[s] = v
            else:
                t.b.w = {s: v}
                t.b.r = {}

    def op(self, e, fn, reads=(), writes=(), join=False):
        self._wait(e, self._deps(reads, writes, join))
        ins = fn(self.eng[e])
        ins.then_inc(self.sem[e], 1)
        self.cnt[e] += 1
        self._mark((e, self.cnt[e]), reads, writes, join)

    def dma(self, q, out, in_, reads=(), writes=(), join=False, **kw):
        k = "d%d" % self.rr
        self.rr = (self.rr + 1) % self.ndma
        deps = self._deps(reads, writes, join)
        if self.cnt[k] > 0:
            deps[k] = max(deps.get(k, 0), self.cnt[k])
        self._wait(q, deps)
        self.eng[q].dma_start(out=out, in_=in_, **kw).then_inc(self.sem[k], 16)
        self.cnt[k] += 16
        self._mark((k, self.cnt[k]), reads, writes, join)

    def barrier(self):
        for e in self.eng:
            for s, v in self.cnt.items():
                if v > 0 and self.seen[e].get(s, 0) < v:
                    self.eng[e].wait_ge(self.sem[s], v)
                    self.seen[e][s] = v


def build(nlayers=NL, debug=False, stages=("xT", "lru", "nsa", "out", "peer")):
    nc = bass.Bass("TRN2", target_bir_lowering=False)

    def din(name, shape, dt=F32):
        return nc.dram_tensor(name, list(shape), dt, kind="ExternalInput").ap()

    dbg = set(debug) if debug else set()

    def dscr(name, shape, dt=F32):
        return nc.dram_tensor(name, list(shape), dt, kind=("ExternalOutput" if name in dbg else "Internal")).ap()

    L = NL
    x_in = din("x", [S_LEN, DM])
    w_in = din("w_in", [L, DM, 2328])
    pc_d = din("pc", [L, 128, NPC])
    pr_d = din("pr", [L, 128, NPR])
    wa_bd = din("wa_bd", [L, 4, 128, 128])
    wx_bd = din("wx_bd", [L, 4, 128, 128])
    w1k_d = din("w1k", [L, 64, 32, 64])
    w1v_d = din("w1v", [L, 64, 32, 64])
    w2k_d = din("w2k", [L, 64, 64])
    w2va_d = din("w2va", [L, 65, 64])
    posk_d = din("posk", [L, 64, 34])
    posv_d = din("posv", [L, 64, 34])
    wout_d = din("w_out", [L, DM, DM])
    wq_d = din("wq", [L, DM, 2048])
    skT_d = din("skT", [L, 128, 2, 128])
    big = "peer" in stages
    uT_d = din("uT", [L, DM, 16384] if big else [L, 8, 8])
    v_d = din("vtab", [L, 16384, DM] if big else [L, 8, 8])
    c_ident = din("c_ident", [128, 128])
    c_qpos = din("c_qpos", [4, 8, S_LEN])
    c_kpos = din("c_kpos", [4, S_LEN])
    c_cpos = din("c_cpos", [4, 256])
    c_selmat = din("c_selmat", [128, 2, 64])
    c_fb = din("c_fb", [128, 32, 64])
    c_expand = din("c_expand", [64, 32, 128])
    y_out = nc.dram_tensor("y", [S_LEN, DM], F32, kind="ExternalOutput").ap()

    xT_d = dscr("xT_d", [DM, S_LEN], BF16)
    ylruT_d = dscr("ylruT_d", [512, S_LEN], BF16)
    ynsa_d = dscr("ynsa_d", [S_LEN, 512])
    x1_d = dscr("x1_d", [S_LEN, DM])
    x1T_d = dscr("x1T_d", [DM, S_LEN], BF16)
    resid_d = dscr("resid_d", [S_LEN, DM])
    uTb_d = dscr("uTb_d", [DM, 16384], BF16)
    vb_d = dscr("vb_d", [16384, DM], BF16)

    with ExitStack() as top:
        S = Sched(nc, top)

        uid = [0]

        def sb(stk, name, shape, dt=F32):
            uid[0] += 1
            return T(stk.enter_context(nc.sbuf_tensor("sb%d_%s" % (uid[0], name), list(shape), dt)))

        P2 = [top.enter_context(nc.psum_tensor("ps%d" % i, [128, 1024], F32)) for i in range(4)]
        BANK = [V(P2[i // 2][:, (i % 2) * 512:(i % 2 + 1) * 512]) for i in range(8)]

        def act(out, in_, func, bias=None, scale=None, accum_out=None):
            kw = {}
            if bias is not None:
                kw["bias"] = bias
            if scale is not None:
                kw["scale"] = scale
            if accum_out is not None:
                kw["accum_out"] = accum_out
            return lambda e: e.activation(out=out, in_=in_, func=func, **kw)

        def copy(out, in_):
            return lambda e: e.tensor_copy(out=out, in_=in_)

        def mm(out, lhsT, rhs, start, stop):
            return lambda e: e.matmul(out, lhsT, rhs, start=start, stop=stop)

        def tt(out, in0, in1, op):
            return lambda e: e.tensor_tensor(out=out, in0=in0, in1=in1, op=op)

        def ts(out, in0, s1, s2, op0, op1=None):
            if op1 is None:
                return lambda e: e.tensor_scalar(out=out, in0=in0, scalar1=s1, scalar2=None, op0=op0)
            return lambda e: e.tensor_scalar(out=out, in0=in0, scalar1=s1, scalar2=s2, op0=op0, op1=op1)

        def stt(out, in0, scalar, in1, op0, op1):
            return lambda e: e.scalar_tensor_tensor(out=out, in0=in0, scalar=scalar, in1=in1, op0=op0, op1=op1)

        def memset(ap, v):
            return lambda e: e.memset(ap, v)

        identf = sb(top, "identf", [128, 128])
        identb = sb(top, "identb", [128, 128], BF16)
        onesf = sb(top, "onesf", [128, 128])
        S.dma("sp", identf[:], c_ident, writes=[identf])
        S.op("dve", copy(identb[:], identf[:]), reads=[identf], writes=[identb])
        S.op("dve", memset(onesf[:], 1.0), writes=[onesf])

        def ldcast(dst_t, dst_ap, src_ap):
            S.dma("pool", dst_ap, src_ap, writes=[dst_t], join=True)

        def stage_xT(src, dst):
            with ExitStack() as stk:
                xin = [sb(stk, "xin%d" % i, [128, DM]) for i in range(2)]
                xo = [sb(stk, "xo%d" % i, [128, 8, 512], BF16) for i in range(2)]
                dstv = dst.rearrange("(k p) t -> p k t", p=128)
                for t_ in range(32):
                    xi = xin[t_ % 2]
                    S.dma("sp", xi[:], src[t_ * 128:(t_ + 1) * 128, :], writes=[xi])
                    g = t_ // 4
                    o = xo[g % 2]
                    j = t_ % 4
                    for hb in range(2):
                        bk = BANK[hb + 2 * (t_ % 2)]
                        for kk in range(4):
                            k = hb * 4 + kk
                            S.op("pe", lambda e: e.transpose(out=bk.ap[:, kk * 128:(kk + 1) * 128], in_=xi[:, k * 128:(k + 1) * 128], identity=identf[:]),
                                 reads=[xi, identf], writes=[bk])
                        S.op("dve" if hb == 0 else "act",
                             copy(o[:, hb * 4:(hb + 1) * 4, j * 128:(j + 1) * 128], bk.ap.rearrange("p (k t) -> p k t", k=4)) if hb == 0 else
                             act(o[:, hb * 4:(hb + 1) * 4, j * 128:(j + 1) * 128], bk.ap.rearrange("p (k t) -> p k t", k=4), AF.Copy),
                             reads=[bk], writes=[o], join=True)
                    if j == 3:
                        S.dma("sp", dstv[:, :, g * 512:(g + 1) * 512], o[:], reads=[o])
                S.barrier()

        def stage_lru(l):
            with ExitStack() as stk:
                xT = sb(stk, "xT", [128, 8, S_LEN], BF16)
                S.dma("sp", xT[:], xT_d.rearrange("(k p) t -> p k t", p=128), writes=[xT])
                wl = sb(stk, "wl", [128, 8, 1024], BF16)
                for k in range(8):
                    ldcast(wl, wl[:, k, :], w_in[l, k * 128:(k + 1) * 128, 0:1024])
                pc = sb(stk, "pc", [128, NPC])
                S.dma("sp", pc[:], pc_d[l], writes=[pc])
                wab = sb(stk, "wab", [128, 4, 128], BF16)
                wxb = sb(stk, "wxb", [128, 4, 128], BF16)
                for c in range(4):
                    ldcast(wab, wab[:, c, :], wa_bd[l, c])
                    ldcast(wxb, wxb[:, c, :], wx_bd[l, c])
                cch = sb(stk, "cch", [128, 4])
                cch2 = sb(stk, "cch2", [128, 4])
                tmpc = sb(stk, "tmpc", [128, 4])
                lam = pc[:, PC_OFF["lam"]:PC_OFF["lam"] + 4]
                S.op("act", act(tmpc[:], lam, AF.Exp, scale=-1.0), reads=[pc], writes=[tmpc])
                S.op("act", act(tmpc[:], tmpc[:], AF.Ln, bias=1.0), reads=[tmpc], writes=[tmpc])
                S.op("dve", ts(cch[:], tmpc[:], -8.0, None, ALU.mult), reads=[tmpc], writes=[cch])
                S.op("dve", ts(cch2[:], tmpc[:], -16.0, None, ALU.mult), reads=[tmpc], writes=[cch2])

                ylru = sb(stk, "ylru", [128, 4, S_LEN], BF16)
                ssq = sb(stk, "ssq", [128, S_LEN])

                def rot(name, shape, dt=F32, n=2):
                    return [sb(stk, "%s%d" % (name, i), shape, dt) for i in range(n)]
                xb_ = rot("xb", [128, 515])
                gg_ = rot("gg", [128, 512])
                xc_ = rot("xc", [128, 512])
                xcb_ = rot("xcb", [128, 512], BF16)
                r_ = rot("r", [128, 512])
                i_ = rot("i", [128, 512])
                a_ = rot("a", [128, 512])
                s_ = rot("s", [128, 512])
                u_ = rot("u", [128, 512])
                h_ = rot("h", [128, 512])
                y_ = rot("y", [128, 512])
                q_ = rot("ysq", [128, 512])
                it = 0
                for c in range(4):
                    def col(nm, j=0):
                        o = PC_OFF[nm] + j
                        return pc[:, o:o + 1]
                    for tb in range(8):
                        p = it % 2
                        xb, gg, xc, xcb, r, i, a, s, u, h, y, ysq = (xb_[p], gg_[p], xc_[p], xcb_[p], r_[p], i_[p], a_[p], s_[p], u_[p], h_[p], y_[p], q_[p])
                        xbp, hp = xb_[1 - p], h_[1 - p]
                        bx, bgk, br, bi = BANK[0 + 4 * p], BANK[1 + 4 * p], BANK[2 + 4 * p], BANK[3 + 4 * p]
                        tsl = slice(tb * 512, (tb + 1) * 512)
                        for k in range(8):
                            S.op("pe", mm(bx.ap, wl[:, k, c * 128:(c + 1) * 128], xT[:, k, tsl], k == 0, k == 7), reads=[wl, xT], writes=[bx])
                        for k in range(8):
                            S.op("pe", mm(bgk.ap, wl[:, k, 512 + c * 128:512 + (c + 1) * 128], xT[:, k, tsl], k == 0, k == 7), reads=[wl, xT], writes=[bgk])
                        if tb == 0:
                            S.op("dve", memset(xb[:, 0:3], 0.0), writes=[xb])
                        else:
                            S.op("dve", copy(xb[:, 0:3], xbp[:, 512:515]), reads=[xbp], writes=[xb])
                        S.op("act", act(xb[:, 3:515], bx.ap, AF.Identity, bias=col("bx", c)), reads=[bx, pc], writes=[xb], join=True)
                        S.op("act", act(gg[:], bgk.ap, AF.Gelu_apprx_tanh, bias=col("bg", c)), reads=[bgk, pc], writes=[gg])
                        S.op("dve", ts(xc[:], xb[:, 0:512], col("cw", 0 * 4 + c), col("cb", c), ALU.mult, ALU.add), reads=[xb, pc], writes=[xc])
                        for j in range(1, 4):
                            S.op("dve", stt(xc[:], xb[:, j:j + 512], col("cw", j * 4 + c), xc[:], ALU.mult, ALU.add), reads=[xb, pc, xc], writes=[xc])
                        S.op("pool", copy(xcb[:], xc[:]), reads=[xc], writes=[xcb])
                        S.op("pe", mm(br.ap, wab[:, c, :], xcb[:], True, True), reads=[wab, xcb], writes=[br])
                        S.op("pe", mm(bi.ap, wxb[:, c, :], xcb[:], True, True), reads=[wxb, xcb], writes=[bi])
                        S.op("act", act(r[:], br.ap, AF.Sigmoid, bias=col("ba", c)), reads=[br, pc], writes=[r])
                        S.op("act", act(i[:], bi.ap, AF.Sigmoid, bias=col("bxg", c)), reads=[bi, pc], writes=[i])
                        S.op("act", act(a[:], r[:], AF.Exp, scale=cch[:, c:c + 1]), reads=[r, cch], writes=[a])
                        S.op("act", act(s[:], r[:], AF.Exp, scale=cch2[:, c:c + 1]), reads=[r, cch2], writes=[s])
                        S.op("dve", ts(s[:], s[:], -1.0, 1.0, ALU.mult, ALU.add), reads=[s], writes=[s])
                        S.op("act", act(s[:], s[:], AF.Sqrt), reads=[s], writes=[s])
                        S.op("pool", tt(u[:], i[:], xc[:], ALU.mult), reads=[i, xc], writes=[u])
                        S.op("dve", tt(u[:], u[:], s[:], ALU.mult), reads=[u, s], writes=[u])
                        init = 0.0 if tb == 0 else hp[:, 511:512]
                        S.op("dve", lambda e: e.tensor_tensor_scan(out=h[:], data0=a[:], data1=u[:], initial=init, op0=ALU.mult, op1=ALU.add),
                             reads=[a, u] + ([] if tb == 0 else [hp]), writes=[h])
                        S.op("dve", tt(y[:], h[:], gg[:], ALU.mult), reads=[h, gg], writes=[y])
                        S.op("act", act(ysq[:], y[:], AF.Square), reads=[y], writes=[ysq])
                        S.op("pool", copy(ylru[:, c, tsl], y[:]), reads=[y], writes=[ylru], join=True)
                        S.op("pe", mm(bx.ap, onesf[:], ysq[:], True, True), reads=[onesf, ysq], writes=[bx])
                        if c == 0:
                            S.op("dve", copy(ssq[:, tsl], bx.ap), reads=[bx], writes=[ssq], join=True)
                        else:
                            S.op("dve", tt(ssq[:, tsl], ssq[:, tsl], bx.ap, ALU.add), reads=[bx, ssq], writes=[ssq])
                        it += 1
                S.op("dve", ts(ssq[:], ssq[:], 1.0 / 512, EPS, ALU.mult, ALU.add), reads=[ssq], writes=[ssq])
                S.op("act", act(ssq[:], ssq[:], AF.Sqrt), reads=[ssq], writes=[ssq])
                S.op("dve", lambda e: e.reciprocal(out=ssq[:], in_=ssq[:]), reads=[ssq], writes=[ssq])
                for c in range(4):
                    o = PC_OFF["gl"] + c
                    S.op("dve", stt(ylru[:, c, :], ylru[:, c, :], pc[:, o:o + 1], ssq[:], ALU.mult, ALU.mult), reads=[ylru, pc, ssq], writes=[ylru])
                S.dma("sp", ylruT_d.rearrange("(c p) t -> p c t", p=128), ylru[:], reads=[ylru])
                S.barrier()

        def stage_nsa(l):
            with ExitStack() as stk:
                xT = sb(stk, "xT", [128, 8, S_LEN], BF16)
                S.dma("sp", xT[:], xT_d.rearrange("(k p) t -> p k t", p=128), writes=[xT])
                wn = sb(stk, "wn", [128, 8, 1304], BF16)
                for k in range(8):
                    ldcast(wn, wn[:, k, 0:1024], w_in[l, k * 128:(k + 1) * 128, 1024:2048])
                    ldcast(wn, wn[:, k, 1024:1304], w_in[l, k * 128:(k + 1) * 128, 2048:2328])
                pc = sb(stk, "pc", [128, NPC])
                S.dma("sp", pc[:], pc_d[l], writes=[pc])
                prn = sb(stk, "prn", [128, 280])
                S.dma("sp", prn[:], pr_d[l, :, 0:280], writes=[prn])
                bq8 = sb(stk, "bq8", [128, 8])
                S.op("dve", ts(bq8[:], pc[:, PC_OFF["bq"]:PC_OFF["bq"] + 8], 0.125, None, ALU.mult), reads=[pc], writes=[bq8])
                fb = sb(stk, "fb", [128, 32, 64])
                S.dma("sp", fb[:], c_fb, writes=[fb])
                expd = sb(stk, "expd", [64, 32, 128], BF16)
                for j in range(4):
                    ldcast(expd, expd[:, j * 8:(j + 1) * 8, :], c_expand[:, j * 8:(j + 1) * 8, :])
                w1k = sb(stk, "w1k", [64, 32, 64], BF16)
                w1v = sb(stk, "w1v", [64, 32, 64], BF16)
                for j in range(2):
                    ldcast(w1k, w1k[:, j * 16:(j + 1) * 16, :], w1k_d[l, :, j * 16:(j + 1) * 16, :])
                    ldcast(w1v, w1v[:, j * 16:(j + 1) * 16, :], w1v_d[l, :, j * 16:(j + 1) * 16, :])
                posk = sb(stk, "posk", [64, 34], BF16)
                posv = sb(stk, "posv", [64, 34], BF16)
                ldcast(posk, posk[:], posk_d[l])
                ldcast(posv, posv[:], posv_d[l])
                w2k = sb(stk, "w2k", [64, 64], BF16)
                w2va = sb(stk, "w2va", [65, 64], BF16)
                ldcast(w2k, w2k[:], w2k_d[l])
                ldcast(w2va, w2va[:], w2va_d[l])

                qTa = sb(stk, "qTa", [68, 4, S_LEN], BF16)
                ksa = sb(stk, "ksa", [68, S_LEN], BF16)
                kwa = sb(stk, "kwa", [68, S_LEN], BF16)
                kca = sb(stk, "kca", [68, 256], BF16)
                kcr = sb(stk, "kcr", [64, S_LEN + 32], BF16)
                vcr = sb(stk, "vcr", [64, S_LEN + 32], BF16)
                vsa = sb(stk, "vsa", [128, 32, 65], BF16)
                vwa = sb(stk, "vwa", [128, 32, 65], BF16)
                vca = sb(stk, "vca", [128, 2, 129], BF16)
                gsig = sb(stk, "gsig", [128, 32, 12])
                h1k = sb(stk, "h1k", [65, 256], BF16)
                h1v = sb(stk, "h1v", [65, 256], BF16)
                b1k = sb(stk, "b1k", [64, 1])
                b1v = sb(stk, "b1v", [64, 1])
                es_ = [sb(stk, "es%d" % i, [128, 512], BF16) for i in range(4)]
                mT_ = [sb(stk, "mT%d" % i, [64, 128], BF16) for i in range(2)]
                impa = sb(stk, "impa", [128, 64])
                impt = sb(stk, "impt", [128, 64])
                selm = sb(stk, "selm", [128, 64])
                m8 = sb(stk, "m8", [128, 16])
                rd = sb(stk, "rd", [128, 3, 4])
                coef = sb(stk, "coef", [128, 3, 4])
                ya_ = [sb(stk, "ya%d" % i, [128, 256]) for i in range(2)]
                ytmp = sb(stk, "ytmp", [128, 256])
                gtmp = sb(stk, "gtmp", [128, 12])

                for g in range(2):
                    for cb in range(4):
                        csl = slice(cb * 1024, (cb + 1) * 1024)
                        ldcast(qTa, qTa[64:68, :, csl], c_qpos[:, 4 * g:4 * g + 4, csl])
                        ldcast(ksa, ksa[64:68, csl], c_kpos[:, csl])
                        ldcast(kwa, kwa[64:68, csl], c_kpos[:, csl])
                    ldcast(kca, kca[64:68, :], c_cpos)
                    S.op("pool", memset(kcr[:, S_LEN:S_LEN + 32], 0.0), writes=[kcr], join=True)
                    S.op("pool", memset(vcr[:, S_LEN:S_LEN + 32], 0.0), writes=[vcr], join=True)
                    S.op("pool", memset(vsa[:], 1.0), writes=[vsa])
                    S.op("pool", memset(vwa[:], 1.0), writes=[vwa])
                    S.op("pool", memset(vca[:], 1.0), writes=[vca])
                    ldcast(vca, vca[:, :, 65:129], c_selmat)
                    S.op("pool", memset(h1k[64:65, :], 1.0), writes=[h1k], join=True)
                    S.op("pool", memset(h1v[64:65, :], 1.0), writes=[h1v], join=True)
                    projs = []
                    for r4 in range(4):
                        hh = 4 * g + r4
                        projs.append((hh * 64, qTa, lambda sl, r4=r4: qTa[0:64, r4, sl], bq8[0:64, hh:hh + 1], 0.125, bq8))
                    projs.append((512 + g * 64, kcr, lambda sl: kcr[0:64, sl], pc[0:64, PC_OFF["bkc"] + g:PC_OFF["bkc"] + g + 1], 1.0, pc))
                    projs.append((640 + g * 64, vcr, lambda sl: vcr[0:64, sl], pc[0:64, PC_OFF["bvc"] + g:PC_OFF["bvc"] + g + 1], 1.0, pc))
                    projs.append((768 + g * 64, ksa, lambda sl: ksa[0:64, sl], pc[0:64, PC_OFF["bks"] + g:PC_OFF["bks"] + g + 1], 1.0, pc))
                    projs.append((1024 + g * 64, kwa, lambda sl: kwa[0:64, sl], pc[0:64, PC_OFF["bkw"] + g:PC_OFF["bkw"] + g + 1], 1.0, pc))
                    it = 0
                    for (coff, dt_, dfn, bias_ap, scl, bias_t) in projs:
                        for tb in range(8):
                            bk = BANK[it % 4]
                            it += 1
                            tsl = slice(tb * 512, (tb + 1) * 512)
                            for k in range(8):
                                S.op("pe", mm(bk.ap[0:64, :], wn[:, k, coff:coff + 64], xT[:, k, tsl], k == 0, k == 7), reads=[wn, xT], writes=[bk])
                            S.op("act", act(dfn(tsl), bk.ap[0:64, :], AF.Identity, bias=bias_ap, scale=scl), reads=[bk, bias_t], writes=[dt_], join=True)
                    for t_ in range(32):
                        tsl = slice(t_ * 128, (t_ + 1) * 128)
                        b1_, b2_, b3_ = BANK[4], BANK[5], BANK[6]
                        for k in range(8):
                            S.op("pe", mm(b1_.ap[:, 0:64], xT[:, k, tsl], wn[:, k, 896 + g * 64:896 + g * 64 + 64], k == 0, k == 7), reads=[wn, xT], writes=[b1_])
                        for k in range(8):
                            S.op("pe", mm(b2_.ap[:, 0:64], xT[:, k, tsl], wn[:, k, 1152 + g * 64:1152 + g * 64 + 64], k == 0, k == 7), reads=[wn, xT], writes=[b2_])
                        for k in range(8):
                            S.op("pe", mm(b3_.ap[:, 0:12], xT[:, k, tsl], wn[:, k, 1280 + g * 12:1280 + g * 12 + 12], k == 0, k == 7), reads=[wn, xT], writes=[b3_])
                        S.op("dve", tt(vsa[:, t_, 0:64], b1_.ap[:, 0:64], prn[:, g * 64:g * 64 + 64], ALU.add), reads=[b1_, prn], writes=[vsa], join=True)
                        S.op("dve", tt(vwa[:, t_, 0:64], b2_.ap[:, 0:64], prn[:, 128 + g * 64:128 + g * 64 + 64], ALU.add), reads=[b2_, prn], writes=[vwa], join=True)
                        S.op("dve", tt(gtmp[:], b3_.ap[:, 0:12], prn[:, 256 + g * 12:256 + g * 12 + 12], ALU.add), reads=[b3_, prn], writes=[gtmp])
                        S.op("act", act(gsig[:, t_, :], gtmp[:], AF.Sigmoid), reads=[gtmp], writes=[gsig], join=True)
                    for (raw, w1, pos, b1t, b1name, h1) in ((kcr, w1k, posk, b1k, "kb1", h1k), (vcr, w1v, posv, b1v, "vb1", h1v)):
                        bA, bB = BANK[0], BANK[1]
                        for l_ in range(32):
                            S.op("pe", mm(bA.ap[0:64, 0:256], w1[:, l_, :], raw[:, l_:l_ + 4096:16], l_ == 0, l_ == 31), reads=[w1, raw], writes=[bA])
                        for l_ in range(32):
                            S.op("pe", mm(bB.ap[0:64, 0:2], w1[:, l_, :], pos[:, l_:l_ + 2], l_ == 0, l_ == 31), reads=[w1, pos], writes=[bB])
                        o = PC_OFF[b1name]
                        S.op("dve", tt(b1t[:], bB.ap[0:64, 0:1], pc[0:64, o:o + 1], ALU.add), reads=[bB, pc], writes=[b1t])
                        S.op("act", act(h1[0:64, :], bA.ap[0:64, 0:256], AF.Gelu_apprx_tanh, bias=b1t[:]), reads=[bA, b1t], writes=[h1], join=True)
                    bA = BANK[2]
                    S.op("pe", mm(bA.ap[0:64, 0:256], w2k[:], h1k[0:64, :], True, True), reads=[w2k, h1k], writes=[bA])
                    o = PC_OFF["kb2"]
                    S.op("act", act(kca[0:64, :], bA.ap[0:64, 0:256], AF.Identity, bias=pc[0:64, o:o + 1]), reads=[bA, pc], writes=[kca], join=True)
                    for ch in range(2):
                        bB = BANK[3]
                        S.op("pe", mm(bB.ap[:, 0:64], h1v[0:65, ch * 128:(ch + 1) * 128], w2va[:], True, True), reads=[h1v, w2va], writes=[bB])
                        S.op("dve", copy(vca[:, ch, 0:64], bB.ap[:, 0:64]), reads=[bB], writes=[vca], join=True)

                    bOc, bI, bT, bOs, bOw = BANK[2], BANK[3], BANK[4], BANK[5], BANK[6]
                    sbank = [BANK[0], BANK[1]]
                    mbank = [BANK[7], BANK[4]]
                    sit = 0
                    eit = 0
                    for qi in range(32):
                        qsl = slice(qi * 128, (qi + 1) * 128)
                        rhsq = qTa[:, :, qsl]
                        ya = ya_[qi % 2]
                        mT = mT_[qi % 2]
                        nch = 1 if qi < 16 else 2
                        for ch in range(nch):
                            bs = sbank[sit % 2]
                            sit += 1
                            es = es_[eit % 4]
                            eit += 1
                            S.op("pe", mm(bs.ap, kca[:, ch * 128:(ch + 1) * 128], rhsq, True, True), reads=[kca, qTa], writes=[bs])
                            S.op("act", act(es[:], bs.ap, AF.Exp), reads=[bs], writes=[es])
                            S.op("pool", lambda e: e.affine_select(out=es[:].rearrange("p (r q) -> p r q", r=4), in_=es[:].rearrange("p (r q) -> p r q", r=4),
                                                                   pattern=[[0, 4], [1, 128]], compare_op=ALU.is_ge, fill=0.0,
                                                                   base=128 * qi - 2048 * ch - 31, channel_multiplier=-16), reads=[es], writes=[es])
                            for r4 in range(4):
                                S.op("pe", mm(bOc.ap[:, r4 * 65:(r4 + 1) * 65], es[:, r4 * 128:(r4 + 1) * 128], vca[:, ch, 0:65], ch == 0 and r4 == 0, ch == nch - 1 and r4 == 3),
                                     reads=[es, vca], writes=[bOc])
                            for r4 in range(4):
                                S.op("pe", mm(bI.ap[:, r4 * 64:(r4 + 1) * 64], es[:, r4 * 128:(r4 + 1) * 128], vca[:, ch, 65:129], ch == 0 and r4 == 0, ch == nch - 1 and r4 == 3),
                                     reads=[es, vca], writes=[bI])
                        ocv = bOc.ap[:, 0:260].rearrange("p (r d) -> p r d", r=4)
                        S.op("dve", ts(rd[:, 0, :], ocv[:, :, 64], 1e-30, None, ALU.add), reads=[bOc], writes=[rd])
                        S.op("dve", lambda e: e.reciprocal(out=rd[:, 0, :], in_=rd[:, 0, :]), reads=[rd], writes=[rd])
                        S.op("dve", ts(impa[:], bI.ap[:, 0:64], rd[:, 0, 0:1], None, ALU.mult), reads=[bI, rd], writes=[impa])
                        for r4 in range(1, 4):
                            S.op("dve", stt(impa[:], bI.ap[:, r4 * 64:(r4 + 1) * 64], rd[:, 0, r4:r4 + 1], impa[:], ALU.mult, ALU.add), reads=[bI, rd, impa], writes=[impa])
                        S.op("dve", tt(impa[:], impa[:], fb[:, qi, :], ALU.add), reads=[impa, fb], writes=[impa])
                        S.op("dve", lambda e: e.max(out=m8[:, 0:8], in_=impa[:]), reads=[impa], writes=[m8])
                        S.op("dve", lambda e: e.match_replace(out=impt[:], in_to_replace=m8[:, 0:8], in_values=impa[:], imm_value=-3e38), reads=[impa, m8], writes=[impt])
                        S.op("dve", lambda e: e.max(out=m8[:, 8:16], in_=impt[:]), reads=[impt], writes=[m8])
                        S.op("dve", ts(selm[:], impa[:], m8[:, 15:16], None, ALU.is_ge), reads=[impa, m8], writes=[selm])
                        S.op("pe", lambda e: e.transpose(out=bT.ap[0:64, 0:128], in_=selm[:], identity=identf[:]), reads=[selm, identf], writes=[bT])
                        S.op("act", act(mT[:], bT.ap[0:64, 0:128], AF.Copy), reads=[bT], writes=[mT])
                        for kj in range(qi + 1):
                            bs = sbank[sit % 2]
                            bm = BANK[7]
                            sit += 1
                            es = es_[eit % 4]
                            eit += 1
                            ksl = slice(kj * 128, (kj + 1) * 128)
                            S.op("pe", mm(bs.ap, ksa[:, ksl], rhsq, True, True), reads=[ksa, qTa], writes=[bs])
                            S.op("act", act(es[:], bs.ap, AF.Exp), reads=[bs], writes=[es])
                            S.op("pe", mm(bm.ap[:, 0:128], expd[:, kj, :], mT[:], True, True), reads=[expd, mT], writes=[bm])
                            esv = es[:].rearrange("p (r q) -> p r q", r=4)
                            S.op("dve", tt(esv, esv, bm.ap[:, 0:128].unsqueeze(1).to_broadcast([128, 4, 128]), ALU.mult), reads=[es, bm], writes=[es])
                            if kj == qi:
                                S.op("pool", lambda e: e.affine_select(out=esv, in_=esv, pattern=[[0, 4], [1, 128]], compare_op=ALU.is_ge, fill=0.0,
                                                                       base=0, channel_multiplier=-1), reads=[es], writes=[es])
                            for r4 in range(4):
                                S.op("pe", mm(bOs.ap[:, r4 * 65:(r4 + 1) * 65], es[:, r4 * 128:(r4 + 1) * 128], vsa[:, kj, :], kj == 0 and r4 == 0, kj == qi and r4 == 3),
                                     reads=[es, vsa], writes=[bOs])
                        k0 = max(0, qi - 4)
                        for kj in range(k0, qi + 1):
                            bs = sbank[sit % 2]
                            sit += 1
                            es = es_[eit % 4]
                            eit += 1
                            ksl = slice(kj * 128, (kj + 1) * 128)
                            S.op("pe", mm(bs.ap, kwa[:, ksl], rhsq, True, True), reads=[kwa, qTa], writes=[bs])
                            S.op("act", act(es[:], bs.ap, AF.Exp), reads=[bs], writes=[es])
                            esv = es[:].rearrange("p (r q) -> p r q", r=4)
                            if kj == qi:
                                S.op("pool", lambda e: e.affine_select(out=esv, in_=esv, pattern=[[0, 4], [1, 128]], compare_op=ALU.is_ge, fill=0.0,
                                                                       base=0, channel_multiplier=-1), reads=[es], writes=[es])
                            if kj == qi - 4:
                                S.op("pool", lambda e: e.affine_select(out=esv, in_=esv, pattern=[[0, 4], [-1, 128]], compare_op=ALU.is_ge, fill=0.0,
                                                                       base=-1, channel_multiplier=1), reads=[es], writes=[es])
                            for r4 in range(4):
                                S.op("pe", mm(bOw.ap[:, r4 * 65:(r4 + 1) * 65], es[:, r4 * 128:(r4 + 1) * 128], vwa[:, kj, :], kj == k0 and r4 == 0, kj == qi and r4 == 3),
                                     reads=[es, vwa], writes=[bOw])
                        osv = bOs.ap[:, 0:260].rearrange("p (r d) -> p r d", r=4)
                        owv = bOw.ap[:, 0:260].rearrange("p (r d) -> p r d", r=4)
                        S.op("dve", ts(rd[:, 1, :], osv[:, :, 64], 1e-30, None, ALU.add), reads=[bOs], writes=[rd], join=True)
                        S.op("dve", ts(rd[:, 2, :], owv[:, :, 64], 1e-30, None, ALU.add), reads=[bOw], writes=[rd], join=True)
                        S.op("dve", lambda e: e.reciprocal(out=rd[:, 1:3, :], in_=rd[:, 1:3, :]), reads=[rd], writes=[rd])
                        S.op("dve", tt(coef[:], rd[:], gsig[:, qi, :].rearrange("p (r b) -> p b r", b=3), ALU.mult), reads=[rd, gsig], writes=[coef])
                        yav = ya[:].rearrange("p (r d) -> p r d", r=4)
                        ytv = ytmp[:].rearrange("p (r d) -> p r d", r=4)
                        S.op("dve", tt(yav, ocv[:, :, 0:64], coef[:, 0, :].unsqueeze(2).to_broadcast([128, 4, 64]), ALU.mult), reads=[bOc, coef], writes=[ya])
                        S.op("dve", tt(ytv, osv[:, :, 0:64], coef[:, 1, :].unsqueeze(2).to_broadcast([128, 4, 64]), ALU.mult), reads=[bOs, coef], writes=[ytmp])
                        S.op("dve", tt(ya[:], ya[:], ytmp[:], ALU.add), reads=[ya, ytmp], writes=[ya])
                        S.op("dve", tt(ytv, owv[:, :, 0:64], coef[:, 2, :].unsqueeze(2).to_broadcast([128, 4, 64]), ALU.mult), reads=[bOw, coef], writes=[ytmp])
                        S.op("dve", tt(ya[:], ya[:], ytmp[:], ALU.add), reads=[ya, ytmp], writes=[ya])
                        S.dma("sp", ynsa_d[qsl, g * 256:(g + 1) * 256], ya[:], reads=[ya])
                S.barrier()

        def layer_norm(stk_tiles, t, gb, goff, boff, out):
            stats, mv, rs = stk_tiles
            S.op("dve", lambda e: e.bn_stats(out=stats[:, 0, :], in_=t[:, 0:512]), reads=[t], writes=[stats])
            S.op("dve", lambda e: e.bn_stats(out=stats[:, 1, :], in_=t[:, 512:1024]), reads=[t], writes=[stats], join=True)
            S.op("dve", lambda e: e.bn_aggr(out=mv[:], in_=stats[:].rearrange("p a b -> p (a b)")), reads=[stats], writes=[mv])
            S.op("dve", ts(rs[:], mv[:, 1:2], EPS, None, ALU.add), reads=[mv], writes=[rs])
            S.op("act", act(rs[:], rs[:], AF.Sqrt), reads=[rs], writes=[rs])
            S.op("dve", lambda e: e.reciprocal(out=rs[:], in_=rs[:]), reads=[rs], writes=[rs])
            S.op("dve", ts(out[:], t[:], mv[:, 0:1], rs[:], ALU.subtract, ALU.mult), reads=[t, mv, rs], writes=[out])
            S.op("pool", tt(out[:], out[:], gb[:, goff:goff + 1024], ALU.mult), reads=[out, gb], writes=[out])
            S.op("pool", tt(out[:], out[:], gb[:, boff:boff + 1024], ALU.add), reads=[out, gb], writes=[out])

        def stage_out(l, resid):
            with ExitStack() as stk:
                wo = sb(stk, "wo", [128, 8, 1024], BF16)
                for k in range(8):
                    ldcast(wo, wo[:, k, :], wout_d[l, k * 128:(k + 1) * 128, :])
                ylru = sb(stk, "ylru", [128, 4, S_LEN], BF16)
                S.dma("sp", ylru[:], ylruT_d.rearrange("(c p) t -> p c t", p=128), writes=[ylru])
                gb = sb(stk, "gb", [128, 2560])
                S.dma("sp", gb[:], pr_d[l, :, 280:2840], writes=[gb])
                yn_ = [sb(stk, "yn%d" % i, [128, 512]) for i in range(2)]
                ynn_ = [sb(stk, "ynn%d" % i, [128, 512]) for i in range(2)]
                nsaT_ = [sb(stk, "nsaT%d" % i, [128, 4, 128], BF16) for i in range(2)]
                xr_ = [sb(stk, "xr%d" % i, [128, 1024]) for i in range(2)]
                t_ = [sb(stk, "t%d" % i, [128, 1024]) for i in range(2)]
                x1_ = [sb(stk, "x1%d" % i, [128, 1024]) for i in range(2)]
                x1T_ = [sb(stk, "x1T%d" % i, [128, 8, 128], BF16) for i in range(2)]
                junk = sb(stk, "junk", [128, 512])
                ss_ = [sb(stk, "ss%d" % i, [128, 1]) for i in range(2)]
                stats = sb(stk, "stats", [128, 2, 6])
                mv = sb(stk, "mv", [128, 2])
                rs = sb(stk, "rs", [128, 1])
                x1Tv = x1T_d.rearrange("(k p) t -> p k t", p=128)
                for tt_ in range(32):
                    p = tt_ % 2
                    yn, ynn, nsaT, xr, t, x1, x1T, ss = yn_[p], ynn_[p], nsaT_[p], xr_[p], t_[p], x1_[p], x1T_[p], ss_[p]
                    tsl = slice(tt_ * 128, (tt_ + 1) * 128)
                    S.dma("sp", yn[:], ynsa_d[tsl, :], writes=[yn])
                    S.dma("sp", xr[:], resid[tsl, :], writes=[xr])
                    S.op("act", act(junk[:], yn[:], AF.Square, accum_out=ss[:]), reads=[yn], writes=[junk, ss])
                    S.op("dve", ts(ss[:], ss[:], 1.0 / 512, EPS, ALU.mult, ALU.add), reads=[ss], writes=[ss])
                    S.op("act", act(ss[:], ss[:], AF.Sqrt), reads=[ss], writes=[ss])
                    S.op("dve", lambda e: e.reciprocal(out=ss[:], in_=ss[:]), reads=[ss], writes=[ss])
                    S.op("dve", stt(ynn[:], yn[:], ss[:], gb[:, 0:512], ALU.mult, ALU.mult), reads=[yn, ss, gb], writes=[ynn])
                    bT = BANK[4 + p]
                    for k in range(4):
                        S.op("pe", lambda e: e.transpose(out=bT.ap[:, k * 128:(k + 1) * 128], in_=ynn[:, k * 128:(k + 1) * 128], identity=identf[:]),
                             reads=[ynn, identf], writes=[bT])
                    S.op("act", act(nsaT[:], bT.ap.rearrange("p (k t) -> p k t", k=4), AF.Copy), reads=[bT], writes=[nsaT])
                    for hf in range(2):
                        bk = BANK[hf + 2 * p]
                        hs = slice(hf * 512, (hf + 1) * 512)
                        for k in range(4):
                            S.op("pe", mm(bk.ap, ylru[:, k, tsl], wo[:, k, hs], k == 0, False), reads=[ylru, wo], writes=[bk])
                        for k in range(4):
                            S.op("pe", mm(bk.ap, nsaT[:, k, :], wo[:, 4 + k, hs], False, k == 3), reads=[nsaT, wo], writes=[bk])
                        S.op("dve", stt(t[:, hs], xr[:, hs], ALPHA, bk.ap, ALU.mult, ALU.add), reads=[xr, bk], writes=[t], join=True)
                    layer_norm((stats, mv, rs), t, gb, 512, 1536, x1)
                    S.dma("sp", x1_d[tsl, :], x1[:], reads=[x1])
                    for hb in range(2):
                        bk = BANK[6 + hb]
                        for kk in range(4):
                            k = hb * 4 + kk
                            S.op("pe", lambda e: e.transpose(out=bk.ap[:, kk * 128:(kk + 1) * 128], in_=x1[:, k * 128:(k + 1) * 128], identity=identf[:]),
                                 reads=[x1, identf], writes=[bk])
                        S.op("act", act(x1T[:, hb * 4:(hb + 1) * 4, :], bk.ap.rearrange("p (k t) -> p k t", k=4), AF.Copy), reads=[bk], writes=[x1T], join=True)
                    S.dma("sp", x1Tv[:, :, tsl], x1T[:], reads=[x1T])
                S.barrier()

        def stage_conv(l):
            with ExitStack() as stk:
                f_ = [sb(stk, "cf%d" % i, [128, 4096]) for i in range(3)]
                b_ = [sb(stk, "cb%d" % i, [128, 4096], BF16) for i in range(3)]
                it = 0
                engs = ["act", "pool", "dve"]
                for kp in range(8):
                    for cb in range(4):
                        f, b = f_[it % 3], b_[it % 3]
                        src = uT_d[l, kp * 128:(kp + 1) * 128, cb * 4096:(cb + 1) * 4096]
                        dst = uTb_d[kp * 128:(kp + 1) * 128, cb * 4096:(cb + 1) * 4096]
                        S.dma("sp", f[:], src, writes=[f])
                        e = engs[it % 3]
                        S.op(e, act(b[:], f[:], AF.Copy) if e == "act" else copy(b[:], f[:]), reads=[f], writes=[b])
                        S.dma("sp", dst, b[:], reads=[b])
                        it += 1
                for rb in range(32):
                    f, b = f_[it % 3], b_[it % 3]
                    src = v_d[l, rb * 512:(rb + 1) * 512, :].rearrange("(a p) d -> p a d", p=128)
                    dst = vb_d[rb * 512:(rb + 1) * 512, :].rearrange("(a p) d -> p a d", p=128)
                    S.dma("sp", f[:].rearrange("p (a d) -> p a d", a=4), src, writes=[f])
                    e = engs[it % 3]
                    S.op(e, act(b[:], f[:], AF.Copy) if e == "act" else copy(b[:], f[:]), reads=[f], writes=[b])
                    S.dma("sp", dst, b[:].rearrange("p (a d) -> p a d", a=4), reads=[b])
                    it += 1
                S.barrier()

        def stage_peer(l, dst):
            with ExitStack() as stk:
                wq = sb(stk, "wq", [128, 8, 2048], BF16)
                for k in range(8):
                    ldcast(wq, wq[:, k, 0:1024], wq_d[l, k * 128:(k + 1) * 128, 0:1024])
                    ldcast(wq, wq[:, k, 1024:2048], wq_d[l, k * 128:(k + 1) * 128, 1024:2048])
                skT = sb(stk, "skT", [128, 2, 128], BF16)
                ldcast(skT, skT[:], skT_d[l])
                gb = sb(stk, "gb", [128, 2048])
                S.dma("sp", gb[:], pr_d[l, :, 2840:4888], writes=[gb])
                x1T_ = [sb(stk, "x1T%d" % i, [128, 8, 256], BF16) for i in range(2)]
                qT = sb(stk, "qT", [128, 16, 256], BF16)
                sab_ = [sb(stk, "sab%d" % i, [128, 8, 2, 128]) for i in range(2)]
                thr_ = [sb(stk, "thr%d" % i, [128, 8]) for i in range(2)]
                bia_ = [sb(stk, "bia%d" % i, [128, 8]) for i in range(2)]
                sv = sb(stk, "sv", [128, 2, 16])
                tmpk = sb(stk, "tmpk", [128, 128])
                cand = sb(stk, "cand", [128, 256])
                ctmp = sb(stk, "ctmp", [128, 256])
                cv = sb(stk, "cv", [128, 16])
                cex = sb(stk, "cex", [128, 16])
                negm = sb(stk, "negm", [128, 1])
                zz = sb(stk, "zz", [128, 1])
                uT_ = [sb(stk, "uTg%d" % i, [128, 8, 512], BF16) for i in range(2)]
                vg_ = [sb(stk, "vg%d" % i, [128, 4, 1024], BF16) for i in range(2)]
                abf_ = [sb(stk, "abf%d" % i, [128, 1024], BF16) for i in range(2)]
                hid_ = [sb(stk, "hid%d" % i, [128, 4, 256], BF16) for i in range(2)]
                ss_ = [sb(stk, "SS%d" % i, [128, 512]) for i in range(3)]
                ee_ = [sb(stk, "EE%d" % i, [128, 512], BF16) for i in range(3)]
                gh_ = [sb(stk, "GH%d" % i, [128, 4, 128], BF16) for i in range(3)]
                xr_ = [sb(stk, "xr%d" % i, [128, 1024]) for i in range(2)]
                t_ = [sb(stk, "t%d" % i, [128, 1024]) for i in range(2)]
                xo_ = [sb(stk, "xo%d" % i, [128, 1024]) for i in range(2)]
                stats = sb(stk, "stats", [128, 2, 6])
                mv = sb(stk, "mv", [128, 2])
                rs = sb(stk, "rs", [128, 1])
                x1Tv = x1T_d.rearrange("(k p) t -> p k t", p=128)
                uTv = uTb_d.rearrange("(k p) e -> p k e", p=128)
                bA = [BANK[0], BANK[1]]
                bG = [BANK[2], BANK[3]]
                bY = [[BANK[4], BANK[5]], [BANK[6], BANK[7]]]
                psA = P2[0]
                psG = P2[1]
                git = 0
                wit = 0
                for st_ in range(16):
                    x1T = x1T_[st_ % 2]
                    S.dma("sp", x1T[:], x1Tv[:, :, st_ * 256:(st_ + 1) * 256], writes=[x1T])
                    for hc in range(16):
                        bk = BANK[hc % 4]
                        for k in range(8):
                            S.op("pe", mm(bk.ap[:, 0:256], wq[:, k, hc * 128:(hc + 1) * 128], x1T[:, k, :], k == 0, k == 7), reads=[wq, x1T], writes=[bk])
                        if hc % 2 == 0:
                            S.op("act", act(qT[:, hc, :], bk.ap[:, 0:256], AF.Copy), reads=[bk], writes=[qT], join=True)
                        else:
                            S.op("dve", copy(qT[:, hc, :], bk.ap[:, 0:256]), reads=[bk], writes=[qT], join=True)
                    for t2 in range(2):
                        sab = sab_[t2]
                        for h in range(8):
                            bk = BANK[h % 4]
                            for c in range(2):
                                S.op("pe", mm(bk.ap[:, c * 128:(c + 1) * 128], qT[:, 2 * h + c, t2 * 128:(t2 + 1) * 128], skT[:, c, :], c == 0, c == 1), reads=[qT, skT], writes=[bk])
                            if h % 2 == 0:
                                S.op("act", act(sab[:, h, :, :], bk.ap[:, 0:256].rearrange("p (c k) -> p c k", c=2), AF.Copy), reads=[bk], writes=[sab], join=True)
                            else:
                                S.op("dve", copy(sab[:, h, :, :], bk.ap[:, 0:256].rearrange("p (c k) -> p c k", c=2)), reads=[bk], writes=[sab], join=True)
                    for t2 in range(2):
                        sab, thr, bia = sab_[t2], thr_[t2], bia_[t2]
                        for h in range(8):
                            for c in range(2):
                                S.op("dve", lambda e: e.max(out=sv[:, c, 0:8], in_=sab[:, h, c, :]), reads=[sab], writes=[sv], join=True)
                                S.op("dve", lambda e: e.match_replace(out=tmpk[:], in_to_replace=sv[:, c, 0:8], in_values=sab[:, h, c, :], imm_value=-3e38), reads=[sab, sv], writes=[tmpk])
                                S.op("dve", lambda e: e.max(out=sv[:, c, 8:16], in_=tmpk[:]), reads=[tmpk], writes=[sv], join=True)
                            S.op("dve", tt(cand[:].rearrange("p (i j) -> p i j", i=16), sv[:, 0, :].unsqueeze(2).to_broadcast([128, 16, 16]),
                                           sv[:, 1, :].unsqueeze(1).to_broadcast([128, 16, 16]), ALU.add), reads=[sv], writes=[cand])
                            S.op("dve", lambda e: e.max(out=cv[:, 0:8], in_=cand[:]), reads=[cand], writes=[cv])
                            S.op("dve", lambda e: e.match_replace(out=ctmp[:], in_to_replace=cv[:, 0:8], in_values=cand[:], imm_value=-3e38), reads=[cand, cv], writes=[ctmp])
                            S.op("dve", lambda e: e.max(out=cv[:, 8:16], in_=ctmp[:]), reads=[ctmp], writes=[cv], join=True)
                            S.op("dve", ts(negm[:], cv[:, 0:1], -1.0, None, ALU.mult), reads=[cv], writes=[negm])
                            S.op("act", act(cex[:], cv[:], AF.Exp, bias=negm[:], accum_out=zz[:]), reads=[cv, negm], writes=[cex, zz])
                            S.op("act", act(zz[:], zz[:], AF.Ln), reads=[zz], writes=[zz])
                            S.op("dve", tt(bia[:, h:h + 1], negm[:], zz[:], ALU.subtract), reads=[negm, zz], writes=[bia], join=True)
                            S.op("dve", copy(thr[:, h:h + 1], cv[:, 15:16]), reads=[cv], writes=[thr], join=True)
                    for ag in range(32):
                        uTg, vg = uT_[wit % 2], vg_[wit % 2]
                        abf, hid = abf_[wit % 2], hid_[wit % 2]
                        wit += 1
                        S.dma("sp", uTg[:], uTv[:, :, ag * 512:(ag + 1) * 512], writes=[uTg])
                        S.dma("sp", vg[:], vb_d[ag * 512:(ag + 1) * 512, :].rearrange("(a p) d -> p a d", p=128), writes=[vg])
                        for a4 in range(4):
                            bk = bA[a4 // 2]
                            for k in range(8):
                                S.op("pe", mm(psA[:, a4 * 256:(a4 + 1) * 256], uTg[:, k, a4 * 128:(a4 + 1) * 128], x1T[:, k, :], k == 0, k == 7), reads=[uTg, x1T], writes=[bk])
                        S.op("act", act(abf[:], psA[:, :], AF.Gelu_apprx_tanh), reads=[bA[0], bA[1]], writes=[abf])
                        first = [True, True]
                        for t2 in range(2):
                            sab, thr, bia = sab_[t2], thr_[t2], bia_[t2]
                            for h in range(8):
                                SS, EE, GH = ss_[git % 3], ee_[git % 3], gh_[git % 3]
                                git += 1
                                ssv = SS[:].rearrange("p (a b) -> p a b", a=4)
                                S.op("pool", tt(ssv, sab[:, h, 0, ag * 4:(ag + 1) * 4].unsqueeze(2).to_broadcast([128, 4, 128]),
                                                sab[:, h, 1, :].unsqueeze(1).to_broadcast([128, 4, 128]), ALU.add), reads=[sab], writes=[SS])
                                S.op("act", act(EE[:], SS[:], AF.Exp, bias=bia[:, h:h + 1]), reads=[SS, bia], writes=[EE])
                                S.op("dve", stt(GH[:].rearrange("p a b -> p (a b)"), SS[:], thr[:, h:h + 1], EE[:], ALU.is_ge, ALU.mult), reads=[SS, thr, EE], writes=[GH])
                                for a4 in range(4):
                                    bk = bG[a4 // 2]
                                    S.op("pe", mm(psG[:, a4 * 256 + t2 * 128:a4 * 256 + (t2 + 1) * 128], GH[:, a4, :], identb[:], first[a4 // 2], (t2 == 1 and h == 7 and a4 % 2 == 1)),
                                         reads=[GH, identb], writes=[bk])
                                    first[a4 // 2] = False
                        S.op("dve", tt(hid[:].rearrange("p a t -> p (a t)"), abf[:], psG[:, :], ALU.mult), reads=[abf, bG[0], bG[1]], writes=[hid])
                        for t2 in range(2):
                            for hf in range(2):
                                bk = bY[t2][hf]
                                for a4 in range(4):
                                    S.op("pe", mm(bk.ap, hid[:, a4, t2 * 128:(t2 + 1) * 128], vg[:, a4, hf * 512:(hf + 1) * 512], ag == 0 and a4 == 0, ag == 31 and a4 == 3),
                                         reads=[hid, vg], writes=[bk])
                    for t2 in range(2):
                        tok = st_ * 256 + t2 * 128
                        xr, t, xo = xr_[t2], t_[t2], xo_[t2]
                        S.dma("sp", xr[:], x1_d[tok:tok + 128, :], writes=[xr])
                        for hf in range(2):
                            hs = slice(hf * 512, (hf + 1) * 512)
                            S.op("dve", stt(t[:, hs], xr[:, hs], ALPHA, bY[t2][hf].ap, ALU.mult, ALU.add), reads=[xr, bY[t2][hf]], writes=[t], join=True)
                        layer_norm((stats, mv, rs), t, gb, 0, 1024, xo)
                        S.dma("sp", dst[tok:tok + 128, :], xo[:], reads=[xo])
                S.barrier()

        resid = x_in
        for l in range(nlayers):
            last = (l == NL - 1)
            if "xT" in stages:
                stage_xT(resid, xT_d)
            if "lru" in stages:
                stage_lru(l)
            if "nsa" in stages:
                stage_nsa(l)
            if "out" in stages:
                stage_out(l, resid)
            if "peer" in stages:
                stage_conv(l)
                stage_peer(l, y_out if last else resid_d)
            resid = resid_d
        S.barrier()
    return nc


def _consts():
    c = {}
    c["c_ident"] = np.eye(128, dtype=np.float32)
    t = np.arange(S_LEN)
    a_t = (t // 16).astype(np.float32)
    b_t = (t % 16).astype(np.float32)
    qpos = np.zeros((4, 8, S_LEN), np.float32)
    for h in range(8):
        sl = 2.0 ** (-(h + 1))
        qpos[0, h] = 16.0 * sl
        qpos[1, h] = sl
        qpos[2, h] = -sl * 16.0 * a_t
        qpos[3, h] = -sl * b_t
    c["c_qpos"] = qpos
    kpos = np.zeros((4, S_LEN), np.float32)
    kpos[0] = a_t
    kpos[1] = b_t
    kpos[2] = 1.0
    kpos[3] = 1.0
    c["c_kpos"] = kpos
    cc = np.arange(256)
    cpos = np.zeros((4, 256), np.float32)
    cpos[0] = cc + 1
    cpos[1] = 15.0
    cpos[2] = 1.0
    cpos[3] = 1.0
    c["c_cpos"] = cpos
    cs = np.arange(255)[:, None] * 16
    ss = np.arange(64)[None, :] * 64
    ov = np.clip(np.minimum(cs + 32, ss + 64) - np.maximum(cs, ss), 0, None)
    sm = np.zeros((256, 64), np.float32)
    sm[:255] = ov / 16.0
    c["c_selmat"] = np.ascontiguousarray(sm.reshape(2, 128, 64).transpose(1, 0, 2))
    fb = np.zeros((128, 32, 64), np.float32)
    for qi in range(32):
        tq = qi * 128 + np.arange(128)
        cur = tq // 64
        j = np.arange(64)[None, :]
        vblk = j <= cur[:, None]
        forced = (j == 0) | (j == cur[:, None]) | (j == cur[:, None] - 1)
        fb[:, qi, :] = np.where(vblk, np.where(forced, 1e4, 0.0), NEG)
    c["c_fb"] = fb
    ex = np.zeros((64, 32, 128), np.float32)
    for kj in range(32):
        ex[2 * kj, kj, 0:64] = 1.0
        ex[2 * kj + 1, kj, 64:128] = 1.0
    c["c_expand"] = ex
    return c


def _pack(inp):
    L = NL
    f = lambda k: np.asarray(inp[k], dtype=np.float32)
    b_in = f("b_in")
    pc = np.zeros((L, 128, NPC), np.float32)
    pr = np.zeros((L, 128, NPR), np.float32)

    def colpack(vec512):
        return vec512.reshape(4, 128).T
    for l in range(L):
        o = PC_OFF
        pc[l, :, o["bx"]:o["bx"] + 4] = colpack(b_in[l, 0:512])
        pc[l, :, o["bg"]:o["bg"] + 4] = colpack(b_in[l, 512:1024])
        cw = f("conv_w")[l]
        for j in range(4):
            pc[l, :, o["cw"] + j * 4:o["cw"] + j * 4 + 4] = colpack(cw[j])
        pc[l, :, o["cb"]:o["cb"] + 4] = colpack(f("conv_b")[l])
        pc[l, :, o["ba"]:o["ba"] + 4] = colpack(f("lru_ba")[l])
        pc[l, :, o["bxg"]:o["bxg"] + 4] = colpack(f("lru_bx")[l])
        pc[l, :, o["lam"]:o["lam"] + 4] = colpack(f("lru_lambda")[l])
        pc[l, :, o["gl"]:o["gl"] + 4] = colpack(f("gn_lru_g")[l])
        for h in range(8):
            pc[l, 0:64, o["bq"] + h] = b_in[l, 1024 + h * 64:1024 + (h + 1) * 64]
        for g in range(2):
            pc[l, 0:64, o["bkc"] + g] = b_in[l, 1536 + g * 64:1536 + (g + 1) * 64]
            pc[l, 0:64, o["bvc"] + g] = b_in[l, 1664 + g * 64:1664 + (g + 1) * 64]
            pc[l, 0:64, o["bks"] + g] = b_in[l, 1792 + g * 64:1792 + (g + 1) * 64]
            pc[l, 0:64, o["bkw"] + g] = b_in[l, 2048 + g * 64:2048 + (g + 1) * 64]
        pc[l, 0:64, o["kb1"]] = f("cmpk_b1")[l]
        pc[l, 0:64, o["kb2"]] = f("cmpk_b2")[l]
        pc[l, 0:64, o["vb1"]] = f("cmpv_b1")[l]
        r = PR_OFF
        pr[l, :, r["bvs"]:r["bvs"] + 128] = b_in[l, 1920:2048][None, :]
        pr[l, :, r["bvw"]:r["bvw"] + 128] = b_in[l, 2176:2304][None, :]
        pr[l, :, r["bgt"]:r["bgt"] + 24] = b_in[l, 2304:2328][None, :]
        pr[l, :, r["gn"]:r["gn"] + 512] = f("gn_nsa_g")[l][None, :]
        pr[l, :, r["l1g"]:r["l1g"] + 1024] = f("ln1_g")[l][None, :]
        pr[l, :, r["l1b"]:r["l1b"] + 1024] = f("ln1_b")[l][None, :]
        pr[l, :, r["l2g"]:r["l2g"] + 1024] = f("ln2_g")[l][None, :]
        pr[l, :, r["l2b"]:r["l2b"] + 1024] = f("ln2_b")[l][None, :]
    m = {"pc": pc, "pr": pr}
    wa = f("lru_wa")
    wx = f("lru_wx")
    wabd = np.zeros((L, 4, 128, 128), np.float32)
    wxbd = np.zeros((L, 4, 128, 128), np.float32)
    for l in range(L):
        for c in range(4):
            for j in range(2):
                wabd[l, c, j * 64:(j + 1) * 64, j * 64:(j + 1) * 64] = wa[l, 2 * c + j]
                wxbd[l, c, j * 64:(j + 1) * 64, j * 64:(j + 1) * 64] = wx[l, 2 * c + j]
    m["wa_bd"] = wabd
    m["wx_bd"] = wxbd
    m["w1k"] = np.ascontiguousarray(f("cmpk_w1").reshape(L, 32, 64, 64).transpose(0, 2, 1, 3))
    m["w1v"] = np.ascontiguousarray(f("cmpv_w1").reshape(L, 32, 64, 64).transpose(0, 2, 1, 3))
    m["w2k"] = f("cmpk_w2")
    m["w2va"] = np.ascontiguousarray(np.concatenate([f("cmpv_w2"), f("cmpv_b2")[:, None, :]], axis=1))
    pk = np.zeros((L, 64, 34), np.float32)
    pv = np.zeros((L, 64, 34), np.float32)
    pk[:, :, 0:32] = f("cmp_pos_k").transpose(0, 2, 1)
    pv[:, :, 0:32] = f("cmp_pos_v").transpose(0, 2, 1)
    m["posk"] = pk
    m["posv"] = pv
    m["w_in"] = f("w_in")
    m["w_out"] = f("w_out")
    m["wq"] = f("peer_wq")
    m["skT"] = np.ascontiguousarray(f("peer_subkeys").transpose(0, 3, 1, 2))
    m["uT"] = np.ascontiguousarray(f("peer_u").transpose(0, 2, 1))
    m["vtab"] = f("peer_v")
    m.update(_consts())
    return m


def kernel(**inputs):
    x = np.asarray(inputs["x"], dtype=np.float32)
    shared = _pack(inputs)
    nc = build()
    in_maps = []
    for b in range(8):
        d = dict(shared)
        d["x"] = np.ascontiguousarray(x[b])
        in_maps.append(d)
    res = run_bass_kernel_spmd(nc, in_maps, core_ids=list(range(8)))
    return np.stack([np.asarray(r["y"], dtype=np.float32) for r in res.results], axis=0)
```

```python
import numpy as np
from contextlib import ExitStack
import concourse.bass as bass
import concourse.mybir as mybir
from concourse.bass_utils import run_bass_kernel_spmd

F32 = mybir.dt.float32
BF16 = mybir.dt.bfloat16
AF = mybir.ActivationFunctionType
ALU = mybir.AluOpType

S_LEN = 4096
DM = 1024
NL = 2
ALPHA = (2 * NL) ** 0.25
EPS = 1e-5
NEG = -1e30

PC_SPEC = [("bx", 4), ("bg", 4), ("cw", 16), ("cb", 4), ("ba", 4), ("bxg", 4), ("lam", 4), ("gl", 4),
           ("bq", 8), ("bkc", 2), ("bvc", 2), ("bks", 2), ("bkw", 2), ("kb1", 1), ("kb2", 1), ("vb1", 1)]
PC_OFF = {}
_o = 0
for _n, _c in PC_SPEC:
    PC_OFF[_n] = _o
    _o += _c
NPC = _o
PR_SPEC = [("bvs", 128), ("bvw", 128), ("bgt", 24), ("gn", 512), ("l1g", 1024), ("l1b", 1024), ("l2g", 1024), ("l2b", 1024)]
PR_OFF = {}
_o = 0
for _n, _c in PR_SPEC:
    PR_OFF[_n] = _o
    _o += _c
NPR = _o


class Buf:
    __slots__ = ("w", "r")

    def __init__(self):
        self.w = {}
        self.r = {}


class T:
    def __init__(self, t):
        self.t = t
        self.b = Buf()

    def __getitem__(self, k):
        return self.t[k]


class V:
    def __init__(self, ap):
        self.ap = ap
        self.b = Buf()


class Sched:
    def __init__(self, nc, stack, ndma=32):
        self.nc = nc
        self.eng = {"pe": nc.tensor, "dve": nc.vector, "act": nc.scalar, "pool": nc.gpsimd, "sp": nc.sync}
        self.sem = {}
        self.cnt = {}
        for k in self.eng:
            self.sem[k] = stack.enter_context(nc.semaphore("s_" + k))
            self.cnt[k] = 0
        self.ndma = ndma
        for i in range(ndma):
            k = "d%d" % i
            self.sem[k] = stack.enter_context(nc.semaphore("s_" + k))
            self.cnt[k] = 0
        self.seen = {e: {} for e in self.eng}
        self.rr = 0

    def _deps(self, reads, writes, join):
        deps = {}

        def add(d):
            for s, v in d.items():
                if deps.get(s, 0) < v:
                    deps[s] = v
        for t in reads:
            add(t.b.w)
        for t in writes:
            if not join:
                add(t.b.w)
            add(t.b.r)
        return deps

    def _wait(self, e, deps):
        for s, v in deps.items():
            if s == "pe" and e == "pe":
                continue
            if self.seen[e].get(s, 0) >= v:
                continue
            self.eng[e].wait_ge(self.sem[s], v)
            self.seen[e][s] = v

    def _mark(self, tok, reads, writes, join):
        s, v = tok
        for t in reads:
            t.b.r[s] = v
        for t in writes:
            if join:
                t.b.w[s] = v
            else:
                t.b.w = {s: v}
                t.b.r = {}

    def op(self, e, fn, reads=(), writes=(), join=False):
        self._wait(e, self._deps(reads, writes, join))
        ins = fn(self.eng[e])
        ins.then_inc(self.sem[e], 1)
        self.cnt[e] += 1
        self._mark((e, self.cnt[e]), reads, writes, join)

    def dma(self, q, out, in_, reads=(), writes=(), join=False, **kw):
        k = "d%d" % self.rr
        self.rr = (self.rr + 1) % self.ndma
        deps = self._deps(reads, writes, join)
        if self.cnt[k] > 0:
            deps[k] = max(deps.get(k, 0), self.cnt[k])
        self._wait(q, deps)
        self.eng[q].dma_start(out=out, in_=in_, **kw).then_inc(self.sem[k], 16)
        self.cnt[k] += 16
        self._mark((k, self.cnt[k]), reads, writes, join)

    def barrier(self):
        for e in self.eng:
            for s, v in self.cnt.items():
                if v > 0 and self.seen[e].get(s, 0) < v:
                    self.eng[e].wait_ge(self.sem[s], v)
                    self.seen[e][s] = v


def build(nlayers=NL, debug=False, stages=("xT", "lru", "nsa", "out", "peer")):
    nc = bass.Bass("TRN2", target_bir_lowering=False)

    def din(name, shape, dt=F32):
        return nc.dram_tensor(name, list(shape), dt, kind="ExternalInput").ap()

    dbg = set(debug) if debug else set()

    def dscr(name, shape, dt=F32):
        return nc.dram_tensor(name, list(shape), dt, kind=("ExternalOutput" if name in dbg else "Internal")).ap()

    L = NL
    x_in = din("x", [S_LEN, DM])
    w_in = din("w_in", [L, DM, 2328])
    pc_d = din("pc", [L, 128, NPC])
    pr_d = din("pr", [L, 128, NPR])
    wa_bd = din("wa_bd", [L, 4, 128, 128])
    wx_bd = din("wx_bd", [L, 4, 128, 128])
    w1k_d = din("w1k", [L, 64, 32, 64])
    w1v_d = din("w1v", [L, 64, 32, 64])
    w2k_d = din("w2k", [L, 64, 64])
    w2va_d = din("w2va", [L, 65, 64])
    posk_d = din("posk", [L, 64, 34])
    posv_d = din("posv", [L, 64, 34])
    wout_d = din("w_out", [L, DM, DM])
    wq_d = din("wq", [L, DM, 2048])
    skT_d = din("skT", [L, 128, 2, 128])
    big = "peer" in stages
    uT_d = din("uT", [L, DM, 16384] if big else [L, 8, 8])
    v_d = din("vtab", [L, 16384, DM] if big else [L, 8, 8])
    c_ident = din("c_ident", [128, 128])
    c_qpos = din("c_qpos", [4, 8, S_LEN])
    c_kpos = din("c_kpos", [4, S_LEN])
    c_cpos = din("c_cpos", [4, 256])
    c_selmat = din("c_selmat", [128, 2, 64])
    c_fb = din("c_fb", [128, 32, 64])
    c_expand = din("c_expand", [64, 32, 128])
    y_out = nc.dram_tensor("y", [S_LEN, DM], F32, kind="ExternalOutput").ap()

    xT_d = dscr("xT_d", [DM, S_LEN], BF16)
    ylruT_d = dscr("ylruT_d", [512, S_LEN], BF16)
    ynsa_d = dscr("ynsa_d", [S_LEN, 512])
    x1_d = dscr("x1_d", [S_LEN, DM])
    x1T_d = dscr("x1T_d", [DM, S_LEN], BF16)
    resid_d = dscr("resid_d", [S_LEN, DM])
    uTb_d = dscr("uTb_d", [DM, 16384], BF16)
    vb_d = dscr("vb_d", [16384, DM], BF16)

    with ExitStack() as top:
        S = Sched(nc, top)

        uid = [0]

        def sb(stk, name, shape, dt=F32):
            uid[0] += 1
            return T(stk.enter_context(nc.sbuf_tensor("sb%d_%s" % (uid[0], name), list(shape), dt)))

        P2 = [top.enter_context(nc.psum_tensor("ps%d" % i, [128, 1024], F32)) for i in range(4)]
        BANK = [V(P2[i // 2][:, (i % 2) * 512:(i % 2 + 1) * 512]) for i in range(8)]

        def act(out, in_, func, bias=None, scale=None, accum_out=None):
            kw = {}
            if bias is not None:
                kw["bias"] = bias
            if scale is not None:
                kw["scale"] = scale
            if accum_out is not None:
                kw["accum_out"] = accum_out
            return lambda e: e.activation(out=out, in_=in_, func=func, **kw)

        def copy(out, in_):
            return lambda e: e.tensor_copy(out=out, in_=in_)

        def mm(out, lhsT, rhs, start, stop):
            return lambda e: e.matmul(out, lhsT, rhs, start=start, stop=stop)

        def tt(out, in0, in1, op):
            return lambda e: e.tensor_tensor(out=out, in0=in0, in1=in1, op=op)

        def ts(out, in0, s1, s2, op0, op1=None):
            if op1 is None:
                return lambda e: e.tensor_scalar(out=out, in0=in0, scalar1=s1, scalar2=None, op0=op0)
            return lambda e: e.tensor_scalar(out=out, in0=in0, scalar1=s1, scalar2=s2, op0=op0, op1=op1)

        def stt(out, in0, scalar, in1, op0, op1):
            return lambda e: e.scalar_tensor_tensor(out=out, in0=in0, scalar=scalar, in1=in1, op0=op0, op1=op1)

        def memset(ap, v):
            return lambda e: e.memset(ap, v)

        identf = sb(top, "identf", [128, 128])
        identb = sb(top, "identb", [128, 128], BF16)
        onesf = sb(top, "onesf", [128, 128])
        S.dma("sp", identf[:], c_ident, writes=[identf])
        S.op("dve", copy(identb[:], identf[:]), reads=[identf], writes=[identb])
        S.op("dve", memset(onesf[:], 1.0), writes=[onesf])

        def ldcast(dst_t, dst_ap, src_ap):
            S.dma("pool", dst_ap, src_ap, writes=[dst_t], join=True)

        def stage_xT(src, dst):
            with ExitStack() as stk:
                xin = [sb(stk, "xin%d" % i, [128, DM]) for i in range(2)]
                xo = [sb(stk, "xo%d" % i, [128, 8, 512], BF16) for i in range(2)]
                dstv = dst.rearrange("(k p) t -> p k t", p=128)
                for t_ in range(32):
                    xi = xin[t_ % 2]
                    S.dma("sp", xi[:], src[t_ * 128:(t_ + 1) * 128, :], writes=[xi])
                    g = t_ // 4
                    o = xo[g % 2]
                    j = t_ % 4
                    for hb in range(2):
                        bk = BANK[hb + 2 * (t_ % 2)]
                        for kk in range(4):
                            k = hb * 4 + kk
                            S.op("pe", lambda e: e.transpose(out=bk.ap[:, kk * 128:(kk + 1) * 128], in_=xi[:, k * 128:(k + 1) * 128], identity=identf[:]),
                                 reads=[xi, identf], writes=[bk])
                        S.op("dve" if hb == 0 else "act",
                             copy(o[:, hb * 4:(hb + 1) * 4, j * 128:(j + 1) * 128], bk.ap.rearrange("p (k t) -> p k t", k=4)) if hb == 0 else
                             act(o[:, hb * 4:(hb + 1) * 4, j * 128:(j + 1) * 128], bk.ap.rearrange("p (k t) -> p k t", k=4), AF.Copy),
                             reads=[bk], writes=[o], join=True)
                    if j == 3:
                        S.dma("sp", dstv[:, :, g * 512:(g + 1) * 512], o[:], reads=[o])
                S.barrier()

        def stage_lru(l):
            with ExitStack() as stk:
                xT = sb(stk, "xT", [128, 8, S_LEN], BF16)
                S.dma("sp", xT[:], xT_d.rearrange("(k p) t -> p k t", p=128), writes=[xT])
                wl = sb(stk, "wl", [128, 8, 1024], BF16)
                for k in range(8):
                    ldcast(wl, wl[:, k, :], w_in[l, k * 128:(k + 1) * 128, 0:1024])
                pc = sb(stk, "pc", [128, NPC])
                S.dma("sp", pc[:], pc_d[l], writes=[pc])
                wab = sb(stk, "wab", [128, 4, 128], BF16)
                wxb = sb(stk, "wxb", [128, 4, 128], BF16)
                for c in range(4):
                    ldcast(wab, wab[:, c, :], wa_bd[l, c])
                    ldcast(wxb, wxb[:, c, :], wx_bd[l, c])
                cch = sb(stk, "cch", [128, 4])
                cch2 = sb(stk, "cch2", [128, 4])
                tmpc = sb(stk, "tmpc", [128, 4])
                lam = pc[:, PC_OFF["lam"]:PC_OFF["lam"] + 4]
                S.op("act", act(tmpc[:], lam, AF.Exp, scale=-1.0), reads=[pc], writes=[tmpc])
                S.op("act", act(tmpc[:], tmpc[:], AF.Ln, bias=1.0), reads=[tmpc], writes=[tmpc])
                S.op("dve", ts(cch[:], tmpc[:], -8.0, None, ALU.mult), reads=[tmpc], writes=[cch])
                S.op("dve", ts(cch2[:], tmpc[:], -16.0, None, ALU.mult), reads=[tmpc], writes=[cch2])

                ylru = sb(stk, "ylru", [128, 4, S_LEN], BF16)
                ssq = sb(stk, "ssq", [128, S_LEN])

                def rot(name, shape, dt=F32, n=2):
                    return [sb(stk, "%s%d" % (name, i), shape, dt) for i in range(n)]
                xb_ = rot("xb", [128, 515])
                gg_ = rot("gg", [128, 512])
                xc_ = rot("xc", [128, 512])
                xcb_ = rot("xcb", [128, 512], BF16)
                r_ = rot("r", [128, 512])
                i_ = rot("i", [128, 512])
                a_ = rot("a", [128, 512])
                s_ = rot("s", [128, 512])
                u_ = rot("u", [128, 512])
                h_ = rot("h", [128, 512])
                y_ = rot("y", [128, 512])
                q_ = rot("ysq", [128, 512])
                it = 0
                for c in range(4):
                    def col(nm, j=0):
                        o = PC_OFF[nm] + j
                        return pc[:, o:o + 1]
                    for tb in range(8):
                        p = it % 2
                        xb, gg, xc, xcb, r, i, a, s, u, h, y, ysq = (xb_[p], gg_[p], xc_[p], xcb_[p], r_[p], i_[p], a_[p], s_[p], u_[p], h_[p], y_[p], q_[p])
                        xbp, hp = xb_[1 - p], h_[1 - p]
                        bx, bgk, br, bi = BANK[0 + 4 * p], BANK[1 + 4 * p], BANK[2 + 4 * p], BANK[3 + 4 * p]
                        tsl = slice(tb * 512, (tb + 1) * 512)
                        for k in range(8):
                            S.op("pe", mm(bx.ap, wl[:, k, c * 128:(c + 1) * 128], xT[:, k, tsl], k == 0, k == 7), reads=[wl, xT], writes=[bx])
                        for k in range(8):
                            S.op("pe", mm(bgk.ap, wl[:, k, 512 + c * 128:512 + (c + 1) * 128], xT[:, k, tsl], k == 0, k == 7), reads=[wl, xT], writes=[bgk])
                        if tb == 0:
                            S.op("dve", memset(xb[:, 0:3], 0.0), writes=[xb])
                        else:
                            S.op("dve", copy(xb[:, 0:3], xbp[:, 512:515]), reads=[xbp], writes=[xb])
                        S.op("act", act(xb[:, 3:515], bx.ap, AF.Identity, bias=col("bx", c)), reads=[bx, pc], writes=[xb], join=True)
                        S.op("act", act(gg[:], bgk.ap, AF.Gelu_apprx_tanh, bias=col("bg", c)), reads=[bgk, pc], writes=[gg])
                        S.op("dve", ts(xc[:], xb[:, 0:512], col("cw", 0 * 4 + c), col("cb", c), ALU.mult, ALU.add), reads=[xb, pc], writes=[xc])
                        for j in range(1, 4):
                            S.op("dve", stt(xc[:], xb[:, j:j + 512], col("cw", j * 4 + c), xc[:], ALU.mult, ALU.add), reads=[xb, pc, xc], writes=[xc])
                        S.op("pool", copy(xcb[:], xc[:]), reads=[xc], writes=[xcb])
                        S.op("pe", mm(br.ap, wab[:, c, :], xcb[:], True, True), reads=[wab, xcb], writes=[br])
                        S.op("pe", mm(bi.ap, wxb[:, c, :], xcb[:], True, True), reads=[wxb, xcb], writes=[bi])
                        S.op("act", act(r[:], br.ap, AF.Sigmoid, bias=col("ba", c)), reads=[br, pc], writes=[r])
                        S.op("act", act(i[:], bi.ap, AF.Sigmoid, bias=col("bxg", c)), reads=[bi, pc], writes=[i])
                        S.op("act", act(a[:], r[:], AF.Exp, scale=cch[:, c:c + 1]), reads=[r, cch], writes=[a])
                        S.op("act", act(s[:], r[:], AF.Exp, scale=cch2[:, c:c + 1]), reads=[r, cch2], writes=[s])
                        S.op("dve", ts(s[:], s[:], -1.0, 1.0, ALU.mult, ALU.add), reads=[s], writes=[s])
                        S.op("act", act(s[:], s[:], AF.Sqrt), reads=[s], writes=[s])
                        S.op("pool", tt(u[:], i[:], xc[:], ALU.mult), reads=[i, xc], writes=[u])
                        S.op("dve", tt(u[:], u[:], s[:], ALU.mult), reads=[u, s], writes=[u])
                        init = 0.0 if tb == 0 else hp[:, 511:512]
                        S.op("dve", lambda e: e.tensor_tensor_scan(out=h[:], data0=a[:], data1=u[:], initial=init, op0=ALU.mult, op1=ALU.add),
                             reads=[a, u] + ([] if tb == 0 else [hp]), writes=[h])
                        S.op("dve", tt(y[:], h[:], gg[:], ALU.mult), reads=[h, gg], writes=[y])
                        S.op("act", act(ysq[:], y[:], AF.Square), reads=[y], writes=[ysq])
                        S.op("pool", copy(ylru[:, c, tsl], y[:]), reads=[y], writes=[ylru], join=True)
                        S.op("pe", mm(bx.ap, onesf[:], ysq[:], True, True), reads=[onesf, ysq], writes=[bx])
                        if c == 0:
                            S.op("dve", copy(ssq[:, tsl], bx.ap), reads=[bx], writes=[ssq], join=True)
                        else:
                            S.op("dve", tt(ssq[:, tsl], ssq[:, tsl], bx.ap, ALU.add), reads=[bx, ssq], writes=[ssq])
                        it += 1
                S.op("dve", ts(ssq[:], ssq[:], 1.0 / 512, EPS, ALU.mult, ALU.add), reads=[ssq], writes=[ssq])
                S.op("act", act(ssq[:], ssq[:], AF.Sqrt), reads=[ssq], writes=[ssq])
                S.op("dve", lambda e: e.reciprocal(out=ssq[:], in_=ssq[:]), reads=[ssq], writes=[ssq])
                for c in range(4):
                    o = PC_OFF["gl"] + c
                    S.op("dve", stt(ylru[:, c, :], ylru[:, c, :], pc[:, o:o + 1], ssq[:], ALU.mult, ALU.mult), reads=[ylru, pc, ssq], writes=[ylru])
                S.dma("sp", ylruT_d.rearrange("(c p) t -> p c t", p=128), ylru[:], reads=[ylru])
                S.barrier()

        def stage_nsa(l):
            with ExitStack() as stk:
                xT = sb(stk, "xT", [128, 8, S_LEN], BF16)
                S.dma("sp", xT[:], xT_d.rearrange("(k p) t -> p k t", p=128), writes=[xT])
                wn = sb(stk, "wn", [128, 8, 1304], BF16)
                for k in range(8):
                    ldcast(wn, wn[:, k, 0:1024], w_in[l, k * 128:(k + 1) * 128, 1024:2048])
                    ldcast(wn, wn[:, k, 1024:1304], w_in[l, k * 128:(k + 1) * 128, 2048:2328])
                pc = sb(stk, "pc", [128, NPC])
                S.dma("sp", pc[:], pc_d[l], writes=[pc])
                prn = sb(stk, "prn", [128, 280])
                S.dma("sp", prn[:], pr_d[l, :, 0:280], writes=[prn])
                bq8 = sb(stk, "bq8", [128, 8])
                S.op("dve", ts(bq8[:], pc[:, PC_OFF["bq"]:PC_OFF["bq"] + 8], 0.125, None, ALU.mult), reads=[pc], writes=[bq8])
                fb = sb(stk, "fb", [128, 32, 64])
                S.dma("sp", fb[:], c_fb, writes=[fb])
                expd = sb(stk, "expd", [64, 32, 128], BF16)
                for j in range(4):
                    ldcast(expd, expd[:, j * 8:(j + 1) * 8, :], c_expand[:, j * 8:(j + 1) * 8, :])
                w1k = sb(stk, "w1k", [64, 32, 64], BF16)
                w1v = sb(stk, "w1v", [64, 32, 64], BF16)
                for j in range(2):
                    ldcast(w1k, w1k[:, j * 16:(j + 1) * 16, :], w1k_d[l, :, j * 16:(j + 1) * 16, :])
                    ldcast(w1v, w1v[:, j * 16:(j + 1) * 16, :], w1v_d[l, :, j * 16:(j + 1) * 16, :])
                posk = sb(stk, "posk", [64, 34], BF16)
                posv = sb(stk, "posv", [64, 34], BF16)
                ldcast(posk, posk[:], posk_d[l])
                ldcast(posv, posv[:], posv_d[l])
                w2k = sb(stk, "w2k", [64, 64], BF16)
                w2va = sb(stk, "w2va", [65, 64], BF16)
                ldcast(w2k, w2k[:], w2k_d[l])
                ldcast(w2va, w2va[:], w2va_d[l])

                qTa = sb(stk, "qTa", [68, 4, S_LEN], BF16)
                ksa = sb(stk, "ksa", [68, S_LEN], BF16)
                kwa = sb(stk, "kwa", [68, S_LEN], BF16)
                kca = sb(stk, "kca", [68, 256], BF16)
                kcr = sb(stk, "kcr", [64, S_LEN + 32], BF16)
                vcr = sb(stk, "vcr", [64, S_LEN + 32], BF16)
                vsa = sb(stk, "vsa", [128, 32, 65], BF16)
                vwa = sb(stk, "vwa", [128, 32, 65], BF16)
                vca = sb(stk, "vca", [128, 2, 129], BF16)
                gsig = sb(stk, "gsig", [128, 32, 12])
                h1k = sb(stk, "h1k", [65, 256], BF16)
                h1v = sb(stk, "h1v", [65, 256], BF16)
                b1k = sb(stk, "b1k", [64, 1])
                b1v = sb(stk, "b1v", [64, 1])
                es_ = [sb(stk, "es%d" % i, [128, 512], BF16) for i in range(4)]
                mT_ = [sb(stk, "mT%d" % i, [64, 128], BF16) for i in range(2)]
                impa = sb(stk, "impa", [128, 64])
                impt = sb(stk, "impt", [128, 64])
                selm = sb(stk, "selm", [128, 64])
                m8 = sb(stk, "m8", [128, 16])
                rd = sb(stk, "rd", [128, 3, 4])
                coef = sb(stk, "coef", [128, 3, 4])
                ya_ = [sb(stk, "ya%d" % i, [128, 256]) for i in range(2)]
                ytmp = sb(stk, "ytmp", [128, 256])
                gtmp = sb(stk, "gtmp", [128, 12])

                for g in range(2):
                    for cb in range(4):
                        csl = slice(cb * 1024, (cb + 1) * 1024)
                        ldcast(qTa, qTa[64:68, :, csl], c_qpos[:, 4 * g:4 * g + 4, csl])
                        ldcast(ksa, ksa[64:68, csl], c_kpos[:, csl])
                        ldcast(kwa, kwa[64:68, csl], c_kpos[:, csl])
                    ldcast(kca, kca[64:68, :], c_cpos)
                    S.op("pool", memset(kcr[:, S_LEN:S_LEN + 32], 0.0), writes=[kcr], join=True)
                    S.op("pool", memset(vcr[:, S_LEN:S_LEN + 32], 0.0), writes=[vcr], join=True)
                    S.op("pool", memset(vsa[:], 1.0), writes=[vsa])
                    S.op("pool", memset(vwa[:], 1.0), writes=[vwa])
                    S.op("pool", memset(vca[:], 1.0), writes=[vca])
                    ldcast(vca, vca[:, :, 65:129], c_selmat)
                    S.op("pool", memset(h1k[64:65, :], 1.0), writes=[h1k], join=True)
                    S.op("pool", memset(h1v[64:65, :], 1.0), writes=[h1v], join=True)
                    projs = []
                    for r4 in range(4):
                        hh = 4 * g + r4
                        projs.append((hh * 64, qTa, lambda sl, r4=r4: qTa[0:64, r4, sl], bq8[0:64, hh:hh + 1], 0.125, bq8))
                    projs.append((512 + g * 64, kcr, lambda sl: kcr[0:64, sl], pc[0:64, PC_OFF["bkc"] + g:PC_OFF["bkc"] + g + 1], 1.0, pc))
                    projs.append((640 + g * 64, vcr, lambda sl: vcr[0:64, sl], pc[0:64, PC_OFF["bvc"] + g:PC_OFF["bvc"] + g + 1], 1.0, pc))
                    projs.append((768 + g * 64, ksa, lambda sl: ksa[0:64, sl], pc[0:64, PC_OFF["bks"] + g:PC_OFF["bks"] + g + 1], 1.0, pc))
                    projs.append((1024 + g * 64, kwa, lambda sl: kwa[0:64, sl], pc[0:64, PC_OFF["bkw"] + g:PC_OFF["bkw"] + g + 1], 1.0, pc))
                    it = 0
                    for (coff, dt_, dfn, bias_ap, scl, bias_t) in projs:
                        for tb in range(8):
                            bk = BANK[it % 4]
                            it += 1
                            tsl = slice(tb * 512, (tb + 1) * 512)
                            for k in range(8):
                                S.op("pe", mm(bk.ap[0:64, :], wn[:, k, coff:coff + 64], xT[:, k, tsl], k == 0, k == 7), reads=[wn, xT], writes=[bk])
                            S.op("act", act(dfn(tsl), bk.ap[0:64, :], AF.Identity, bias=bias_ap, scale=scl), reads=[bk, bias_t], writes=[dt_], join=True)
                    for t_ in range(32):
                        tsl = slice(t_ * 128, (t_ + 1) * 128)
                        b1_, b2_, b3_ = BANK[4], BANK[5], BANK[6]
                        for k in range(8):
                            S.op("pe", mm(b1_.ap[:, 0:64], xT[:, k, tsl], wn[:, k, 896 + g * 64:896 + g * 64 + 64], k == 0, k == 7), reads=[wn, xT], writes=[b1_])
                        for k in range(8):
                            S.op("pe", mm(b2_.ap[:, 0:64], xT[:, k, tsl], wn[:, k, 1152 + g * 64:1152 + g * 64 + 64], k == 0, k == 7), reads=[wn, xT], writes=[b2_])
                        for k in range(8):
                            S.op("pe", mm(b3_.ap[:, 0:12], xT[:, k, tsl], wn[:, k, 1280 + g * 12:1280 + g * 12 + 12], k == 0, k == 7), reads=[wn, xT], writes=[b3_])
                        S.op("dve", tt(vsa[:, t_, 0:64], b1_.ap[:, 0:64], prn[:, g * 64:g * 64 + 64], ALU.add), reads=[b1_, prn], writes=[vsa], join=True)
                        S.op("dve", tt(vwa[:, t_, 0:64], b2_.ap[:, 0:64], prn[:, 128 + g * 64:128 + g * 64 + 64], ALU.add), reads=[b2_, prn], writes=[vwa], join=True)
                        S.op("dve", tt(gtmp[:], b3_.ap[:, 0:12], prn[:, 256 + g * 12:256 + g * 12 + 12], ALU.add), reads=[b3_, prn], writes=[gtmp])
                        S.op("act", act(gsig[:, t_, :], gtmp[:], AF.Sigmoid), reads=[gtmp], writes=[gsig], join=True)
                    for (raw, w1, pos, b1t, b1name, h1) in ((kcr, w1k, posk, b1k, "kb1", h1k), (vcr, w1v, posv, b1v, "vb1", h1v)):
                        bA, bB = BANK[0], BANK[1]
                        for l_ in range(32):
                            S.op("pe", mm(bA.ap[0:64, 0:256], w1[:, l_, :], raw[:, l_:l_ + 4096:16], l_ == 0, l_ == 31), reads=[w1, raw], writes=[bA])
                        for l_ in range(32):
                            S.op("pe", mm(bB.ap[0:64, 0:2], w1[:, l_, :], pos[:, l_:l_ + 2], l_ == 0, l_ == 31), reads=[w1, pos], writes=[bB])
                        o = PC_OFF[b1name]
                        S.op("dve", tt(b1t[:], bB.ap[0:64, 0:1], pc[0:64, o:o + 1], ALU.add), reads=[bB, pc], writes=[b1t])
                        S.op("act", act(h1[0:64, :], bA.ap[0:64, 0:256], AF.Gelu_apprx_tanh, bias=b1t[:]), reads=[bA, b1t], writes=[h1], join=True)
                    bA = BANK[2]
                    S.op("pe", mm(bA.ap[0:64, 0:256], w2k[:], h1k[0:64, :], True, True), reads=[w2k, h1k], writes=[bA])
                    o = PC_OFF["kb2"]
                    S.op("act", act(kca[0:64, :], bA.ap[0:64, 0:256], AF.Identity, bias=pc[0:64, o:o + 1]), reads=[bA, pc], writes=[kca], join=True)
                    for ch in range(2):
                        bB = BANK[3]
                        S.op("pe", mm(bB.ap[:, 0:64], h1v[0:65, ch * 128:(ch + 1) * 128], w2va[:], True, True), reads=[h1v, w2va], writes=[bB])
                        S.op("dve", copy(vca[:, ch, 0:64], bB.ap[:, 0:64]), reads=[bB], writes=[vca], join=True)

                    bOc, bI, bT, bOs, bOw = BANK[2], BANK[3], BANK[4], BANK[5], BANK[6]
                    sbank = [BANK[0], BANK[1]]
                    mbank = [BANK[7], BANK[4]]
                    sit = 0
                    eit = 0
                    for qi in range(32):
                        qsl = slice(qi * 128, (qi + 1) * 128)
                        rhsq = qTa[:, :, qsl]
                        ya = ya_[qi % 2]
                        mT = mT_[qi % 2]
                        nch = 1 if qi < 16 else 2
                        for ch in range(nch):
                            bs = sbank[sit % 2]
                            sit += 1
                            es = es_[eit % 4]
                            eit += 1
                            S.op("pe", mm(bs.ap, kca[:, ch * 128:(ch + 1) * 128], rhsq, True, True), reads=[kca, qTa], writes=[bs])
                            S.op("act", act(es[:], bs.ap, AF.Exp), reads=[bs], writes=[es])
                            S.op("pool", lambda e: e.affine_select(out=es[:].rearrange("p (r q) -> p r q", r=4), in_=es[:].rearrange("p (r q) -> p r q", r=4),
                                                                   pattern=[[0, 4], [1, 128]], compare_op=ALU.is_ge, fill=0.0,
                                                                   base=128 * qi - 2048 * ch - 31, channel_multiplier=-16), reads=[es], writes=[es])
                            for r4 in range(4):
                                S.op("pe", mm(bOc.ap[:, r4 * 65:(r4 + 1) * 65], es[:, r4 * 128:(r4 + 1) * 128], vca[:, ch, 0:65], ch == 0 and r4 == 0, ch == nch - 1 and r4 == 3),
                                     reads=[es, vca], writes=[bOc])
                            for r4 in range(4):
                                S.op("pe", mm(bI.ap[:, r4 * 64:(r4 + 1) * 64], es[:, r4 * 128:(r4 + 1) * 128], vca[:, ch, 65:129], ch == 0 and r4 == 0, ch == nch - 1 and r4 == 3),
                                     reads=[es, vca], writes=[bI])
                        ocv = bOc.ap[:, 0:260].rearrange("p (r d) -> p r d", r=4)
                        S.op("dve", ts(rd[:, 0, :], ocv[:, :, 64], 1e-30, None, ALU.add), reads=[bOc], writes=[rd])
                        S.op("dve", lambda e: e.reciprocal(out=rd[:, 0, :], in_=rd[:, 0, :]), reads=[rd], writes=[rd])
                        S.op("dve", ts(impa[:], bI.ap[:, 0:64], rd[:, 0, 0:1], None, ALU.mult), reads=[bI, rd], writes=[impa])
                        for r4 in range(1, 4):
                            S.op("dve", stt(impa[:], bI.ap[:, r4 * 64:(r4 + 1) * 64], rd[:, 0, r4:r4 + 1], impa[:], ALU.mult, ALU.add), reads=[bI, rd, impa], writes=[impa])
                        S.op("dve", tt(impa[:], impa[:], fb[:, qi, :], ALU.add), reads=[impa, fb], writes=[impa])
                        S.op("dve", lambda e: e.max(out=m8[:, 0:8], in_=impa[:]), reads=[impa], writes=[m8])
                        S.op("dve", lambda e: e.match_replace(out=impt[:], in_to_replace=m8[:, 0:8], in_values=impa[:], imm_value=-3e38), reads=[impa, m8], writes=[impt])
                        S.op("dve", lambda e: e.max(out=m8[:, 8:16], in_=impt[:]), reads=[impt], writes=[m8])
                        S.op("dve", ts(selm[:], impa[:], m8[:, 15:16], None, ALU.is_ge), reads=[impa, m8], writes=[selm])
                        S.op("pe", lambda e: e.transpose(out=bT.ap[0:64, 0:128], in_=selm[:], identity=identf[:]), reads=[selm, identf], writes=[bT])
                        S.op("act", act(mT[:], bT.ap[0:64, 0:128], AF.Copy), reads=[bT], writes=[mT])
                        for kj in range(qi + 1):
                            bs = sbank[sit % 2]
                            bm = BANK[7]
                            sit += 1
                            es = es_[eit % 4]
                            eit += 1
                            ksl = slice(kj * 128, (kj + 1) * 128)
                            S.op("pe", mm(bs.ap, ksa[:, ksl], rhsq, True, True), reads=[ksa, qTa], writes=[bs])
                            S.op("act", act(es[:], bs.ap, AF.Exp), reads=[bs], writes=[es])
                            S.op("pe", mm(bm.ap[:, 0:128], expd[:, kj, :], mT[:], True, True), reads=[expd, mT], writes=[bm])
                            esv = es[:].rearrange("p (r q) -> p r q", r=4)
                            S.op("dve", tt(esv, esv, bm.ap[:, 0:128].unsqueeze(1).to_broadcast([128, 4, 128]), ALU.mult), reads=[es, bm], writes=[es])
                            if kj == qi:
                                S.op("pool", lambda e: e.affine_select(out=esv, in_=esv, pattern=[[0, 4], [1, 128]], compare_op=ALU.is_ge, fill=0.0,
                                                                       base=0, channel_multiplier=-1), reads=[es], writes=[es])
                            for r4 in range(4):
                                S.op("pe", mm(bOs.ap[:, r4 * 65:(r4 + 1) * 65], es[:, r4 * 128:(r4 + 1) * 128], vsa[:, kj, :], kj == 0 and r4 == 0, kj == qi and r4 == 3),
                                     reads=[es, vsa], writes=[bOs])
                        k0 = max(0, qi - 4)
                        for kj in range(k0, qi + 1):
                            bs = sbank[sit % 2]
                            sit += 1
                            es = es_[eit % 4]
                            eit += 1
                            ksl = slice(kj * 128, (kj + 1) * 128)
                            S.op("pe", mm(bs.ap, kwa[:, ksl], rhsq, True, True), reads=[kwa, qTa], writes=[bs])
                            S.op("act", act(es[:], bs.ap, AF.Exp), reads=[bs], writes=[es])
                            esv = es[:].rearrange("p (r q) -> p r q", r=4)
                            if kj == qi:
                                S.op("pool", lambda e: e.affine_select(out=esv, in_=esv, pattern=[[0, 4], [1, 128]], compare_op=ALU.is_ge, fill=0.0,
                                                                       base=0, channel_multiplier=-1), reads=[es], writes=[es])
                            if kj == qi - 4:
                                S.op("pool", lambda e: e.affine_select(out=esv, in_=esv, pattern=[[0, 4], [-1, 128]], compare_op=ALU.is_ge, fill=0.0,
                                                                       base=-1, channel_multiplier=1), reads=[es], writes=[es])
                            for r4 in range(4):
                                S.op("pe", mm(bOw.ap[:, r4 * 65:(r4 + 1) * 65], es[:, r4 * 128:(r4 + 1) * 128], vwa[:, kj, :], kj == k0 and r4 == 0, kj == qi and r4 == 3),
                                     reads=[es, vwa], writes=[bOw])
                        osv = bOs.ap[:, 0:260].rearrange("p (r d) -> p r d", r=4)
                        owv = bOw.ap[:, 0:260].rearrange("p (r d) -> p r d", r=4)
                        S.op("dve", ts(rd[:, 1, :], osv[:, :, 64], 1e-30, None, ALU.add), reads=[bOs], writes=[rd], join=True)
                        S.op("dve", ts(rd[:, 2, :], owv[:, :, 64], 1e-30, None, ALU.add), reads=[bOw], writes=[rd], join=True)
                        S.op("dve", lambda e: e.reciprocal(out=rd[:, 1:3, :], in_=rd[:, 1:3, :]), reads=[rd], writes=[rd])
                        S.op("dve", tt(coef[:], rd[:], gsig[:, qi, :].rearrange("p (r b) -> p b r", b=3), ALU.mult), reads=[rd, gsig], writes=[coef])
                        yav = ya[:].rearrange("p (r d) -> p r d", r=4)
                        ytv = ytmp[:].rearrange("p (r d) -> p r d", r=4)
                        S.op("dve", tt(yav, ocv[:, :, 0:64], coef[:, 0, :].unsqueeze(2).to_broadcast([128, 4, 64]), ALU.mult), reads=[bOc, coef], writes=[ya])
                        S.op("dve", tt(ytv, osv[:, :, 0:64], coef[:, 1, :].unsqueeze(2).to_broadcast([128, 4, 64]), ALU.mult), reads=[bOs, coef], writes=[ytmp])
                        S.op("dve", tt(ya[:], ya[:], ytmp[:], ALU.add), reads=[ya, ytmp], writes=[ya])
                        S.op("dve", tt(ytv, owv[:, :, 0:64], coef[:, 2, :].unsqueeze(2).to_broadcast([128, 4, 64]), ALU.mult), reads=[bOw, coef], writes=[ytmp])
                        S.op("dve", tt(ya[:], ya[:], ytmp[:], ALU.add), reads=[ya, ytmp], writes=[ya])
                        S.dma("sp", ynsa_d[qsl, g * 256:(g + 1) * 256], ya[:], reads=[ya])
                S.barrier()

        def layer_norm(stk_tiles, t, gb, goff, boff, out):
            stats, mv, rs = stk_tiles
            S.op("dve", lambda e: e.bn_stats(out=stats[:, 0, :], in_=t[:, 0:512]), reads=[t], writes=[stats])
            S.op("dve", lambda e: e.bn_stats(out=stats[:, 1, :], in_=t[:, 512:1024]), reads=[t], writes=[stats], join=True)
            S.op("dve", lambda e: e.bn_aggr(out=mv[:], in_=stats[:].rearrange("p a b -> p (a b)")), reads=[stats], writes=[mv])
            S.op("dve", ts(rs[:], mv[:, 1:2], EPS, None, ALU.add), reads=[mv], writes=[rs])
            S.op("act", act(rs[:], rs[:], AF.Sqrt), reads=[rs], writes=[rs])
            S.op("dve", lambda e: e.reciprocal(out=rs[:], in_=rs[:]), reads=[rs], writes=[rs])
            S.op("dve", ts(out[:], t[:], mv[:, 0:1], rs[:], ALU.subtract, ALU.mult), reads=[t, mv, rs], writes=[out])
            S.op("pool", tt(out[:], out[:], gb[:, goff:goff + 1024], ALU.mult), reads=[out, gb], writes=[out])
            S.op("pool", tt(out[:], out[:], gb[:, boff:boff + 1024], ALU.add), reads=[out, gb], writes=[out])

        def stage_out(l, resid):
            with ExitStack() as stk:
                wo = sb(stk, "wo", [128, 8, 1024], BF16)
                for k in range(8):
                    ldcast(wo, wo[:, k, :], wout_d[l, k * 128:(k + 1) * 128, :])
                ylru = sb(stk, "ylru", [128, 4, S_LEN], BF16)
                S.dma("sp", ylru[:], ylruT_d.rearrange("(c p) t -> p c t", p=128), writes=[ylru])
                gb = sb(stk, "gb", [128, 2560])
                S.dma("sp", gb[:], pr_d[l, :, 280:2840], writes=[gb])
                yn_ = [sb(stk, "yn%d" % i, [128, 512]) for i in range(2)]
                ynn_ = [sb(stk, "ynn%d" % i, [128, 512]) for i in range(2)]
                nsaT_ = [sb(stk, "nsaT%d" % i, [128, 4, 128], BF16) for i in range(2)]
                xr_ = [sb(stk, "xr%d" % i, [128, 1024]) for i in range(2)]
                t_ = [sb(stk, "t%d" % i, [128, 1024]) for i in range(2)]
                x1_ = [sb(stk, "x1%d" % i, [128, 1024]) for i in range(2)]
                x1T_ = [sb(stk, "x1T%d" % i, [128, 8, 128], BF16) for i in range(2)]
                junk = sb(stk, "junk", [128, 512])
                ss_ = [sb(stk, "ss%d" % i, [128, 1]) for i in range(2)]
                stats = sb(stk, "stats", [128, 2, 6])
                mv = sb(stk, "mv", [128, 2])
                rs = sb(stk, "rs", [128, 1])
                x1Tv = x1T_d.rearrange("(k p) t -> p k t", p=128)
                for tt_ in range(32):
                    p = tt_ % 2
                    yn, ynn, nsaT, xr, t, x1, x1T, ss = yn_[p], ynn_[p], nsaT_[p], xr_[p], t_[p], x1_[p], x1T_[p], ss_[p]
                    tsl = slice(tt_ * 128, (tt_ + 1) * 128)
                    S.dma("sp", yn[:], ynsa_d[tsl, :], writes=[yn])
                    S.dma("sp", xr[:], resid[tsl, :], writes=[xr])
                    S.op("act", act(junk[:], yn[:], AF.Square, accum_out=ss[:]), reads=[yn], writes=[junk, ss])
                    S.op("dve", ts(ss[:], ss[:], 1.0 / 512, EPS, ALU.mult, ALU.add), reads=[ss], writes=[ss])
                    S.op("act", act(ss[:], ss[:], AF.Sqrt), reads=[ss], writes=[ss])
                    S.op("dve", lambda e: e.reciprocal(out=ss[:], in_=ss[:]), reads=[ss], writes=[ss])
                    S.op("dve", stt(ynn[:], yn[:], ss[:], gb[:, 0:512], ALU.mult, ALU.mult), reads=[yn, ss, gb], writes=[ynn])
                    bT = BANK[4 + p]
                    for k in range(4):
                        S.op("pe", lambda e: e.transpose(out=bT.ap[:, k * 128:(k + 1) * 128], in_=ynn[:, k * 128:(k + 1) * 128], identity=identf[:]),
                             reads=[ynn, identf], writes=[bT])
                    S.op("act", act(nsaT[:], bT.ap.rearrange("p (k t) -> p k t", k=4), AF.Copy), reads=[bT], writes=[nsaT])
                    for hf in range(2):
                        bk = BANK[hf + 2 * p]
                        hs = slice(hf * 512, (hf + 1) * 512)
                        for k in range(4):
                            S.op("pe", mm(bk.ap, ylru[:, k, tsl], wo[:, k, hs], k == 0, False), reads=[ylru, wo], writes=[bk])
                        for k in range(4):
                            S.op("pe", mm(bk.ap, nsaT[:, k, :], wo[:, 4 + k, hs], False, k == 3), reads=[nsaT, wo], writes=[bk])
                        S.op("dve", stt(t[:, hs], xr[:, hs], ALPHA, bk.ap, ALU.mult, ALU.add), reads=[xr, bk], writes=[t], join=True)
                    layer_norm((stats, mv, rs), t, gb, 512, 1536, x1)
                    S.dma("sp", x1_d[tsl, :], x1[:], reads=[x1])
                    for hb in range(2):
                        bk = BANK[6 + hb]
                        for kk in range(4):
                            k = hb * 4 + kk
                            S.op("pe", lambda e: e.transpose(out=bk.ap[:, kk * 128:(kk + 1) * 128], in_=x1[:, k * 128:(k + 1) * 128], identity=identf[:]),
                                 reads=[x1, identf], writes=[bk])
                        S.op("act", act(x1T[:, hb * 4:(hb + 1) * 4, :], bk.ap.rearrange("p (k t) -> p k t", k=4), AF.Copy), reads=[bk], writes=[x1T], join=True)
                    S.dma("sp", x1Tv[:, :, tsl], x1T[:], reads=[x1T])
                S.barrier()

        def stage_conv(l):
            with ExitStack() as stk:
                f_ = [sb(stk, "cf%d" % i, [128, 4096]) for i in range(3)]
                b_ = [sb(stk, "cb%d" % i, [128, 4096], BF16) for i in range(3)]
                it = 0
                engs = ["act", "pool", "dve"]
                for kp in range(8):
                    for cb in range(4):
                        f, b = f_[it % 3], b_[it % 3]
                        src = uT_d[l, kp * 128:(kp + 1) * 128, cb * 4096:(cb + 1) * 4096]
                        dst = uTb_d[kp * 128:(kp + 1) * 128, cb * 4096:(cb + 1) * 4096]
                        S.dma("sp", f[:], src, writes=[f])
                        e = engs[it % 3]
                        S.op(e, act(b[:], f[:], AF.Copy) if e == "act" else copy(b[:], f[:]), reads=[f], writes=[b])
                        S.dma("sp", dst, b[:], reads=[b])
                        it += 1
                for rb in range(32):
                    f, b = f_[it % 3], b_[it % 3]
                    src = v_d[l, rb * 512:(rb + 1) * 512, :].rearrange("(a p) d -> p a d", p=128)
                    dst = vb_d[rb * 512:(rb + 1) * 512, :].rearrange("(a p) d -> p a d", p=128)
                    S.dma("sp", f[:].rearrange("p (a d) -> p a d", a=4), src, writes=[f])
                    e = engs[it % 3]
                    S.op(e, act(b[:], f[:], AF.Copy) if e == "act" else copy(b[:], f[:]), reads=[f], writes=[b])
                    S.dma("sp", dst, b[:].rearrange("p (a d) -> p a d", a=4), reads=[b])
                    it += 1
                S.barrier()

        def stage_peer(l, dst):
            with ExitStack() as stk:
                wq = sb(stk, "wq", [128, 8, 2048], BF16)
                for k in range(8):
                    ldcast(wq, wq[:, k, 0:1024], wq_d[l, k * 128:(k + 1) * 128, 0:1024])
                    ldcast(wq, wq[:, k, 1024:2048], wq_d[l, k * 128:(k + 1) * 128, 1024:2048])
                skT = sb(stk, "skT", [128, 2, 128], BF16)
                ldcast(skT, skT[:], skT_d[l])
                gb = sb(stk, "gb", [128, 2048])
                S.dma("sp", gb[:], pr_d[l, :, 2840:4888], writes=[gb])
                x1T_ = [sb(stk, "x1T%d" % i, [128, 8, 256], BF16) for i in range(2)]
                qT = sb(stk, "qT", [128, 16, 256], BF16)
                sab_ = [[sb(stk, "sab%d_%d" % (j, i), [128, 8, 2, 128]) for i in range(2)] for j in range(2)]
                thr_ = [[sb(stk, "thr%d_%d" % (j, i), [128, 8]) for i in range(2)] for j in range(2)]
                bia_ = [[sb(stk, "bia%d_%d" % (j, i), [128, 8]) for i in range(2)] for j in range(2)]
                sv = sb(stk, "sv", [128, 2, 16])
                tmpk = sb(stk, "tmpk", [128, 128])
                cand = sb(stk, "cand", [128, 256])
                ctmp = sb(stk, "ctmp", [128, 256])
                cv = sb(stk, "cv", [128, 16])
                cex = sb(stk, "cex", [128, 16])
                negm = sb(stk, "negm", [128, 1])
                zz = sb(stk, "zz", [128, 1])
                uT_ = [sb(stk, "uTg%d" % i, [128, 8, 512], BF16) for i in range(2)]
                vg_ = [sb(stk, "vg%d" % i, [128, 4, 1024], BF16) for i in range(2)]
                abf_ = [sb(stk, "abf%d" % i, [128, 1024], BF16) for i in range(2)]
                hid_ = [sb(stk, "hid%d" % i, [128, 4, 256], BF16) for i in range(2)]
                NR = 48
                ee_ = [sb(stk, "EE%d" % i, [128, 128], BF16) for i in range(NR)]
                gh_ = [sb(stk, "GH%d" % i, [128, 128], BF16) for i in range(NR)]
                sabi_ = [[sb(stk, "sabi%d_%d" % (j, i), [128, 8, 128]) for i in range(2)] for j in range(2)]
                rneg_ = [[sb(stk, "rneg%d_%d" % (j, i), [128, 8, 128]) for i in range(2)] for j in range(2)]
                thrm = sb(stk, "thrm", [128, 1])
                negone = sb(stk, "negone", [128, 1])
                S.op("dve", memset(negone[:], -1.0), writes=[negone])
                xr_ = [sb(stk, "xr%d" % i, [128, 1024]) for i in range(1)] * 2
                t_ = [sb(stk, "t%d" % i, [128, 1024]) for i in range(1)] * 2
                xo_ = [sb(stk, "xo%d" % i, [128, 1024]) for i in range(1)] * 2
                stats = sb(stk, "stats", [128, 2, 6])
                mv = sb(stk, "mv", [128, 2])
                rs = sb(stk, "rs", [128, 1])
                x1Tv = x1T_d.rearrange("(k p) t -> p k t", p=128)
                uTv = uTb_d.rearrange("(k p) e -> p k e", p=128)
                bA = [BANK[0], BANK[1]]
                bG = [BANK[2], BANK[3]]
                bY = [[BANK[4], BANK[5]], [BANK[6], BANK[7]]]
                psA = P2[0]
                psG = P2[1]
                cnt = {"g": 0, "w": 0, "gh": 0}

                def emit_ab(st_):
                    x1T = x1T_[st_ % 2]
                    S.dma("sp", x1T[:], x1Tv[:, :, st_ * 256:(st_ + 1) * 256], writes=[x1T])
                    for hc in range(16):
                        bk = BANK[hc % 2]
                        for k in range(8):
                            S.op("pe", mm(bk.ap[:, 0:256], wq[:, k, hc * 128:(hc + 1) * 128], x1T[:, k, :], k == 0, k == 7), reads=[wq, x1T], writes=[bk])
                        S.op("act", act(qT[:, hc, :], bk.ap[:, 0:256], AF.Copy), reads=[bk], writes=[qT], join=True)
                    for t2 in range(2):
                        sab = sab_[st_ % 2][t2]
                        for h in range(8):
                            bk = BANK[h % 2]
                            for c in range(2):
                                S.op("pe", mm(bk.ap[:, c * 128:(c + 1) * 128], qT[:, 2 * h + c, t2 * 128:(t2 + 1) * 128], skT[:, c, :], c == 0, c == 1), reads=[qT, skT], writes=[bk])
                            S.op("act", act(sab[:, h, :, :], bk.ap[:, 0:256].rearrange("p (c k) -> p c k", c=2), AF.Copy), reads=[bk], writes=[sab], join=True)
                    for t2 in range(2):
                        sab, thr, bia = sab_[st_ % 2][t2], thr_[st_ % 2][t2], bia_[st_ % 2][t2]
                        for h in range(8):
                            for c in range(2):
                                S.op("dve", lambda e: e.max(out=sv[:, c, 0:8], in_=sab[:, h, c, :]), reads=[sab], writes=[sv], join=True)
                                S.op("dve", lambda e: e.match_replace(out=tmpk[:], in_to_replace=sv[:, c, 0:8], in_values=sab[:, h, c, :], imm_value=-3e38), reads=[sab, sv], writes=[tmpk])
                                S.op("dve", lambda e: e.max(out=sv[:, c, 8:16], in_=tmpk[:]), reads=[tmpk], writes=[sv], join=True)
                            S.op("dve", tt(cand[:].rearrange("p (i j) -> p i j", i=16), sv[:, 0, :].unsqueeze(2).to_broadcast([128, 16, 16]),
                                           sv[:, 1, :].unsqueeze(1).to_broadcast([128, 16, 16]), ALU.add), reads=[sv], writes=[cand])
                            S.op("dve", lambda e: e.max(out=cv[:, 0:8], in_=cand[:]), reads=[cand], writes=[cv])
                            S.op("dve", lambda e: e.match_replace(out=ctmp[:], in_to_replace=cv[:, 0:8], in_values=cand[:], imm_value=-3e38), reads=[cand, cv], writes=[ctmp])
                            S.op("dve", lambda e: e.max(out=cv[:, 8:16], in_=ctmp[:]), reads=[ctmp], writes=[cv], join=True)
                            S.op("dve", ts(negm[:], cv[:, 0:1], -1.0, None, ALU.mult), reads=[cv], writes=[negm])
                            S.op("act", act(cex[:], cv[:], AF.Exp, bias=negm[:], accum_out=zz[:]), reads=[cv, negm], writes=[cex, zz])
                            S.op("act", act(zz[:], zz[:], AF.Ln), reads=[zz], writes=[zz])
                            S.op("dve", tt(bia[:, h:h + 1], negm[:], zz[:], ALU.subtract), reads=[negm, zz], writes=[bia], join=True)
                            S.op("dve", ts(thrm[:], cv[:, 15:16], -4e-6, None, ALU.add), reads=[cv], writes=[thrm])
                            sabi, rneg = sabi_[st_ % 2][t2], rneg_[st_ % 2][t2]
                            S.op("dve", ts(sabi[:, h, :], sab[:, h, 0, :], bia[:, h:h + 1], None, ALU.add), reads=[sab, bia], writes=[sabi], join=True)
                            S.op("dve", ts(rneg[:, h, :], sab[:, h, 0, :], negone[:], thrm[:], ALU.mult, ALU.add), reads=[sab, thrm, negone], writes=[rneg], join=True)

                def emit_A(st_, ag):
                    x1T = x1T_[st_ % 2]
                    n = st_ * 32 + ag
                    uTg = uT_[n % 2]
                    abf = abf_[n % 2]
                    S.dma("sp", uTg[:], uTv[:, :, ag * 512:(ag + 1) * 512], writes=[uTg])
                    for a4 in range(4):
                        bk = bA[a4 // 2]
                        for k in range(8):
                            S.op("pe", mm(psA[:, a4 * 256:(a4 + 1) * 256], uTg[:, k, a4 * 128:(a4 + 1) * 128], x1T[:, k, :], k == 0, k == 7), reads=[uTg, x1T], writes=[bk])
                        if a4 % 2 == 1:
                            hb = a4 // 2
                            S.op("act", act(abf[:, hb * 512:(hb + 1) * 512], bk.ap, AF.Gelu_apprx_tanh), reads=[bk], writes=[abf], join=True)

                def emit_G(st_, ag, t2, first):
                    sab = sab_[st_ % 2][t2]
                    sabi, rneg = sabi_[st_ % 2][t2], rneg_[st_ % 2][t2]
                    for h in range(8):
                        for a4 in range(4):
                            g = cnt["g"]
                            cnt["g"] += 1
                            EE, GH = ee_[g % NR], gh_[g % NR]
                            a = ag * 4 + a4
                            S.op("act", act(EE[:], sab[:, h, 1, :], AF.Exp, bias=sabi[:, h, a:a + 1]), reads=[sab, sabi], writes=[EE])
                            S.op("dve", stt(GH[:], sab[:, h, 1, :], rneg[:, h, a:a + 1], EE[:], ALU.is_ge, ALU.mult), reads=[sab, rneg, EE], writes=[GH])
                            bk = bG[a4 // 2]
                            S.op("pe", mm(psG[:, a4 * 256 + t2 * 128:a4 * 256 + (t2 + 1) * 128], GH[:], identb[:], first[a4 // 2], (t2 == 1 and h == 7 and a4 % 2 == 1)),
                                 reads=[GH, identb], writes=[bk])
                            first[a4 // 2] = False

                def emit_HV(st_, ag):
                    n = st_ * 32 + ag
                    abf, hid, vg = abf_[n % 2], hid_[n % 2], vg_[n % 2]
                    for hb in range(2):
                        S.op("dve", tt(hid[:, 2 * hb:2 * hb + 2, :].rearrange("p a t -> p (a t)"), abf[:, hb * 512:(hb + 1) * 512], bG[hb].ap, ALU.mult),
                             reads=[abf, bG[hb]], writes=[hid], join=True)
                    for t2 in range(2):
                        for hf in range(2):
                            bk = bY[t2][hf]
                            for a4 in range(4):
                                S.op("pe", mm(bk.ap, hid[:, a4, t2 * 128:(t2 + 1) * 128], vg[:, a4, hf * 512:(hf + 1) * 512], ag == 0 and a4 == 0, ag == 31 and a4 == 3),
                                     reads=[hid, vg], writes=[bk])

                def emit_epi(st_):
                    for t2 in range(2):
                        tok = st_ * 256 + t2 * 128
                        xr, t, xo = xr_[t2], t_[t2], xo_[t2]
                        S.dma("sp", xr[:], x1_d[tok:tok + 128, :], writes=[xr])
                        for hf in range(2):
                            hs = slice(hf * 512, (hf + 1) * 512)
                            S.op("dve", stt(t[:, hs], xr[:, hs], ALPHA, bY[t2][hf].ap, ALU.mult, ALU.add), reads=[xr, bY[t2][hf]], writes=[t], join=True)
                        layer_norm((stats, mv, rs), t, gb, 0, 1024, xo)
                        S.dma("sp", dst[tok:tok + 128, :], xo[:], reads=[xo])

                emit_ab(0)
                emit_A(0, 0)
                for n in range(512):
                    st_, ag = divmod(n, 32)
                    if ag == 12 and st_ + 1 < 16:
                        emit_ab(st_ + 1)
                    vg = vg_[n % 2]
                    S.dma("sp", vg[:], vb_d[ag * 512:(ag + 1) * 512, :].rearrange("(a p) d -> p a d", p=128), writes=[vg])
                    first = [True, True]
                    emit_G(st_, ag, 0, first)
                    if n + 1 < 512:
                        emit_A(*divmod(n + 1, 32))
                    emit_G(st_, ag, 1, first)
                    emit_HV(st_, ag)
                    if ag == 31:
                        emit_epi(st_)
                S.barrier()

        resid = x_in
        for l in range(nlayers):
            last = (l == NL - 1)
            if "xT" in stages:
                stage_xT(resid, xT_d)
            if "lru" in stages:
                stage_lru(l)
            if "nsa" in stages:
                stage_nsa(l)
            if "out" in stages:
                stage_out(l, resid)
            if "peer" in stages:
                stage_conv(l)
                stage_peer(l, y_out if last else resid_d)
            resid = resid_d
        S.barrier()
    return nc


def _consts():
    c = {}
    c["c_ident"] = np.eye(128, dtype=np.float32)
    t = np.arange(S_LEN)
    a_t = (t // 16).astype(np.float32)
    b_t = (t % 16).astype(np.float32)
    qpos = np.zeros((4, 8, S_LEN), np.float32)
    for h in range(8):
        sl = 2.0 ** (-(h + 1))
        qpos[0, h] = 16.0 * sl
        qpos[1, h] = sl
        qpos[2, h] = -sl * 16.0 * a_t
        qpos[3, h] = -sl * b_t
    c["c_qpos"] = qpos
    kpos = np.zeros((4, S_LEN), np.float32)
    kpos[0] = a_t
    kpos[1] = b_t
    kpos[2] = 1.0
    kpos[3] = 1.0
    c["c_kpos"] = kpos
    cc = np.arange(256)
    cpos = np.zeros((4, 256), np.float32)
    cpos[0] = cc + 1
    cpos[1] = 15.0
    cpos[2] = 1.0
    cpos[3] = 1.0
    c["c_cpos"] = cpos
    cs = np.arange(255)[:, None] * 16
    ss = np.arange(64)[None, :] * 64
    ov = np.clip(np.minimum(cs + 32, ss + 64) - np.maximum(cs, ss), 0, None)
    sm = np.zeros((256, 64), np.float32)
    sm[:255] = ov / 16.0
    c["c_selmat"] = np.ascontiguousarray(sm.reshape(2, 128, 64).transpose(1, 0, 2))
    fb = np.zeros((128, 32, 64), np.float32)
    for qi in range(32):
        tq = qi * 128 + np.arange(128)
        cur = tq // 64
        j = np.arange(64)[None, :]
        vblk = j <= cur[:, None]
        forced = (j == 0) | (j == cur[:, None]) | (j == cur[:, None] - 1)
        fb[:, qi, :] = np.where(vblk, np.where(forced, 1e4, 0.0), NEG)
    c["c_fb"] = fb
    ex = np.zeros((64, 32, 128), np.float32)
    for kj in range(32):
        ex[2 * kj, kj, 0:64] = 1.0
        ex[2 * kj + 1, kj, 64:128] = 1.0
    c["c_expand"] = ex
    return c


def _pack(inp):
    L = NL
    f = lambda k: np.asarray(inp[k], dtype=np.float32)
    b_in = f("b_in")
    pc = np.zeros((L, 128, NPC), np.float32)
    pr = np.zeros((L, 128, NPR), np.float32)

    def colpack(vec512):
        return vec512.reshape(4, 128).T
    for l in range(L):
        o = PC_OFF
        pc[l, :, o["bx"]:o["bx"] + 4] = colpack(b_in[l, 0:512])
        pc[l, :, o["bg"]:o["bg"] + 4] = colpack(b_in[l, 512:1024])
        cw = f("conv_w")[l]
        for j in range(4):
            pc[l, :, o["cw"] + j * 4:o["cw"] + j * 4 + 4] = colpack(cw[j])
        pc[l, :, o["cb"]:o["cb"] + 4] = colpack(f("conv_b")[l])
        pc[l, :, o["ba"]:o["ba"] + 4] = colpack(f("lru_ba")[l])
        pc[l, :, o["bxg"]:o["bxg"] + 4] = colpack(f("lru_bx")[l])
        pc[l, :, o["lam"]:o["lam"] + 4] = colpack(f("lru_lambda")[l])
        pc[l, :, o["gl"]:o["gl"] + 4] = colpack(f("gn_lru_g")[l])
        for h in range(8):
            pc[l, 0:64, o["bq"] + h] = b_in[l, 1024 + h * 64:1024 + (h + 1) * 64]
        for g in range(2):
            pc[l, 0:64, o["bkc"] + g] = b_in[l, 1536 + g * 64:1536 + (g + 1) * 64]
            pc[l, 0:64, o["bvc"] + g] = b_in[l, 1664 + g * 64:1664 + (g + 1) * 64]
            pc[l, 0:64, o["bks"] + g] = b_in[l, 1792 + g * 64:1792 + (g + 1) * 64]
            pc[l, 0:64, o["bkw"] + g] = b_in[l, 2048 + g * 64:2048 + (g + 1) * 64]
        pc[l, 0:64, o["kb1"]] = f("cmpk_b1")[l]
        pc[l, 0:64, o["kb2"]] = f("cmpk_b2")[l]
        pc[l, 0:64, o["vb1"]] = f("cmpv_b1")[l]
        r = PR_OFF
        pr[l, :, r["bvs"]:r["bvs"] + 128] = b_in[l, 1920:2048][None, :]
        pr[l, :, r["bvw"]:r["bvw"] + 128] = b_in[l, 2176:2304][None, :]
        pr[l, :, r["bgt"]:r["bgt"] + 24] = b_in[l, 2304:2328][None, :]
        pr[l, :, r["gn"]:r["gn"] + 512] = f("gn_nsa_g")[l][None, :]
        pr[l, :, r["l1g"]:r["l1g"] + 1024] = f("ln1_g")[l][None, :]
        pr[l, :, r["l1b"]:r["l1b"] + 1024] = f("ln1_b")[l][None, :]
        pr[l, :, r["l2g"]:r["l2g"] + 1024] = f("ln2_g")[l][None, :]
        pr[l, :, r["l2b"]:r["l2b"] + 1024] = f("ln2_b")[l][None, :]
    m = {"pc": pc, "pr": pr}
    wa = f("lru_wa")
    wx = f("lru_wx")
    wabd = np.zeros((L, 4, 128, 128), np.float32)
    wxbd = np.zeros((L, 4, 128, 128), np.float32)
    for l in range(L):
        for c in range(4):
            for j in range(2):
                wabd[l, c, j * 64:(j + 1) * 64, j * 64:(j + 1) * 64] = wa[l, 2 * c + j]
                wxbd[l, c, j * 64:(j + 1) * 64, j * 64:(j + 1) * 64] = wx[l, 2 * c + j]
    m["wa_bd"] = wabd
    m["wx_bd"] = wxbd
    m["w1k"] = np.ascontiguousarray(f("cmpk_w1").reshape(L, 32, 64, 64).transpose(0, 2, 1, 3))
    m["w1v"] = np.ascontiguousarray(f("cmpv_w1").reshape(L, 32, 64, 64).transpose(0, 2, 1, 3))
    m["w2k"] = f("cmpk_w2")
    m["w2va"] = np.ascontiguousarray(np.concatenate([f("cmpv_w2"), f("cmpv_b2")[:, None, :]], axis=1))
    pk = np.zeros((L, 64, 34), np.float32)
    pv = np.zeros((L, 64, 34), np.float32)
    pk[:, :, 0:32] = f("cmp_pos_k").transpose(0, 2, 1)
    pv[:, :, 0:32] = f("cmp_pos_v").transpose(0, 2, 1)
    m["posk"] = pk
    m["posv"] = pv
    m["w_in"] = f("w_in")
    m["w_out"] = f("w_out")
    m["wq"] = f("peer_wq")
    m["skT"] = np.ascontiguousarray(f("peer_subkeys").transpose(0, 3, 1, 2))
    m["uT"] = np.ascontiguousarray(f("peer_u").transpose(0, 2, 1))
    m["vtab"] = f("peer_v")
    m.update(_consts())
    return m


def kernel(**inputs):
    x = np.asarray(inputs["x"], dtype=np.float32)
    shared = _pack(inputs)
    nc = build()
    in_maps = []
    for b in range(8):
        d = dict(shared)
        d["x"] = np.ascontiguousarray(x[b])
        in_maps.append(d)
    res = run_bass_kernel_spmd(nc, in_maps, core_ids=list(range(8)))
    return np.stack([np.asarray(r["y"], dtype=np.float32) for r in res.results], axis=0)
```

```python
import numpy as np
from contextlib import ExitStack
import concourse.bass as bass
import concourse.mybir as mybir
from concourse.bass_utils import run_bass_kernel_spmd

F32 = mybir.dt.float32
BF16 = mybir.dt.bfloat16
AF = mybir.ActivationFunctionType
ALU = mybir.AluOpType

S_LEN = 4096
DM = 1024
NL = 2
ALPHA = (2 * NL) ** 0.25
EPS = 1e-5
NEG = -1e30

PC_SPEC = [("bx", 4), ("bg", 4), ("cw", 16), ("cb", 4), ("ba", 4), ("bxg", 4), ("lam", 4), ("gl", 4),
           ("bq", 8), ("bkc", 2), ("bvc", 2), ("bks", 2), ("bkw", 2), ("kb1", 1), ("kb2", 1), ("vb1", 1)]
PC_OFF = {}
_o = 0
for _n, _c in PC_SPEC:
    PC_OFF[_n] = _o
    _o += _c
NPC = _o
PR_SPEC = [("bvs", 128), ("bvw", 128), ("bgt", 24), ("gn", 512), ("l1g", 1024), ("l1b", 1024), ("l2g", 1024), ("l2b", 1024)]
PR_OFF = {}
_o = 0
for _n, _c in PR_SPEC:
    PR_OFF[_n] = _o
    _o += _c
NPR = _o


class Buf:
    __slots__ = ("w", "r")

    def __init__(self):
        self.w = {}
        self.r = {}


class T:
    def __init__(self, t):
        self.t = t
        self.b = Buf()

    def __getitem__(self, k):
        return self.t[k]


class V:
    def __init__(self, ap):
        self.ap = ap
        self.b = Buf()


class Sched:
    def __init__(self, nc, stack, ndma=32):
        self.nc = nc
        self.eng = {"pe": nc.tensor, "dve": nc.vector, "act": nc.scalar, "pool": nc.gpsimd, "sp": nc.sync}
        self.sem = {}
        self.cnt = {}
        for k in self.eng:
            self.sem[k] = stack.enter_context(nc.semaphore("s_" + k))
            self.cnt[k] = 0
        self.ndma = ndma
        for i in range(ndma):
            k = "d%d" % i
            self.sem[k] = stack.enter_context(nc.semaphore("s_" + k))
            self.cnt[k] = 0
        self.seen = {e: {} for e in self.eng}
        self.rr = 0

    def _deps(self, reads, writes, join):
        deps = {}

        def add(d):
            for s, v in d.items():
                if deps.get(s, 0) < v:
                    deps[s] = v
        for t in reads:
            add(t.b.w)
        for t in writes:
            if not join:
                add(t.b.w)
            add(t.b.r)
        return deps

    def _wait(self, e, deps):
        for s, v in deps.items():
            if s == "pe" and e == "pe":
                continue
            if self.seen[e].get(s, 0) >= v:
                continue
            self.eng[e].wait_ge(self.sem[s], v)
            self.seen[e][s] = v

    def _mark(self, tok, reads, writes, join):
        s, v = tok
        for t in reads:
            t.b.r[s] = v
        for t in writes:
            if join:
                t.b.w[s] = v
            else:
                t.b.w = {s: v}
                t.b.r = {}

    def op(self, e, fn, reads=(), writes=(), join=False):
        self._wait(e, self._deps(reads, writes, join))
        ins = fn(self.eng[e])
        ins.then_inc(self.sem[e], 1)
        self.cnt[e] += 1
        self._mark((e, self.cnt[e]), reads, writes, join)

    def dma(self, q, out, in_, reads=(), writes=(), join=False, **kw):
        k = "d%d" % self.rr
        self.rr = (self.rr + 1) % self.ndma
        deps = self._deps(reads, writes, join)
        if self.cnt[k] > 0:
            deps[k] = max(deps.get(k, 0), self.cnt[k])
        self._wait(q, deps)
        self.eng[q].dma_start(out=out, in_=in_, **kw).then_inc(self.sem[k], 16)
        self.cnt[k] += 16
        self._mark((k, self.cnt[k]), reads, writes, join)

    def barrier(self):
        for e in self.eng:
            for s, v in self.cnt.items():
                if v > 0 and self.seen[e].get(s, 0) < v:
                    self.eng[e].wait_ge(self.sem[s], v)
                    self.seen[e][s] = v


def build(nlayers=NL, debug=False, stages=("xT", "lru", "nsa", "out", "peer")):
    nc = bass.Bass("TRN2", target_bir_lowering=False)

    def din(name, shape, dt=F32):
        return nc.dram_tensor(name, list(shape), dt, kind="ExternalInput").ap()

    dbg = set(debug) if debug else set()

    def dscr(name, shape, dt=F32):
        return nc.dram_tensor(name, list(shape), dt, kind=("ExternalOutput" if name in dbg else "Internal")).ap()

    L = NL
    x_in = din("x", [S_LEN, DM])
    w_in = din("w_in", [L, DM, 2328])
    pc_d = din("pc", [L, 128, NPC])
    pr_d = din("pr", [L, 128, NPR])
    wa_bd = din("wa_bd", [L, 4, 128, 128])
    wx_bd = din("wx_bd", [L, 4, 128, 128])
    w1k_d = din("w1k", [L, 64, 32, 64])
    w1v_d = din("w1v", [L, 64, 32, 64])
    w2k_d = din("w2k", [L, 64, 64])
    w2va_d = din("w2va", [L, 65, 64])
    posk_d = din("posk", [L, 64, 34])
    posv_d = din("posv", [L, 64, 34])
    wout_d = din("w_out", [L, DM, DM])
    wq_d = din("wq", [L, DM, 2048])
    skT_d = din("skT", [L, 128, 2, 128])
    big = "peer" in stages
    uT_d = din("uT", [L, DM, 16384] if big else [L, 8, 8])
    v_d = din("vtab", [L, 16384, DM] if big else [L, 8, 8])
    c_ident = din("c_ident", [128, 128])
    c_qpos = din("c_qpos", [4, 8, S_LEN])
    c_kpos = din("c_kpos", [4, S_LEN])
    c_cpos = din("c_cpos", [4, 256])
    c_selmat = din("c_selmat", [128, 2, 64])
    c_fb = din("c_fb", [128, 32, 64])
    c_expand = din("c_expand", [64, 32, 128])
    y_out = nc.dram_tensor("y", [S_LEN, DM], F32, kind="ExternalOutput").ap()

    xT_d = dscr("xT_d", [DM, S_LEN], BF16)
    ylruT_d = dscr("ylruT_d", [512, S_LEN], BF16)
    ynsa_d = dscr("ynsa_d", [S_LEN, 512])
    x1_d = dscr("x1_d", [S_LEN, DM])
    x1T_d = dscr("x1T_d", [DM, S_LEN], BF16)
    resid_d = dscr("resid_d", [S_LEN, DM])
    uTb_l = [dscr("uTb_d%d" % i, [DM, 16384], BF16) for i in range(NL)]
    vb_l = [dscr("vb_d%d" % i, [16384, DM], BF16) for i in range(NL)]

    with ExitStack() as top:
        S = Sched(nc, top)

        uid = [0]

        def sb(stk, name, shape, dt=F32):
            uid[0] += 1
            return T(stk.enter_context(nc.sbuf_tensor("sb%d_%s" % (uid[0], name), list(shape), dt)))

        P2 = [top.enter_context(nc.psum_tensor("ps%d" % i, [128, 1024], F32)) for i in range(4)]
        BANK = [V(P2[i // 2][:, (i % 2) * 512:(i % 2 + 1) * 512]) for i in range(8)]

        def act(out, in_, func, bias=None, scale=None, accum_out=None):
            kw = {}
            if bias is not None:
                kw["bias"] = bias
            if scale is not None:
                kw["scale"] = scale
            if accum_out is not None:
                kw["accum_out"] = accum_out
            return lambda e: e.activation(out=out, in_=in_, func=func, **kw)

        def copy(out, in_):
            return lambda e: e.tensor_copy(out=out, in_=in_)

        def mm(out, lhsT, rhs, start, stop):
            return lambda e: e.matmul(out, lhsT, rhs, start=start, stop=stop)

        def tt(out, in0, in1, op):
            return lambda e: e.tensor_tensor(out=out, in0=in0, in1=in1, op=op)

        def ts(out, in0, s1, s2, op0, op1=None):
            if op1 is None:
                return lambda e: e.tensor_scalar(out=out, in0=in0, scalar1=s1, scalar2=None, op0=op0)
            return lambda e: e.tensor_scalar(out=out, in0=in0, scalar1=s1, scalar2=s2, op0=op0, op1=op1)

        def stt(out, in0, scalar, in1, op0, op1):
            return lambda e: e.scalar_tensor_tensor(out=out, in0=in0, scalar=scalar, in1=in1, op0=op0, op1=op1)

        def memset(ap, v):
            return lambda e: e.memset(ap, v)

        identf = sb(top, "identf", [128, 128])
        identb = sb(top, "identb", [128, 128], BF16)
        onesf = sb(top, "onesf", [128, 128])
        S.dma("sp", identf[:], c_ident, writes=[identf])
        S.op("dve", copy(identb[:], identf[:]), reads=[identf], writes=[identb])
        S.op("dve", memset(onesf[:], 1.0), writes=[onesf])

        def ldcast(dst_t, dst_ap, src_ap):
            S.dma("pool", dst_ap, src_ap, writes=[dst_t], join=True)

        def stage_xT(src, dst):
            with ExitStack() as stk:
                xin = [sb(stk, "xin%d" % i, [128, DM]) for i in range(2)]
                xo = [sb(stk, "xo%d" % i, [128, 8, 512], BF16) for i in range(2)]
                dstv = dst.rearrange("(k p) t -> p k t", p=128)
                for t_ in range(32):
                    xi = xin[t_ % 2]
                    S.dma("sp", xi[:], src[t_ * 128:(t_ + 1) * 128, :], writes=[xi])
                    g = t_ // 4
                    o = xo[g % 2]
                    j = t_ % 4
                    for hb in range(2):
                        bk = BANK[hb + 2 * (t_ % 2)]
                        for kk in range(4):
                            k = hb * 4 + kk
                            S.op("pe", lambda e: e.transpose(out=bk.ap[:, kk * 128:(kk + 1) * 128], in_=xi[:, k * 128:(k + 1) * 128], identity=identf[:]),
                                 reads=[xi, identf], writes=[bk])
                        S.op("dve" if hb == 0 else "act",
                             copy(o[:, hb * 4:(hb + 1) * 4, j * 128:(j + 1) * 128], bk.ap.rearrange("p (k t) -> p k t", k=4)) if hb == 0 else
                             act(o[:, hb * 4:(hb + 1) * 4, j * 128:(j + 1) * 128], bk.ap.rearrange("p (k t) -> p k t", k=4), AF.Copy),
                             reads=[bk], writes=[o], join=True)
                    if j == 3:
                        S.dma("sp", dstv[:, :, g * 512:(g + 1) * 512], o[:], reads=[o])
                    pump(4)
                S.barrier()

        def stage_lru(l):
            with ExitStack() as stk:
                xT = sb(stk, "xT", [128, 8, S_LEN], BF16)
                S.dma("sp", xT[:], xT_d.rearrange("(k p) t -> p k t", p=128), writes=[xT])
                wl = sb(stk, "wl", [128, 8, 1024], BF16)
                for k in range(8):
                    ldcast(wl, wl[:, k, :], w_in[l, k * 128:(k + 1) * 128, 0:1024])
                pc = sb(stk, "pc", [128, NPC])
                S.dma("sp", pc[:], pc_d[l], writes=[pc])
                wab = sb(stk, "wab", [128, 4, 128], BF16)
                wxb = sb(stk, "wxb", [128, 4, 128], BF16)
                for c in range(4):
                    ldcast(wab, wab[:, c, :], wa_bd[l, c])
                    ldcast(wxb, wxb[:, c, :], wx_bd[l, c])
                cch = sb(stk, "cch", [128, 4])
                cch2 = sb(stk, "cch2", [128, 4])
                tmpc = sb(stk, "tmpc", [128, 4])
                lam = pc[:, PC_OFF["lam"]:PC_OFF["lam"] + 4]
                S.op("act", act(tmpc[:], lam, AF.Exp, scale=-1.0), reads=[pc], writes=[tmpc])
                S.op("act", act(tmpc[:], tmpc[:], AF.Ln, bias=1.0), reads=[tmpc], writes=[tmpc])
                S.op("dve", ts(cch[:], tmpc[:], -8.0, None, ALU.mult), reads=[tmpc], writes=[cch])
                S.op("dve", ts(cch2[:], tmpc[:], -16.0, None, ALU.mult), reads=[tmpc], writes=[cch2])

                ylru = sb(stk, "ylru", [128, 4, S_LEN], BF16)
                ssq = sb(stk, "ssq", [128, S_LEN])

                def rot(name, shape, dt=F32, n=2):
                    return [sb(stk, "%s%d" % (name, i), shape, dt) for i in range(n)]
                xb_ = rot("xb", [128, 515])
                gg_ = rot("gg", [128, 512])
                xc_ = rot("xc", [128, 512])
                xcb_ = rot("xcb", [128, 512], BF16)
                r_ = rot("r", [128, 512])
                i_ = rot("i", [128, 512])
                a_ = rot("a", [128, 512])
                s_ = rot("s", [128, 512])
                u_ = rot("u", [128, 512])
                h_ = rot("h", [128, 512])
                y_ = rot("y", [128, 512])
                q_ = rot("ysq", [128, 512])
                it = 0
                for c in range(4):
                    def col(nm, j=0):
                        o = PC_OFF[nm] + j
                        return pc[:, o:o + 1]
                    for tb in range(8):
                        p = it % 2
                        xb, gg, xc, xcb, r, i, a, s, u, h, y, ysq = (xb_[p], gg_[p], xc_[p], xcb_[p], r_[p], i_[p], a_[p], s_[p], u_[p], h_[p], y_[p], q_[p])
                        xbp, hp = xb_[1 - p], h_[1 - p]
                        bx, bgk, br, bi = BANK[0 + 4 * p], BANK[1 + 4 * p], BANK[2 + 4 * p], BANK[3 + 4 * p]
                        tsl = slice(tb * 512, (tb + 1) * 512)
                        for k in range(8):
                            S.op("pe", mm(bx.ap, wl[:, k, c * 128:(c + 1) * 128], xT[:, k, tsl], k == 0, k == 7), reads=[wl, xT], writes=[bx])
                        for k in range(8):
                            S.op("pe", mm(bgk.ap, wl[:, k, 512 + c * 128:512 + (c + 1) * 128], xT[:, k, tsl], k == 0, k == 7), reads=[wl, xT], writes=[bgk])
                        if tb == 0:
                            S.op("dve", memset(xb[:, 0:3], 0.0), writes=[xb])
                        else:
                            S.op("dve", copy(xb[:, 0:3], xbp[:, 512:515]), reads=[xbp], writes=[xb])
                        S.op("act", act(xb[:, 3:515], bx.ap, AF.Identity, bias=col("bx", c)), reads=[bx, pc], writes=[xb], join=True)
                        S.op("act", act(gg[:], bgk.ap, AF.Gelu_apprx_tanh, bias=col("bg", c)), reads=[bgk, pc], writes=[gg])
                        S.op("dve", ts(xc[:], xb[:, 0:512], col("cw", 0 * 4 + c), col("cb", c), ALU.mult, ALU.add), reads=[xb, pc], writes=[xc])
                        for j in range(1, 4):
                            S.op("dve", stt(xc[:], xb[:, j:j + 512], col("cw", j * 4 + c), xc[:], ALU.mult, ALU.add), reads=[xb, pc, xc], writes=[xc])
                        S.op("pool", copy(xcb[:], xc[:]), reads=[xc], writes=[xcb])
                        S.op("pe", mm(br.ap, wab[:, c, :], xcb[:], True, True), reads=[wab, xcb], writes=[br])
                        S.op("pe", mm(bi.ap, wxb[:, c, :], xcb[:], True, True), reads=[wxb, xcb], writes=[bi])
                        S.op("act", act(r[:], br.ap, AF.Sigmoid, bias=col("ba", c)), reads=[br, pc], writes=[r])
                        S.op("act", act(i[:], bi.ap, AF.Sigmoid, bias=col("bxg", c)), reads=[bi, pc], writes=[i])
                        S.op("act", act(a[:], r[:], AF.Exp, scale=cch[:, c:c + 1]), reads=[r, cch], writes=[a])
                        S.op("act", act(s[:], r[:], AF.Exp, scale=cch2[:, c:c + 1]), reads=[r, cch2], writes=[s])
                        S.op("dve", ts(s[:], s[:], -1.0, 1.0, ALU.mult, ALU.add), reads=[s], writes=[s])
                        S.op("act", act(s[:], s[:], AF.Sqrt), reads=[s], writes=[s])
                        S.op("pool", tt(u[:], i[:], xc[:], ALU.mult), reads=[i, xc], writes=[u])
                        S.op("dve", tt(u[:], u[:], s[:], ALU.mult), reads=[u, s], writes=[u])
                        init = 0.0 if tb == 0 else hp[:, 511:512]
                        S.op("dve", lambda e: e.tensor_tensor_scan(out=h[:], data0=a[:], data1=u[:], initial=init, op0=ALU.mult, op1=ALU.add),
                             reads=[a, u] + ([] if tb == 0 else [hp]), writes=[h])
                        S.op("dve", tt(y[:], h[:], gg[:], ALU.mult), reads=[h, gg], writes=[y])
                        S.op("act", act(ysq[:], y[:], AF.Square), reads=[y], writes=[ysq])
                        S.op("pool", copy(ylru[:, c, tsl], y[:]), reads=[y], writes=[ylru], join=True)
                        S.op("pe", mm(bx.ap, onesf[:], ysq[:], True, True), reads=[onesf, ysq], writes=[bx])
                        if c == 0:
                            S.op("dve", copy(ssq[:, tsl], bx.ap), reads=[bx], writes=[ssq], join=True)
                        else:
                            S.op("dve", tt(ssq[:, tsl], ssq[:, tsl], bx.ap, ALU.add), reads=[bx, ssq], writes=[ssq])
                        it += 1
                S.op("dve", ts(ssq[:], ssq[:], 1.0 / 512, EPS, ALU.mult, ALU.add), reads=[ssq], writes=[ssq])
                S.op("act", act(ssq[:], ssq[:], AF.Sqrt), reads=[ssq], writes=[ssq])
                S.op("dve", lambda e: e.reciprocal(out=ssq[:], in_=ssq[:]), reads=[ssq], writes=[ssq])
                for c in range(4):
                    o = PC_OFF["gl"] + c
                    S.op("dve", stt(ylru[:, c, :], ylru[:, c, :], pc[:, o:o + 1], ssq[:], ALU.mult, ALU.mult), reads=[ylru, pc, ssq], writes=[ylru])
                S.dma("sp", ylruT_d.rearrange("(c p) t -> p c t", p=128), ylru[:], reads=[ylru])
                S.barrier()

        def stage_nsa(l):
            with ExitStack() as stk:
                xT = sb(stk, "xT", [128, 8, S_LEN], BF16)
                S.dma("sp", xT[:], xT_d.rearrange("(k p) t -> p k t", p=128), writes=[xT])
                wn = sb(stk, "wn", [128, 8, 1304], BF16)
                for k in range(8):
                    ldcast(wn, wn[:, k, 0:1024], w_in[l, k * 128:(k + 1) * 128, 1024:2048])
                    ldcast(wn, wn[:, k, 1024:1304], w_in[l, k * 128:(k + 1) * 128, 2048:2328])
                pc = sb(stk, "pc", [128, NPC])
                S.dma("sp", pc[:], pc_d[l], writes=[pc])
                prn = sb(stk, "prn", [128, 280])
                S.dma("sp", prn[:], pr_d[l, :, 0:280], writes=[prn])
                bq8 = sb(stk, "bq8", [128, 8])
                S.op("dve", ts(bq8[:], pc[:, PC_OFF["bq"]:PC_OFF["bq"] + 8], 0.125, None, ALU.mult), reads=[pc], writes=[bq8])
                fb = sb(stk, "fb", [128, 32, 64])
                S.dma("sp", fb[:], c_fb, writes=[fb])
                expd = sb(stk, "expd", [64, 32, 128], BF16)
                for j in range(4):
                    ldcast(expd, expd[:, j * 8:(j + 1) * 8, :], c_expand[:, j * 8:(j + 1) * 8, :])
                w1k = sb(stk, "w1k", [64, 32, 64], BF16)
                w1v = sb(stk, "w1v", [64, 32, 64], BF16)
                for j in range(2):
                    ldcast(w1k, w1k[:, j * 16:(j + 1) * 16, :], w1k_d[l, :, j * 16:(j + 1) * 16, :])
                    ldcast(w1v, w1v[:, j * 16:(j + 1) * 16, :], w1v_d[l, :, j * 16:(j + 1) * 16, :])
                posk = sb(stk, "posk", [64, 34], BF16)
                posv = sb(stk, "posv", [64, 34], BF16)
                ldcast(posk, posk[:], posk_d[l])
                ldcast(posv, posv[:], posv_d[l])
                w2k = sb(stk, "w2k", [64, 64], BF16)
                w2va = sb(stk, "w2va", [65, 64], BF16)
                ldcast(w2k, w2k[:], w2k_d[l])
                ldcast(w2va, w2va[:], w2va_d[l])

                qTa = sb(stk, "qTa", [68, 4, S_LEN], BF16)
                ksa = sb(stk, "ksa", [68, S_LEN], BF16)
                kwa = sb(stk, "kwa", [68, S_LEN], BF16)
                kca = sb(stk, "kca", [68, 256], BF16)
                kcr = sb(stk, "kcr", [64, S_LEN + 32], BF16)
                vcr = sb(stk, "vcr", [64, S_LEN + 32], BF16)
                vsa = sb(stk, "vsa", [128, 32, 65], BF16)
                vwa = sb(stk, "vwa", [128, 32, 65], BF16)
                vca = sb(stk, "vca", [128, 2, 129], BF16)
                gsig = sb(stk, "gsig", [128, 32, 12])
                h1k = sb(stk, "h1k", [65, 256], BF16)
                h1v = sb(stk, "h1v", [65, 256], BF16)
                b1k = sb(stk, "b1k", [64, 1])
                b1v = sb(stk, "b1v", [64, 1])
                es_ = [sb(stk, "es%d" % i, [128, 512], BF16) for i in range(4)]
                mT_ = [sb(stk, "mT%d" % i, [64, 128], BF16) for i in range(2)]
                impa = sb(stk, "impa", [128, 64])
                impt = sb(stk, "impt", [128, 64])
                selm = sb(stk, "selm", [128, 64])
                m8 = sb(stk, "m8", [128, 16])
                rd = sb(stk, "rd", [128, 3, 4])
                coef = sb(stk, "coef", [128, 3, 4])
                ya_ = [sb(stk, "ya%d" % i, [128, 256]) for i in range(2)]
                ytmp = sb(stk, "ytmp", [128, 256])
                gtmp = sb(stk, "gtmp", [128, 12])

                for g in range(2):
                    for cb in range(4):
                        csl = slice(cb * 1024, (cb + 1) * 1024)
                        ldcast(qTa, qTa[64:68, :, csl], c_qpos[:, 4 * g:4 * g + 4, csl])
                        ldcast(ksa, ksa[64:68, csl], c_kpos[:, csl])
                        ldcast(kwa, kwa[64:68, csl], c_kpos[:, csl])
                    ldcast(kca, kca[64:68, :], c_cpos)
                    S.op("pool", memset(kcr[:, S_LEN:S_LEN + 32], 0.0), writes=[kcr], join=True)
                    S.op("pool", memset(vcr[:, S_LEN:S_LEN + 32], 0.0), writes=[vcr], join=True)
                    S.op("pool", memset(vsa[:], 1.0), writes=[vsa])
                    S.op("pool", memset(vwa[:], 1.0), writes=[vwa])
                    S.op("pool", memset(vca[:], 1.0), writes=[vca])
                    ldcast(vca, vca[:, :, 65:129], c_selmat)
                    S.op("pool", memset(h1k[64:65, :], 1.0), writes=[h1k], join=True)
                    S.op("pool", memset(h1v[64:65, :], 1.0), writes=[h1v], join=True)
                    projs = []
                    for r4 in range(4):
                        hh = 4 * g + r4
                        projs.append((hh * 64, qTa, lambda sl, r4=r4: qTa[0:64, r4, sl], bq8[0:64, hh:hh + 1], 0.125, bq8))
                    projs.append((512 + g * 64, kcr, lambda sl: kcr[0:64, sl], pc[0:64, PC_OFF["bkc"] + g:PC_OFF["bkc"] + g + 1], 1.0, pc))
                    projs.append((640 + g * 64, vcr, lambda sl: vcr[0:64, sl], pc[0:64, PC_OFF["bvc"] + g:PC_OFF["bvc"] + g + 1], 1.0, pc))
                    projs.append((768 + g * 64, ksa, lambda sl: ksa[0:64, sl], pc[0:64, PC_OFF["bks"] + g:PC_OFF["bks"] + g + 1], 1.0, pc))
                    projs.append((1024 + g * 64, kwa, lambda sl: kwa[0:64, sl], pc[0:64, PC_OFF["bkw"] + g:PC_OFF["bkw"] + g + 1], 1.0, pc))
                    it = 0
                    for (coff, dt_, dfn, bias_ap, scl, bias_t) in projs:
                        for tb in range(8):
                            bk = BANK[it % 4]
                            it += 1
                            tsl = slice(tb * 512, (tb + 1) * 512)
                            for k in range(8):
                                S.op("pe", mm(bk.ap[0:64, :], wn[:, k, coff:coff + 64], xT[:, k, tsl], k == 0, k == 7), reads=[wn, xT], writes=[bk])
                            S.op("act", act(dfn(tsl), bk.ap[0:64, :], AF.Identity, bias=bias_ap, scale=scl), reads=[bk, bias_t], writes=[dt_], join=True)
                    for t_ in range(32):
                        tsl = slice(t_ * 128, (t_ + 1) * 128)
                        b1_, b2_, b3_ = BANK[4], BANK[5], BANK[6]
                        for k in range(8):
                            S.op("pe", mm(b1_.ap[:, 0:64], xT[:, k, tsl], wn[:, k, 896 + g * 64:896 + g * 64 + 64], k == 0, k == 7), reads=[wn, xT], writes=[b1_])
                        for k in range(8):
                            S.op("pe", mm(b2_.ap[:, 0:64], xT[:, k, tsl], wn[:, k, 1152 + g * 64:1152 + g * 64 + 64], k == 0, k == 7), reads=[wn, xT], writes=[b2_])
                        for k in range(8):
                            S.op("pe", mm(b3_.ap[:, 0:12], xT[:, k, tsl], wn[:, k, 1280 + g * 12:1280 + g * 12 + 12], k == 0, k == 7), reads=[wn, xT], writes=[b3_])
                        S.op("dve", tt(vsa[:, t_, 0:64], b1_.ap[:, 0:64], prn[:, g * 64:g * 64 + 64], ALU.add), reads=[b1_, prn], writes=[vsa], join=True)
                        S.op("dve", tt(vwa[:, t_, 0:64], b2_.ap[:, 0:64], prn[:, 128 + g * 64:128 + g * 64 + 64], ALU.add), reads=[b2_, prn], writes=[vwa], join=True)
                        S.op("dve", tt(gtmp[:], b3_.ap[:, 0:12], prn[:, 256 + g * 12:256 + g * 12 + 12], ALU.add), reads=[b3_, prn], writes=[gtmp])
                        S.op("act", act(gsig[:, t_, :], gtmp[:], AF.Sigmoid), reads=[gtmp], writes=[gsig], join=True)
                    for (raw, w1, pos, b1t, b1name, h1) in ((kcr, w1k, posk, b1k, "kb1", h1k), (vcr, w1v, posv, b1v, "vb1", h1v)):
                        bA, bB = BANK[0], BANK[1]
                        for l_ in range(32):
                            S.op("pe", mm(bA.ap[0:64, 0:256], w1[:, l_, :], raw[:, l_:l_ + 4096:16], l_ == 0, l_ == 31), reads=[w1, raw], writes=[bA])
                        for l_ in range(32):
                            S.op("pe", mm(bB.ap[0:64, 0:2], w1[:, l_, :], pos[:, l_:l_ + 2], l_ == 0, l_ == 31), reads=[w1, pos], writes=[bB])
                        o = PC_OFF[b1name]
                        S.op("dve", tt(b1t[:], bB.ap[0:64, 0:1], pc[0:64, o:o + 1], ALU.add), reads=[bB, pc], writes=[b1t])
                        S.op("act", act(h1[0:64, :], bA.ap[0:64, 0:256], AF.Gelu_apprx_tanh, bias=b1t[:]), reads=[bA, b1t], writes=[h1], join=True)
                    bA = BANK[2]
                    S.op("pe", mm(bA.ap[0:64, 0:256], w2k[:], h1k[0:64, :], True, True), reads=[w2k, h1k], writes=[bA])
                    o = PC_OFF["kb2"]
                    S.op("act", act(kca[0:64, :], bA.ap[0:64, 0:256], AF.Identity, bias=pc[0:64, o:o + 1]), reads=[bA, pc], writes=[kca], join=True)
                    for ch in range(2):
                        bB = BANK[3]
                        S.op("pe", mm(bB.ap[:, 0:64], h1v[0:65, ch * 128:(ch + 1) * 128], w2va[:], True, True), reads=[h1v, w2va], writes=[bB])
                        S.op("dve", copy(vca[:, ch, 0:64], bB.ap[:, 0:64]), reads=[bB], writes=[vca], join=True)

                    bOc, bI, bOs, bOw = BANK[2], BANK[3], BANK[5], BANK[6]
                    sbank = [BANK[0], BANK[1]]
                    mbank = [BANK[7], BANK[4]]
                    ctr = {"s": 0, "e": 0, "m": 0}

                    def mk_step(kind, qi, idx):
                        qsl = slice(qi * 128, (qi + 1) * 128)
                        rhsq = qTa[:, :, qsl]
                        nch = 1 if qi < 16 else 2
                        k0 = max(0, qi - 4)
                        st = {}

                        def s1():
                            bs = sbank[ctr["s"] % 2]
                            ctr["s"] += 1
                            es = es_[ctr["e"] % 4]
                            ctr["e"] += 1
                            st["es"] = es
                            esv = es[:].rearrange("p (r q) -> p r q", r=4)
                            if kind == "c":
                                ch = idx
                                S.op("pe", mm(bs.ap, kca[:, ch * 128:(ch + 1) * 128], rhsq, True, True), reads=[kca, qTa], writes=[bs])
                                S.op("act", act(es[:], bs.ap, AF.Exp), reads=[bs], writes=[es])
                                S.op("pool", lambda e: e.affine_select(out=esv, in_=esv, pattern=[[0, 4], [1, 128]], compare_op=ALU.is_ge, fill=0.0,
                                                                       base=128 * qi - 2048 * ch - 31, channel_multiplier=-16), reads=[es], writes=[es])
                            elif kind == "w":
                                kj = idx
                                S.op("pe", mm(bs.ap, kwa[:, kj * 128:(kj + 1) * 128], rhsq, True, True), reads=[kwa, qTa], writes=[bs])
                                S.op("act", act(es[:], bs.ap, AF.Exp), reads=[bs], writes=[es])
                                if kj == qi:
                                    S.op("pool", lambda e: e.affine_select(out=esv, in_=esv, pattern=[[0, 4], [1, 128]], compare_op=ALU.is_ge, fill=0.0,
                                                                           base=0, channel_multiplier=-1), reads=[es], writes=[es])
                                if kj == qi - 4:
                                    S.op("pool", lambda e: e.affine_select(out=esv, in_=esv, pattern=[[0, 4], [-1, 128]], compare_op=ALU.is_ge, fill=0.0,
                                                                           base=-1, channel_multiplier=1), reads=[es], writes=[es])
                            else:
                                kj = idx
                                mT = mT_[qi % 2]
                                bm = mbank[ctr["m"] % 2]
                                ctr["m"] += 1
                                S.op("pe", mm(bs.ap, ksa[:, kj * 128:(kj + 1) * 128], rhsq, True, True), reads=[ksa, qTa], writes=[bs])
                                S.op("act", act(es[:], bs.ap, AF.Exp), reads=[bs], writes=[es])
                                S.op("pe", mm(bm.ap[:, 0:128], expd[:, kj, :], mT[:], True, True), reads=[expd, mT], writes=[bm])
                                S.op("dve", tt(esv, esv, bm.ap[:, 0:128].unsqueeze(1).to_broadcast([128, 4, 128]), ALU.mult), reads=[es, bm], writes=[es])
                                if kj == qi:
                                    S.op("pool", lambda e: e.affine_select(out=esv, in_=esv, pattern=[[0, 4], [1, 128]], compare_op=ALU.is_ge, fill=0.0,
                                                                           base=0, channel_multiplier=-1), reads=[es], writes=[es])

                        def s2():
                            es = st["es"]
                            if kind == "c":
                                ch = idx
                                for r4 in range(4):
                                    S.op("pe", mm(bOc.ap[:, r4 * 65:(r4 + 1) * 65], es[:, r4 * 128:(r4 + 1) * 128], vca[:, ch, 0:65], ch == 0 and r4 == 0, ch == nch - 1 and r4 == 3),
                                         reads=[es, vca], writes=[bOc])
                                for r4 in range(4):
                                    S.op("pe", mm(bI.ap[:, r4 * 64:(r4 + 1) * 64], es[:, r4 * 128:(r4 + 1) * 128], vca[:, ch, 65:129], ch == 0 and r4 == 0, ch == nch - 1 and r4 == 3),
                                         reads=[es, vca], writes=[bI])
                                if ch == nch - 1:
                                    chain(qi)
                            elif kind == "w":
                                kj = idx
                                for r4 in range(4):
                                    S.op("pe", mm(bOw.ap[:, r4 * 65:(r4 + 1) * 65], es[:, r4 * 128:(r4 + 1) * 128], vwa[:, kj, :], kj == k0 and r4 == 0, kj == qi and r4 == 3),
                                         reads=[es, vwa], writes=[bOw])
                            else:
                                kj = idx
                                for r4 in range(4):
                                    S.op("pe", mm(bOs.ap[:, r4 * 65:(r4 + 1) * 65], es[:, r4 * 128:(r4 + 1) * 128], vsa[:, kj, :], kj == 0 and r4 == 0, kj == qi and r4 == 3),
                                         reads=[es, vsa], writes=[bOs])
                                if kj == qi:
                                    combine(qi)
                        return s1, s2

                    def chain(qi):
                        mT = mT_[qi % 2]
                        ocv = bOc.ap[:, 0:260].rearrange("p (r d) -> p r d", r=4)
                        S.op("dve", ts(rd[:, 0, :], ocv[:, :, 64], 1e-30, None, ALU.add), reads=[bOc], writes=[rd])
                        S.op("dve", lambda e: e.reciprocal(out=rd[:, 0, :], in_=rd[:, 0, :]), reads=[rd], writes=[rd])
                        S.op("dve", ts(impa[:], bI.ap[:, 0:64], rd[:, 0, 0:1], None, ALU.mult), reads=[bI, rd], writes=[impa])
                        for r4 in range(1, 4):
                            S.op("dve", stt(impa[:], bI.ap[:, r4 * 64:(r4 + 1) * 64], rd[:, 0, r4:r4 + 1], impa[:], ALU.mult, ALU.add), reads=[bI, rd, impa], writes=[impa])
                        S.op("dve", tt(impa[:], impa[:], fb[:, qi, :], ALU.add), reads=[impa, fb], writes=[impa])
                        S.op("dve", lambda e: e.max(out=m8[:, 0:8], in_=impa[:]), reads=[impa], writes=[m8])
                        S.op("dve", lambda e: e.match_replace(out=impt[:], in_to_replace=m8[:, 0:8], in_values=impa[:], imm_value=-3e38), reads=[impa, m8], writes=[impt])
                        S.op("dve", lambda e: e.max(out=m8[:, 8:16], in_=impt[:]), reads=[impt], writes=[m8])
                        S.op("dve", ts(selm[:], impa[:], m8[:, 15:16], None, ALU.is_ge), reads=[impa, m8], writes=[selm])
                        S.op("pe", lambda e: e.transpose(out=bI.ap[0:64, 0:128], in_=selm[:], identity=identf[:]), reads=[selm, identf], writes=[bI])
                        S.op("act", act(mT[:], bI.ap[0:64, 0:128], AF.Copy), reads=[bI], writes=[mT])

                    def combine(qi):
                        qsl = slice(qi * 128, (qi + 1) * 128)
                        ya = ya_[qi % 2]
                        ocv = bOc.ap[:, 0:260].rearrange("p (r d) -> p r d", r=4)
                        osv = bOs.ap[:, 0:260].rearrange("p (r d) -> p r d", r=4)
                        owv = bOw.ap[:, 0:260].rearrange("p (r d) -> p r d", r=4)
                        S.op("dve", ts(rd[:, 1, :], osv[:, :, 64], 1e-30, None, ALU.add), reads=[bOs], writes=[rd], join=True)
                        S.op("dve", ts(rd[:, 2, :], owv[:, :, 64], 1e-30, None, ALU.add), reads=[bOw], writes=[rd], join=True)
                        S.op("dve", lambda e: e.reciprocal(out=rd[:, 1:3, :], in_=rd[:, 1:3, :]), reads=[rd], writes=[rd])
                        S.op("dve", tt(coef[:], rd[:], gsig[:, qi, :].rearrange("p (r b) -> p b r", b=3), ALU.mult), reads=[rd, gsig], writes=[coef])
                        yav = ya[:].rearrange("p (r d) -> p r d", r=4)
                        ytv = ytmp[:].rearrange("p (r d) -> p r d", r=4)
                        S.op("dve", tt(yav, ocv[:, :, 0:64], coef[:, 0, :].unsqueeze(2).to_broadcast([128, 4, 64]), ALU.mult), reads=[bOc, coef], writes=[ya])
                        S.op("dve", tt(ytv, osv[:, :, 0:64], coef[:, 1, :].unsqueeze(2).to_broadcast([128, 4, 64]), ALU.mult), reads=[bOs, coef], writes=[ytmp])
                        S.op("dve", tt(ya[:], ya[:], ytmp[:], ALU.add), reads=[ya, ytmp], writes=[ya])
                        S.op("dve", tt(ytv, owv[:, :, 0:64], coef[:, 2, :].unsqueeze(2).to_broadcast([128, 4, 64]), ALU.mult), reads=[bOw, coef], writes=[ytmp])
                        S.op("dve", tt(ya[:], ya[:], ytmp[:], ALU.add), reads=[ya, ytmp], writes=[ya])
                        S.dma("sp", ynsa_d[qsl, g * 256:(g + 1) * 256], ya[:], reads=[ya])

                    steps = []
                    pump(64)
                    for qi in range(32):
                        nch = 1 if qi < 16 else 2
                        for ch in range(nch):
                            steps.append(mk_step("c", qi, ch))
                        for kj in range(max(0, qi - 4), qi + 1):
                            steps.append(mk_step("w", qi, kj))
                        for kj in range(qi + 1):
                            steps.append(mk_step("s", qi, kj))
                    steps[0][0]()
                    for i in range(len(steps)):
                        if i + 1 < len(steps):
                            steps[i + 1][0]()
                        steps[i][1]()
                S.barrier()

        def layer_norm(stk_tiles, t, gb, goff, boff, out):
            stats, mv, rs = stk_tiles
            S.op("dve", lambda e: e.bn_stats(out=stats[:, 0, :], in_=t[:, 0:512]), reads=[t], writes=[stats])
            S.op("dve", lambda e: e.bn_stats(out=stats[:, 1, :], in_=t[:, 512:1024]), reads=[t], writes=[stats], join=True)
            S.op("dve", lambda e: e.bn_aggr(out=mv[:], in_=stats[:].rearrange("p a b -> p (a b)")), reads=[stats], writes=[mv])
            S.op("dve", ts(rs[:], mv[:, 1:2], EPS, None, ALU.add), reads=[mv], writes=[rs])
            S.op("act", act(rs[:], rs[:], AF.Sqrt), reads=[rs], writes=[rs])
            S.op("dve", lambda e: e.reciprocal(out=rs[:], in_=rs[:]), reads=[rs], writes=[rs])
            S.op("dve", ts(out[:], t[:], mv[:, 0:1], rs[:], ALU.subtract, ALU.mult), reads=[t, mv, rs], writes=[out])
            S.op("pool", tt(out[:], out[:], gb[:, goff:goff + 1024], ALU.mult), reads=[out, gb], writes=[out])
            S.op("pool", tt(out[:], out[:], gb[:, boff:boff + 1024], ALU.add), reads=[out, gb], writes=[out])

        def stage_out(l, resid):
            with ExitStack() as stk:
                wo = sb(stk, "wo", [128, 8, 1024], BF16)
                for k in range(8):
                    ldcast(wo, wo[:, k, :], wout_d[l, k * 128:(k + 1) * 128, :])
                ylru = sb(stk, "ylru", [128, 4, S_LEN], BF16)
                S.dma("sp", ylru[:], ylruT_d.rearrange("(c p) t -> p c t", p=128), writes=[ylru])
                gb = sb(stk, "gb", [128, 2560])
                S.dma("sp", gb[:], pr_d[l, :, 280:2840], writes=[gb])
                yn_ = [sb(stk, "yn%d" % i, [128, 512]) for i in range(2)]
                ynn_ = [sb(stk, "ynn%d" % i, [128, 512]) for i in range(2)]
                nsaT_ = [sb(stk, "nsaT%d" % i, [128, 4, 128], BF16) for i in range(2)]
                xr_ = [sb(stk, "xr%d" % i, [128, 1024]) for i in range(2)]
                t_ = [sb(stk, "t%d" % i, [128, 1024]) for i in range(2)]
                x1_ = [sb(stk, "x1%d" % i, [128, 1024]) for i in range(2)]
                x1T_ = [sb(stk, "x1T%d" % i, [128, 8, 128], BF16) for i in range(2)]
                junk = sb(stk, "junk", [128, 512])
                ss_ = [sb(stk, "ss%d" % i, [128, 1]) for i in range(2)]
                stats = sb(stk, "stats", [128, 2, 6])
                mv = sb(stk, "mv", [128, 2])
                rs = sb(stk, "rs", [128, 1])
                x1Tv = x1T_d.rearrange("(k p) t -> p k t", p=128)
                for tt_ in range(32):
                    p = tt_ % 2
                    yn, ynn, nsaT, xr, t, x1, x1T, ss = yn_[p], ynn_[p], nsaT_[p], xr_[p], t_[p], x1_[p], x1T_[p], ss_[p]
                    tsl = slice(tt_ * 128, (tt_ + 1) * 128)
                    S.dma("sp", yn[:], ynsa_d[tsl, :], writes=[yn])
                    S.dma("sp", xr[:], resid[tsl, :], writes=[xr])
                    S.op("act", act(junk[:], yn[:], AF.Square, accum_out=ss[:]), reads=[yn], writes=[junk, ss])
                    S.op("dve", ts(ss[:], ss[:], 1.0 / 512, EPS, ALU.mult, ALU.add), reads=[ss], writes=[ss])
                    S.op("act", act(ss[:], ss[:], AF.Sqrt), reads=[ss], writes=[ss])
                    S.op("dve", lambda e: e.reciprocal(out=ss[:], in_=ss[:]), reads=[ss], writes=[ss])
                    S.op("dve", stt(ynn[:], yn[:], ss[:], gb[:, 0:512], ALU.mult, ALU.mult), reads=[yn, ss, gb], writes=[ynn])
                    bT = BANK[4 + p]
                    for k in range(4):
                        S.op("pe", lambda e: e.transpose(out=bT.ap[:, k * 128:(k + 1) * 128], in_=ynn[:, k * 128:(k + 1) * 128], identity=identf[:]),
                             reads=[ynn, identf], writes=[bT])
                    S.op("act", act(nsaT[:], bT.ap.rearrange("p (k t) -> p k t", k=4), AF.Copy), reads=[bT], writes=[nsaT])
                    for hf in range(2):
                        bk = BANK[hf + 2 * p]
                        hs = slice(hf * 512, (hf + 1) * 512)
                        for k in range(4):
                            S.op("pe", mm(bk.ap, ylru[:, k, tsl], wo[:, k, hs], k == 0, False), reads=[ylru, wo], writes=[bk])
                        for k in range(4):
                            S.op("pe", mm(bk.ap, nsaT[:, k, :], wo[:, 4 + k, hs], False, k == 3), reads=[nsaT, wo], writes=[bk])
                        S.op("dve", stt(t[:, hs], xr[:, hs], ALPHA, bk.ap, ALU.mult, ALU.add), reads=[xr, bk], writes=[t], join=True)
                    layer_norm((stats, mv, rs), t, gb, 512, 1536, x1)
                    S.dma("sp", x1_d[tsl, :], x1[:], reads=[x1])
                    for hb in range(2):
                        bk = BANK[6 + hb]
                        for kk in range(4):
                            k = hb * 4 + kk
                            S.op("pe", lambda e: e.transpose(out=bk.ap[:, kk * 128:(kk + 1) * 128], in_=x1[:, k * 128:(k + 1) * 128], identity=identf[:]),
                                 reads=[x1, identf], writes=[bk])
                        S.op("act", act(x1T[:, hb * 4:(hb + 1) * 4, :], bk.ap.rearrange("p (k t) -> p k t", k=4), AF.Copy), reads=[bk], writes=[x1T], join=True)
                    S.dma("sp", x1Tv[:, :, tsl], x1T[:], reads=[x1T])
                S.barrier()

        conv_q = []

        def conv_fill():
            for l in range(nlayers):
                for kp in range(8):
                    for cb in range(16):
                        conv_q.append((uTb_l[l][kp * 128:(kp + 1) * 128, cb * 1024:(cb + 1) * 1024], uT_d[l, kp * 128:(kp + 1) * 128, cb * 1024:(cb + 1) * 1024]))
                for rb in range(128):
                    conv_q.append((vb_l[l][rb * 128:(rb + 1) * 128, :], v_d[l, rb * 128:(rb + 1) * 128, :]))

        def pump(n):
            for _ in range(n):
                if not conv_q:
                    return
                dst, src = conv_q.pop(0)
                S.dma("pool", dst, src)

        def stage_peer(l, dst):
            with ExitStack() as stk:
                wq = sb(stk, "wq", [128, 8, 2048], BF16)
                for k in range(8):
                    ldcast(wq, wq[:, k, 0:1024], wq_d[l, k * 128:(k + 1) * 128, 0:1024])
                    ldcast(wq, wq[:, k, 1024:2048], wq_d[l, k * 128:(k + 1) * 128, 1024:2048])
                skT = sb(stk, "skT", [128, 2, 128], BF16)
                ldcast(skT, skT[:], skT_d[l])
                gb = sb(stk, "gb", [128, 2048])
                S.dma("sp", gb[:], pr_d[l, :, 2840:4888], writes=[gb])
                x1T_ = [sb(stk, "x1T%d" % i, [128, 8, 256], BF16) for i in range(2)]
                qT = sb(stk, "qT", [128, 16, 256], BF16)
                sab_ = [[sb(stk, "sab%d_%d" % (j, i), [128, 8, 2, 128]) for i in range(2)] for j in range(2)]
                thr_ = [[sb(stk, "thr%d_%d" % (j, i), [128, 8]) for i in range(2)] for j in range(2)]
                bia_ = [[sb(stk, "bia%d_%d" % (j, i), [128, 8]) for i in range(2)] for j in range(2)]
                sv = sb(stk, "sv", [128, 2, 16])
                tmpk = sb(stk, "tmpk", [128, 128])
                cand = sb(stk, "cand", [128, 256])
                ctmp = sb(stk, "ctmp", [128, 256])
                cv = sb(stk, "cv", [128, 16])
                cex = sb(stk, "cex", [128, 16])
                negm = sb(stk, "negm", [128, 1])
                zz = sb(stk, "zz", [128, 1])
                uT_ = [sb(stk, "uTg%d" % i, [128, 8, 512], BF16) for i in range(2)]
                vg_ = [sb(stk, "vg%d" % i, [128, 4, 1024], BF16) for i in range(2)]
                abf_ = [sb(stk, "abf%d" % i, [128, 1024], BF16) for i in range(2)]
                hid_ = [sb(stk, "hid%d" % i, [128, 4, 256], BF16) for i in range(2)]
                NR = 48
                ee_ = [sb(stk, "EE%d" % i, [128, 128], BF16) for i in range(NR)]
                gh_ = [sb(stk, "GH%d" % i, [128, 128], BF16) for i in range(NR)]
                sabi_ = [[sb(stk, "sabi%d_%d" % (j, i), [128, 8, 128]) for i in range(2)] for j in range(2)]
                rneg_ = [[sb(stk, "rneg%d_%d" % (j, i), [128, 8, 128]) for i in range(2)] for j in range(2)]
                thrm = sb(stk, "thrm", [128, 1])
                negone = sb(stk, "negone", [128, 1])
                S.op("dve", memset(negone[:], -1.0), writes=[negone])
                xr_ = [sb(stk, "xr%d" % i, [128, 1024]) for i in range(1)] * 2
                t_ = [sb(stk, "t%d" % i, [128, 1024]) for i in range(1)] * 2
                xo_ = [sb(stk, "xo%d" % i, [128, 1024]) for i in range(1)] * 2
                stats = sb(stk, "stats", [128, 2, 6])
                mv = sb(stk, "mv", [128, 2])
                rs = sb(stk, "rs", [128, 1])
                x1Tv = x1T_d.rearrange("(k p) t -> p k t", p=128)
                uTb_d, vb_d = uTb_l[l], vb_l[l]
                uTv = uTb_d.rearrange("(k p) e -> p k e", p=128)
                bA = [BANK[0], BANK[1]]
                bG = [BANK[2], BANK[3]]
                bY = [[BANK[4], BANK[5]], [BANK[6], BANK[7]]]
                psA = P2[0]
                psG = P2[1]
                cnt = {"g": 0, "w": 0, "gh": 0}

                def emit_ab(st_):
                    x1T = x1T_[st_ % 2]
                    S.dma("sp", x1T[:], x1Tv[:, :, st_ * 256:(st_ + 1) * 256], writes=[x1T])
                    for hc in range(16):
                        bk = BANK[hc % 2]
                        for k in range(8):
                            S.op("pe", mm(bk.ap[:, 0:256], wq[:, k, hc * 128:(hc + 1) * 128], x1T[:, k, :], k == 0, k == 7), reads=[wq, x1T], writes=[bk])
                        S.op("act", act(qT[:, hc, :], bk.ap[:, 0:256], AF.Copy), reads=[bk], writes=[qT], join=True)
                    for t2 in range(2):
                        sab = sab_[st_ % 2][t2]
                        for h in range(8):
                            bk = BANK[h % 2]
                            for c in range(2):
                                S.op("pe", mm(bk.ap[:, c * 128:(c + 1) * 128], qT[:, 2 * h + c, t2 * 128:(t2 + 1) * 128], skT[:, c, :], c == 0, c == 1), reads=[qT, skT], writes=[bk])
                            S.op("act", act(sab[:, h, :, :], bk.ap[:, 0:256].rearrange("p (c k) -> p c k", c=2), AF.Copy), reads=[bk], writes=[sab], join=True)
                    for t2 in range(2):
                        sab, thr, bia = sab_[st_ % 2][t2], thr_[st_ % 2][t2], bia_[st_ % 2][t2]
                        for h in range(8):
                            for c in range(2):
                                S.op("dve", lambda e: e.max(out=sv[:, c, 0:8], in_=sab[:, h, c, :]), reads=[sab], writes=[sv], join=True)
                                S.op("dve", lambda e: e.match_replace(out=tmpk[:], in_to_replace=sv[:, c, 0:8], in_values=sab[:, h, c, :], imm_value=-3e38), reads=[sab, sv], writes=[tmpk])
                                S.op("dve", lambda e: e.max(out=sv[:, c, 8:16], in_=tmpk[:]), reads=[tmpk], writes=[sv], join=True)
                            S.op("dve", tt(cand[:].rearrange("p (i j) -> p i j", i=16), sv[:, 0, :].unsqueeze(2).to_broadcast([128, 16, 16]),
                                           sv[:, 1, :].unsqueeze(1).to_broadcast([128, 16, 16]), ALU.add), reads=[sv], writes=[cand])
                            S.op("dve", lambda e: e.max(out=cv[:, 0:8], in_=cand[:]), reads=[cand], writes=[cv])
                            S.op("dve", lambda e: e.match_replace(out=ctmp[:], in_to_replace=cv[:, 0:8], in_values=cand[:], imm_value=-3e38), reads=[cand, cv], writes=[ctmp])
                            S.op("dve", lambda e: e.max(out=cv[:, 8:16], in_=ctmp[:]), reads=[ctmp], writes=[cv], join=True)
                            S.op("dve", ts(negm[:], cv[:, 0:1], -1.0, None, ALU.mult), reads=[cv], writes=[negm])
                            S.op("act", act(cex[:], cv[:], AF.Exp, bias=negm[:], accum_out=zz[:]), reads=[cv, negm], writes=[cex, zz])
                            S.op("act", act(zz[:], zz[:], AF.Ln), reads=[zz], writes=[zz])
                            S.op("dve", tt(bia[:, h:h + 1], negm[:], zz[:], ALU.subtract), reads=[negm, zz], writes=[bia], join=True)
                            S.op("dve", ts(thrm[:], cv[:, 15:16], -4e-6, None, ALU.add), reads=[cv], writes=[thrm])
                            sabi, rneg = sabi_[st_ % 2][t2], rneg_[st_ % 2][t2]
                            S.op("dve", ts(sabi[:, h, :], sab[:, h, 0, :], bia[:, h:h + 1], None, ALU.add), reads=[sab, bia], writes=[sabi], join=True)
                            S.op("dve", ts(rneg[:, h, :], sab[:, h, 0, :], negone[:], thrm[:], ALU.mult, ALU.add), reads=[sab, thrm, negone], writes=[rneg], join=True)

                def emit_A(st_, ag):
                    x1T = x1T_[st_ % 2]
                    n = st_ * 32 + ag
                    uTg = uT_[n % 2]
                    S.dma("sp", uTg[:], uTv[:, :, ag * 512:(ag + 1) * 512], writes=[uTg])
                    for a4 in range(4):
                        bk = bA[a4 // 2]
                        for k in range(8):
                            S.op("pe", mm(psA[:, a4 * 256:(a4 + 1) * 256], uTg[:, k, a4 * 128:(a4 + 1) * 128], x1T[:, k, :], k == 0, k == 7), reads=[uTg, x1T], writes=[bk])

                def emit_gelu(st_, ag):
                    n = st_ * 32 + ag
                    abf = abf_[n % 2]
                    for hb in range(2):
                        S.op("act", act(abf[:, hb * 512:(hb + 1) * 512], bA[hb].ap, AF.Gelu_apprx_tanh), reads=[bA[hb]], writes=[abf], join=True)

                def emit_G(st_, ag, t2, first):
                    sab = sab_[st_ % 2][t2]
                    sabi, rneg = sabi_[st_ % 2][t2], rneg_[st_ % 2][t2]
                    for h in range(8):
                        for a4 in range(4):
                            g = cnt["g"]
                            cnt["g"] += 1
                            EE, GH = ee_[g % NR], gh_[g % NR]
                            a = ag * 4 + a4
                            S.op("act", act(EE[:], sab[:, h, 1, :], AF.Exp, bias=sabi[:, h, a:a + 1]), reads=[sab, sabi], writes=[EE])
                            S.op("dve", stt(GH[:], sab[:, h, 1, :], rneg[:, h, a:a + 1], EE[:], ALU.is_ge, ALU.mult), reads=[sab, rneg, EE], writes=[GH])
                            bk = bG[a4 // 2]
                            S.op("pe", mm(psG[:, a4 * 256 + t2 * 128:a4 * 256 + (t2 + 1) * 128], GH[:], identb[:], first[a4 // 2], (t2 == 1 and h == 7 and a4 % 2 == 1)),
                                 reads=[GH, identb], writes=[bk])
                            first[a4 // 2] = False

                def emit_HV(st_, ag):
                    n = st_ * 32 + ag
                    abf, hid, vg = abf_[n % 2], hid_[n % 2], vg_[n % 2]
                    for hb in range(2):
                        S.op("dve", tt(hid[:, 2 * hb:2 * hb + 2, :].rearrange("p a t -> p (a t)"), abf[:, hb * 512:(hb + 1) * 512], bG[hb].ap, ALU.mult),
                             reads=[abf, bG[hb]], writes=[hid], join=True)
                    for t2 in range(2):
                        for hf in range(2):
                            bk = bY[t2][hf]
                            for a4 in range(4):
                                S.op("pe", mm(bk.ap, hid[:, a4, t2 * 128:(t2 + 1) * 128], vg[:, a4, hf * 512:(hf + 1) * 512], ag == 0 and a4 == 0, ag == 31 and a4 == 3),
                                     reads=[hid, vg], writes=[bk])

                def emit_epi(st_):
                    for t2 in range(2):
                        tok = st_ * 256 + t2 * 128
                        xr, t, xo = xr_[t2], t_[t2], xo_[t2]
                        S.dma("sp", xr[:], x1_d[tok:tok + 128, :], writes=[xr])
                        for hf in range(2):
                            hs = slice(hf * 512, (hf + 1) * 512)
                            S.op("dve", stt(t[:, hs], xr[:, hs], ALPHA, bY[t2][hf].ap, ALU.mult, ALU.add), reads=[xr, bY[t2][hf]], writes=[t], join=True)
                        layer_norm((stats, mv, rs), t, gb, 0, 1024, xo)
                        S.dma("sp", dst[tok:tok + 128, :], xo[:], reads=[xo])

                emit_ab(0)
                emit_A(0, 0)
                emit_gelu(0, 0)
                for n in range(512):
                    st_, ag = divmod(n, 32)
                    if ag == 12 and st_ + 1 < 16:
                        emit_ab(st_ + 1)
                    vg = vg_[n % 2]
                    S.dma("sp", vg[:], vb_d[ag * 512:(ag + 1) * 512, :].rearrange("(a p) d -> p a d", p=128), writes=[vg])
                    first = [True, True]
                    emit_G(st_, ag, 0, first)
                    if n + 1 < 512:
                        emit_A(*divmod(n + 1, 32))
                    emit_G(st_, ag, 1, first)
                    emit_HV(st_, ag)
                    if n + 1 < 512:
                        emit_gelu(*divmod(n + 1, 32))
                    if ag == 31:
                        emit_epi(st_)
                S.barrier()

        resid = x_in
        if "peer" in stages:
            conv_fill()
        for l in range(nlayers):
            last = (l == NL - 1)
            if "xT" in stages:
                stage_xT(resid, xT_d)
            if "lru" in stages:
                stage_lru(l)
            if "nsa" in stages:
                stage_nsa(l)
            if "out" in stages:
                stage_out(l, resid)
            if "peer" in stages:
                pump((2 - l) * 256 if l == 0 else 10 ** 6)
                stage_peer(l, y_out if last else resid_d)
            resid = resid_d
        S.barrier()
    return nc


def _consts():
    c = {}
    c["c_ident"] = np.eye(128, dtype=np.float32)
    t = np.arange(S_LEN)
    a_t = (t // 16).astype(np.float32)
    b_t = (t % 16).astype(np.float32)
    qpos = np.zeros((4, 8, S_LEN), np.float32)
    for h in range(8):
        sl = 2.0 ** (-(h + 1))
        qpos[0, h] = 16.0 * sl
        qpos[1, h] = sl
        qpos[2, h] = -sl * 16.0 * a_t
        qpos[3, h] = -sl * b_t
    c["c_qpos"] = qpos
    kpos = np.zeros((4, S_LEN), np.float32)
    kpos[0] = a_t
    kpos[1] = b_t
    kpos[2] = 1.0
    kpos[3] = 1.0
    c["c_kpos"] = kpos
    cc = np.arange(256)
    cpos = np.zeros((4, 256), np.float32)
    cpos[0] = cc + 1
    cpos[1] = 15.0
    cpos[2] = 1.0
    cpos[3] = 1.0
    c["c_cpos"] = cpos
    cs = np.arange(255)[:, None] * 16
    ss = np.arange(64)[None, :] * 64
    ov = np.clip(np.minimum(cs + 32, ss + 64) - np.maximum(cs, ss), 0, None)
    sm = np.zeros((256, 64), np.float32)
    sm[:255] = ov / 16.0
    c["c_selmat"] = np.ascontiguousarray(sm.reshape(2, 128, 64).transpose(1, 0, 2))
    fb = np.zeros((128, 32, 64), np.float32)
    for qi in range(32):
        tq = qi * 128 + np.arange(128)
        cur = tq // 64
        j = np.arange(64)[None, :]
        vblk = j <= cur[:, None]
        forced = (j == 0) | (j == cur[:, None]) | (j == cur[:, None] - 1)
        fb[:, qi, :] = np.where(vblk, np.where(forced, 1e4, 0.0), NEG)
    c["c_fb"] = fb
    ex = np.zeros((64, 32, 128), np.float32)
    for kj in range(32):
        ex[2 * kj, kj, 0:64] = 1.0
        ex[2 * kj + 1, kj, 64:128] = 1.0
    c["c_expand"] = ex
    return c


def _pack(inp):
    L = NL
    f = lambda k: np.asarray(inp[k], dtype=np.float32)
    b_in = f("b_in")
    pc = np.zeros((L, 128, NPC), np.float32)
    pr = np.zeros((L, 128, NPR), np.float32)

    def colpack(vec512):
        return vec512.reshape(4, 128).T
    for l in range(L):
        o = PC_OFF
        pc[l, :, o["bx"]:o["bx"] + 4] = colpack(b_in[l, 0:512])
        pc[l, :, o["bg"]:o["bg"] + 4] = colpack(b_in[l, 512:1024])
        cw = f("conv_w")[l]
        for j in range(4):
            pc[l, :, o["cw"] + j * 4:o["cw"] + j * 4 + 4] = colpack(cw[j])
        pc[l, :, o["cb"]:o["cb"] + 4] = colpack(f("conv_b")[l])
        pc[l, :, o["ba"]:o["ba"] + 4] = colpack(f("lru_ba")[l])
        pc[l, :, o["bxg"]:o["bxg"] + 4] = colpack(f("lru_bx")[l])
        pc[l, :, o["lam"]:o["lam"] + 4] = colpack(f("lru_lambda")[l])
        pc[l, :, o["gl"]:o["gl"] + 4] = colpack(f("gn_lru_g")[l])
        for h in range(8):
            pc[l, 0:64, o["bq"] + h] = b_in[l, 1024 + h * 64:1024 + (h + 1) * 64]
        for g in range(2):
            pc[l, 0:64, o["bkc"] + g] = b_in[l, 1536 + g * 64:1536 + (g + 1) * 64]
            pc[l, 0:64, o["bvc"] + g] = b_in[l, 1664 + g * 64:1664 + (g + 1) * 64]
            pc[l, 0:64, o["bks"] + g] = b_in[l, 1792 + g * 64:1792 + (g + 1) * 64]
            pc[l, 0:64, o["bkw"] + g] = b_in[l, 2048 + g * 64:2048 + (g + 1) * 64]
        pc[l, 0:64, o["kb1"]] = f("cmpk_b1")[l]
        pc[l, 0:64, o["kb2"]] = f("cmpk_b2")[l]
        pc[l, 0:64, o["vb1"]] = f("cmpv_b1")[l]
        r = PR_OFF
        pr[l, :, r["bvs"]:r["bvs"] + 128] = b_in[l, 1920:2048][None, :]
        pr[l, :, r["bvw"]:r["bvw"] + 128] = b_in[l, 2176:2304][None, :]
        pr[l, :, r["bgt"]:r["bgt"] + 24] = b_in[l, 2304:2328][None, :]
        pr[l, :, r["gn"]:r["gn"] + 512] = f("gn_nsa_g")[l][None, :]
        pr[l, :, r["l1g"]:r["l1g"] + 1024] = f("ln1_g")[l][None, :]
        pr[l, :, r["l1b"]:r["l1b"] + 1024] = f("ln1_b")[l][None, :]
        pr[l, :, r["l2g"]:r["l2g"] + 1024] = f("ln2_g")[l][None, :]
        pr[l, :, r["l2b"]:r["l2b"] + 1024] = f("ln2_b")[l][None, :]
    m = {"pc": pc, "pr": pr}
    wa = f("lru_wa")
    wx = f("lru_wx")
    wabd = np.zeros((L, 4, 128, 128), np.float32)
    wxbd = np.zeros((L, 4, 128, 128), np.float32)
    for l in range(L):
        for c in range(4):
            for j in range(2):
                wabd[l, c, j * 64:(j + 1) * 64, j * 64:(j + 1) * 64] = wa[l, 2 * c + j]
                wxbd[l, c, j * 64:(j + 1) * 64, j * 64:(j + 1) * 64] = wx[l, 2 * c + j]
    m["wa_bd"] = wabd
    m["wx_bd"] = wxbd
    m["w1k"] = np.ascontiguousarray(f("cmpk_w1").reshape(L, 32, 64, 64).transpose(0, 2, 1, 3))
    m["w1v"] = np.ascontiguousarray(f("cmpv_w1").reshape(L, 32, 64, 64).transpose(0, 2, 1, 3))
    m["w2k"] = f("cmpk_w2")
    m["w2va"] = np.ascontiguousarray(np.concatenate([f("cmpv_w2"), f("cmpv_b2")[:, None, :]], axis=1))
    pk = np.zeros((L, 64, 34), np.float32)
    pv = np.zeros((L, 64, 34), np.float32)
    pk[:, :, 0:32] = f("cmp_pos_k").transpose(0, 2, 1)
    pv[:, :, 0:32] = f("cmp_pos_v").transpose(0, 2, 1)
    m["posk"] = pk
    m["posv"] = pv
    m["w_in"] = f("w_in")
    m["w_out"] = f("w_out")
    m["wq"] = f("peer_wq")
    m["skT"] = np.ascontiguousarray(f("peer_subkeys").transpose(0, 3, 1, 2))
    m["uT"] = np.ascontiguousarray(f("peer_u").transpose(0, 2, 1))
    m["vtab"] = f("peer_v")
    m.update(_consts())
    return m


def kernel(**inputs):
    x = np.asarray(inputs["x"], dtype=np.float32)
    shared = _pack(inputs)
    nc = build()
    in_maps = []
    for b in range(8):
        d = dict(shared)
        d["x"] = np.ascontiguousarray(x[b])
        in_maps.append(d)
    res = run_bass_kernel_spmd(nc, in_maps, core_ids=list(range(8)))
    return np.stack([np.asarray(r["y"], dtype=np.float32) for r in res.results], axis=0)
```

```python
import numpy as np
from contextlib import ExitStack
import concourse.bass as bass
import concourse.mybir as mybir
from concourse.bass_utils import run_bass_kernel_spmd

F32 = mybir.dt.float32
BF16 = mybir.dt.bfloat16
AF = mybir.ActivationFunctionType
ALU = mybir.AluOpType

S_LEN = 4096
DM = 1024
NL = 2
ALPHA = (2 * NL) ** 0.25
EPS = 1e-5
NEG = -1e30

PC_SPEC = [("bx", 4), ("bg", 4), ("cw", 16), ("cb", 4), ("ba", 4), ("bxg", 4), ("lam", 4), ("gl", 4),
           ("bq", 8), ("bkc", 2), ("bvc", 2), ("bks", 2), ("bkw", 2), ("kb1", 1), ("kb2", 1), ("vb1", 1)]
PC_OFF = {}
_o = 0
for _n, _c in PC_SPEC:
    PC_OFF[_n] = _o
    _o += _c
NPC = _o
PR_SPEC = [("bvs", 128), ("bvw", 128), ("bgt", 24), ("gn", 512), ("l1g", 1024), ("l1b", 1024), ("l2g", 1024), ("l2b", 1024)]
PR_OFF = {}
_o = 0
for _n, _c in PR_SPEC:
    PR_OFF[_n] = _o
    _o += _c
NPR = _o


class Buf:
    __slots__ = ("w", "r")

    def __init__(self):
        self.w = {}
        self.r = {}


class T:
    def __init__(self, t):
        self.t = t
        self.b = Buf()

    def __getitem__(self, k):
        return self.t[k]


class V:
    def __init__(self, ap):
        self.ap = ap
        self.b = Buf()


class Sched:
    def __init__(self, nc, stack, ndma=32):
        self.nc = nc
        self.eng = {"pe": nc.tensor, "dve": nc.vector, "act": nc.scalar, "pool": nc.gpsimd, "sp": nc.sync}
        self.sem = {}
        self.cnt = {}
        for k in self.eng:
            self.sem[k] = stack.enter_context(nc.semaphore("s_" + k))
            self.cnt[k] = 0
        self.ndma = ndma
        for i in range(ndma):
            k = "d%d" % i
            self.sem[k] = stack.enter_context(nc.semaphore("s_" + k))
            self.cnt[k] = 0
        self.seen = {e: {} for e in self.eng}
        self.rr = 0

    def _deps(self, reads, writes, join):
        deps = {}

        def add(d):
            for s, v in d.items():
                if deps.get(s, 0) < v:
                    deps[s] = v
        for t in reads:
            add(t.b.w)
        for t in writes:
            if not join:
                add(t.b.w)
            add(t.b.r)
        return deps

    def _wait(self, e, deps):
        for s, v in deps.items():
            if s == "pe" and e == "pe":
                continue
            if self.seen[e].get(s, 0) >= v:
                continue
            self.eng[e].wait_ge(self.sem[s], v)
            self.seen[e][s] = v

    def _mark(self, tok, reads, writes, join):
        s, v = tok
        for t in reads:
            t.b.r[s] = v
        for t in writes:
            if join:
                t.b.w[s] = v
            else:
                t.b.w = {s: v}
                t.b.r = {}

    def op(self, e, fn, reads=(), writes=(), join=False):
        self._wait(e, self._deps(reads, writes, join))
        ins = fn(self.eng[e])
        ins.then_inc(self.sem[e], 1)
        self.cnt[e] += 1
        self._mark((e, self.cnt[e]), reads, writes, join)

    def dma(self, q, out, in_, reads=(), writes=(), join=False, **kw):
        k = "d%d" % self.rr
        self.rr = (self.rr + 1) % self.ndma
        deps = self._deps(reads, writes, join)
        if self.cnt[k] > 0:
            deps[k] = max(deps.get(k, 0), self.cnt[k])
        self._wait(q, deps)
        self.eng[q].dma_start(out=out, in_=in_, **kw).then_inc(self.sem[k], 16)
        self.cnt[k] += 16
        self._mark((k, self.cnt[k]), reads, writes, join)

    def barrier(self):
        for e in self.eng:
            for s, v in self.cnt.items():
                if v > 0 and self.seen[e].get(s, 0) < v:
                    self.eng[e].wait_ge(self.sem[s], v)
                    self.seen[e][s] = v


def build(nlayers=NL, debug=False, stages=("xT", "lru", "nsa", "out", "peer")):
    nc = bass.Bass("TRN2", target_bir_lowering=False)

    def din(name, shape, dt=F32):
        return nc.dram_tensor(name, list(shape), dt, kind="ExternalInput").ap()

    dbg = set(debug) if debug else set()

    def dscr(name, shape, dt=F32):
        return nc.dram_tensor(name, list(shape), dt, kind=("ExternalOutput" if name in dbg else "Internal")).ap()

    L = NL
    x_in = din("x", [S_LEN, DM])
    w_in = din("w_in", [L, DM, 2328])
    pc_d = din("pc", [L, 128, NPC])
    pr_d = din("pr", [L, 128, NPR])
    wa_bd = din("wa_bd", [L, 4, 128, 128])
    wx_bd = din("wx_bd", [L, 4, 128, 128])
    w1k_d = din("w1k", [L, 64, 32, 64])
    w1v_d = din("w1v", [L, 64, 32, 64])
    w2k_d = din("w2k", [L, 64, 64])
    w2va_d = din("w2va", [L, 65, 64])
    posk_d = din("posk", [L, 64, 34])
    posv_d = din("posv", [L, 64, 34])
    wout_d = din("w_out", [L, DM, DM])
    wq_d = din("wq", [L, DM, 2048])
    skT_d = din("skT", [L, 128, 2, 128])
    big = "peer" in stages
    uT_d = din("uT", [L, DM, 16384] if big else [L, 8, 8])
    v_d = din("vtab", [L, 16384, DM] if big else [L, 8, 8])
    c_ident = din("c_ident", [128, 128])
    c_qpos = din("c_qpos", [4, 8, S_LEN])
    c_kpos = din("c_kpos", [4, S_LEN])
    c_cpos = din("c_cpos", [4, 256])
    c_selmat = din("c_selmat", [128, 2, 64])
    c_fb = din("c_fb", [128, 32, 64])
    c_expand = din("c_expand", [64, 32, 128])
    y_out = nc.dram_tensor("y", [S_LEN, DM], F32, kind="ExternalOutput").ap()

    xT_d = dscr("xT_d", [DM, S_LEN], BF16)
    ylruT_d = dscr("ylruT_d", [512, S_LEN], BF16)
    ynsa_d = dscr("ynsa_d", [S_LEN, 512])
    x1_d = dscr("x1_d", [S_LEN, DM])
    x1T_d = dscr("x1T_d", [DM, S_LEN], BF16)
    resid_d = dscr("resid_d", [S_LEN, DM])
    uTb_l = [dscr("uTb_d%d" % i, [DM, 16384], BF16) for i in range(NL)]
    vb_l = [dscr("vb_d%d" % i, [16384, DM], BF16) for i in range(NL)]

    with ExitStack() as top:
        S = Sched(nc, top)

        uid = [0]

        def sb(stk, name, shape, dt=F32):
            uid[0] += 1
            return T(stk.enter_context(nc.sbuf_tensor("sb%d_%s" % (uid[0], name), list(shape), dt)))

        P2 = [top.enter_context(nc.psum_tensor("ps%d" % i, [128, 1024], F32)) for i in range(4)]
        BANK = [V(P2[i // 2][:, (i % 2) * 512:(i % 2 + 1) * 512]) for i in range(8)]

        def act(out, in_, func, bias=None, scale=None, accum_out=None):
            kw = {}
            if bias is not None:
                kw["bias"] = bias
            if scale is not None:
                kw["scale"] = scale
            if accum_out is not None:
                kw["accum_out"] = accum_out
            return lambda e: e.activation(out=out, in_=in_, func=func, **kw)

        def copy(out, in_):
            return lambda e: e.tensor_copy(out=out, in_=in_)

        def mm(out, lhsT, rhs, start, stop):
            return lambda e: e.matmul(out, lhsT, rhs, start=start, stop=stop)

        def tt(out, in0, in1, op):
            return lambda e: e.tensor_tensor(out=out, in0=in0, in1=in1, op=op)

        def ts(out, in0, s1, s2, op0, op1=None):
            if op1 is None:
                return lambda e: e.tensor_scalar(out=out, in0=in0, scalar1=s1, scalar2=None, op0=op0)
            return lambda e: e.tensor_scalar(out=out, in0=in0, scalar1=s1, scalar2=s2, op0=op0, op1=op1)

        def stt(out, in0, scalar, in1, op0, op1):
            return lambda e: e.scalar_tensor_tensor(out=out, in0=in0, scalar=scalar, in1=in1, op0=op0, op1=op1)

        def memset(ap, v):
            return lambda e: e.memset(ap, v)

        identf = sb(top, "identf", [128, 128])
        identb = sb(top, "identb", [128, 128], BF16)
        onesf = sb(top, "onesf", [128, 128])
        S.dma("sp", identf[:], c_ident, writes=[identf])
        S.op("dve", copy(identb[:], identf[:]), reads=[identf], writes=[identb])
        S.op("dve", memset(onesf[:], 1.0), writes=[onesf])

        def ldcast(dst_t, dst_ap, src_ap):
            S.dma("pool", dst_ap, src_ap, writes=[dst_t], join=True)

        def stage_xT(src, dst):
            with ExitStack() as stk:
                xin = [sb(stk, "xin%d" % i, [128, DM]) for i in range(2)]
                xo = [sb(stk, "xo%d" % i, [128, 8, 512], BF16) for i in range(2)]
                dstv = dst.rearrange("(k p) t -> p k t", p=128)
                for t_ in range(32):
                    xi = xin[t_ % 2]
                    S.dma("sp", xi[:], src[t_ * 128:(t_ + 1) * 128, :], writes=[xi])
                    g = t_ // 4
                    o = xo[g % 2]
                    j = t_ % 4
                    for hb in range(2):
                        bk = BANK[hb + 2 * (t_ % 2)]
                        for kk in range(4):
                            k = hb * 4 + kk
                            S.op("pe", lambda e: e.transpose(out=bk.ap[:, kk * 128:(kk + 1) * 128], in_=xi[:, k * 128:(k + 1) * 128], identity=identf[:]),
                                 reads=[xi, identf], writes=[bk])
                        S.op("dve" if hb == 0 else "act",
                             copy(o[:, hb * 4:(hb + 1) * 4, j * 128:(j + 1) * 128], bk.ap.rearrange("p (k t) -> p k t", k=4)) if hb == 0 else
                             act(o[:, hb * 4:(hb + 1) * 4, j * 128:(j + 1) * 128], bk.ap.rearrange("p (k t) -> p k t", k=4), AF.Copy),
                             reads=[bk], writes=[o], join=True)
                    if j == 3:
                        S.dma("sp", dstv[:, :, g * 512:(g + 1) * 512], o[:], reads=[o])
                    pump(4)
                S.barrier()

        def stage_lru(l):
            with ExitStack() as stk:
                xT = sb(stk, "xT", [128, 8, S_LEN], BF16)
                S.dma("sp", xT[:], xT_d.rearrange("(k p) t -> p k t", p=128), writes=[xT])
                wl = sb(stk, "wl", [128, 8, 1024], BF16)
                for k in range(8):
                    ldcast(wl, wl[:, k, :], w_in[l, k * 128:(k + 1) * 128, 0:1024])
                pc = sb(stk, "pc", [128, NPC])
                S.dma("sp", pc[:], pc_d[l], writes=[pc])
                wab = sb(stk, "wab", [128, 4, 128], BF16)
                wxb = sb(stk, "wxb", [128, 4, 128], BF16)
                for c in range(4):
                    ldcast(wab, wab[:, c, :], wa_bd[l, c])
                    ldcast(wxb, wxb[:, c, :], wx_bd[l, c])
                cch = sb(stk, "cch", [128, 4])
                cch2 = sb(stk, "cch2", [128, 4])
                tmpc = sb(stk, "tmpc", [128, 4])
                lam = pc[:, PC_OFF["lam"]:PC_OFF["lam"] + 4]
                S.op("act", act(tmpc[:], lam, AF.Exp, scale=-1.0), reads=[pc], writes=[tmpc])
                S.op("act", act(tmpc[:], tmpc[:], AF.Ln, bias=1.0), reads=[tmpc], writes=[tmpc])
                S.op("dve", ts(cch[:], tmpc[:], -8.0, None, ALU.mult), reads=[tmpc], writes=[cch])
                S.op("dve", ts(cch2[:], tmpc[:], -16.0, None, ALU.mult), reads=[tmpc], writes=[cch2])

                ylru = sb(stk, "ylru", [128, 4, S_LEN], BF16)
                ssq = sb(stk, "ssq", [128, S_LEN])

                def rot(name, shape, dt=F32, n=2):
                    return [sb(stk, "%s%d" % (name, i), shape, dt) for i in range(n)]
                xb_ = rot("xb", [128, 515])
                gg_ = rot("gg", [128, 512])
                xc_ = rot("xc", [128, 512])
                xcb_ = rot("xcb", [128, 512], BF16)
                r_ = rot("r", [128, 512])
                i_ = rot("i", [128, 512])
                a_ = rot("a", [128, 512])
                s_ = rot("s", [128, 512])
                u_ = rot("u", [128, 512])
                h_ = rot("h", [128, 512])
                y_ = rot("y", [128, 512])
                q_ = rot("ysq", [128, 512])
                it = 0
                for c in range(4):
                    def col(nm, j=0):
                        o = PC_OFF[nm] + j
                        return pc[:, o:o + 1]
                    for tb in range(8):
                        p = it % 2
                        xb, gg, xc, xcb, r, i, a, s, u, h, y, ysq = (xb_[p], gg_[p], xc_[p], xcb_[p], r_[p], i_[p], a_[p], s_[p], u_[p], h_[p], y_[p], q_[p])
                        xbp, hp = xb_[1 - p], h_[1 - p]
                        bx, bgk, br, bi = BANK[0 + 4 * p], BANK[1 + 4 * p], BANK[2 + 4 * p], BANK[3 + 4 * p]
                        tsl = slice(tb * 512, (tb + 1) * 512)
                        for k in range(8):
                            S.op("pe", mm(bx.ap, wl[:, k, c * 128:(c + 1) * 128], xT[:, k, tsl], k == 0, k == 7), reads=[wl, xT], writes=[bx])
                        for k in range(8):
                            S.op("pe", mm(bgk.ap, wl[:, k, 512 + c * 128:512 + (c + 1) * 128], xT[:, k, tsl], k == 0, k == 7), reads=[wl, xT], writes=[bgk])
                        if tb == 0:
                            S.op("dve", memset(xb[:, 0:3], 0.0), writes=[xb])
                        else:
                            S.op("dve", copy(xb[:, 0:3], xbp[:, 512:515]), reads=[xbp], writes=[xb])
                        S.op("act", act(xb[:, 3:515], bx.ap, AF.Identity, bias=col("bx", c)), reads=[bx, pc], writes=[xb], join=True)
                        S.op("act", act(gg[:], bgk.ap, AF.Gelu_apprx_tanh, bias=col("bg", c)), reads=[bgk, pc], writes=[gg])
                        S.op("dve", ts(xc[:], xb[:, 0:512], col("cw", 0 * 4 + c), col("cb", c), ALU.mult, ALU.add), reads=[xb, pc], writes=[xc])
                        for j in range(1, 4):
                            S.op("dve", stt(xc[:], xb[:, j:j + 512], col("cw", j * 4 + c), xc[:], ALU.mult, ALU.add), reads=[xb, pc, xc], writes=[xc])
                        S.op("pool", copy(xcb[:], xc[:]), reads=[xc], writes=[xcb])
                        S.op("pe", mm(br.ap, wab[:, c, :], xcb[:], True, True), reads=[wab, xcb], writes=[br])
                        S.op("pe", mm(bi.ap, wxb[:, c, :], xcb[:], True, True), reads=[wxb, xcb], writes=[bi])
                        S.op("act", act(r[:], br.ap, AF.Sigmoid, bias=col("ba", c)), reads=[br, pc], writes=[r])
                        S.op("act", act(i[:], bi.ap, AF.Sigmoid, bias=col("bxg", c)), reads=[bi, pc], writes=[i])
                        S.op("act", act(a[:], r[:], AF.Exp, scale=cch[:, c:c + 1]), reads=[r, cch], writes=[a])
                        S.op("act", act(s[:], r[:], AF.Exp, scale=cch2[:, c:c + 1]), reads=[r, cch2], writes=[s])
                        S.op("dve", ts(s[:], s[:], -1.0, 1.0, ALU.mult, ALU.add), reads=[s], writes=[s])
                        S.op("act", act(s[:], s[:], AF.Sqrt), reads=[s], writes=[s])
                        S.op("pool", tt(u[:], i[:], xc[:], ALU.mult), reads=[i, xc], writes=[u])
                        S.op("dve", tt(u[:], u[:], s[:], ALU.mult), reads=[u, s], writes=[u])
                        init = 0.0 if tb == 0 else hp[:, 511:512]
                        S.op("dve", lambda e: e.tensor_tensor_scan(out=h[:], data0=a[:], data1=u[:], initial=init, op0=ALU.mult, op1=ALU.add),
                             reads=[a, u] + ([] if tb == 0 else [hp]), writes=[h])
                        S.op("dve", tt(y[:], h[:], gg[:], ALU.mult), reads=[h, gg], writes=[y])
                        S.op("act", act(ysq[:], y[:], AF.Square), reads=[y], writes=[ysq])
                        S.op("pool", copy(ylru[:, c, tsl], y[:]), reads=[y], writes=[ylru], join=True)
                        S.op("pe", mm(bx.ap, onesf[:], ysq[:], True, True), reads=[onesf, ysq], writes=[bx])
                        if c == 0:
                            S.op("dve", copy(ssq[:, tsl], bx.ap), reads=[bx], writes=[ssq], join=True)
                        else:
                            S.op("dve", tt(ssq[:, tsl], ssq[:, tsl], bx.ap, ALU.add), reads=[bx, ssq], writes=[ssq])
                        it += 1
                S.op("dve", ts(ssq[:], ssq[:], 1.0 / 512, EPS, ALU.mult, ALU.add), reads=[ssq], writes=[ssq])
                S.op("act", act(ssq[:], ssq[:], AF.Sqrt), reads=[ssq], writes=[ssq])
                S.op("dve", lambda e: e.reciprocal(out=ssq[:], in_=ssq[:]), reads=[ssq], writes=[ssq])
                for c in range(4):
                    o = PC_OFF["gl"] + c
                    S.op("dve", stt(ylru[:, c, :], ylru[:, c, :], pc[:, o:o + 1], ssq[:], ALU.mult, ALU.mult), reads=[ylru, pc, ssq], writes=[ylru])
                S.dma("sp", ylruT_d.rearrange("(c p) t -> p c t", p=128), ylru[:], reads=[ylru])
                S.barrier()

        def stage_nsa(l):
            with ExitStack() as stk:
                xT = sb(stk, "xT", [128, 8, S_LEN], BF16)
                S.dma("sp", xT[:], xT_d.rearrange("(k p) t -> p k t", p=128), writes=[xT])
                wn = sb(stk, "wn", [128, 8, 1304], BF16)
                for k in range(8):
                    ldcast(wn, wn[:, k, 0:1024], w_in[l, k * 128:(k + 1) * 128, 1024:2048])
                    ldcast(wn, wn[:, k, 1024:1304], w_in[l, k * 128:(k + 1) * 128, 2048:2328])
                pc = sb(stk, "pc", [128, NPC])
                S.dma("sp", pc[:], pc_d[l], writes=[pc])
                prn = sb(stk, "prn", [128, 280])
                S.dma("sp", prn[:], pr_d[l, :, 0:280], writes=[prn])
                bq8 = sb(stk, "bq8", [128, 8])
                S.op("dve", ts(bq8[:], pc[:, PC_OFF["bq"]:PC_OFF["bq"] + 8], 0.125, None, ALU.mult), reads=[pc], writes=[bq8])
                fb = sb(stk, "fb", [128, 32, 64])
                S.dma("sp", fb[:], c_fb, writes=[fb])
                expd = sb(stk, "expd", [64, 32, 128], BF16)
                for j in range(4):
                    ldcast(expd, expd[:, j * 8:(j + 1) * 8, :], c_expand[:, j * 8:(j + 1) * 8, :])
                w1k = sb(stk, "w1k", [64, 32, 64], BF16)
                w1v = sb(stk, "w1v", [64, 32, 64], BF16)
                for j in range(2):
                    ldcast(w1k, w1k[:, j * 16:(j + 1) * 16, :], w1k_d[l, :, j * 16:(j + 1) * 16, :])
                    ldcast(w1v, w1v[:, j * 16:(j + 1) * 16, :], w1v_d[l, :, j * 16:(j + 1) * 16, :])
                posk = sb(stk, "posk", [64, 34], BF16)
                posv = sb(stk, "posv", [64, 34], BF16)
                ldcast(posk, posk[:], posk_d[l])
                ldcast(posv, posv[:], posv_d[l])
                w2k = sb(stk, "w2k", [64, 64], BF16)
                w2va = sb(stk, "w2va", [65, 64], BF16)
                ldcast(w2k, w2k[:], w2k_d[l])
                ldcast(w2va, w2va[:], w2va_d[l])

                qTa = sb(stk, "qTa", [68, 4, S_LEN], BF16)
                ksa = sb(stk, "ksa", [68, S_LEN], BF16)
                kwa = sb(stk, "kwa", [68, S_LEN], BF16)
                kca = sb(stk, "kca", [68, 256], BF16)
                kcr = sb(stk, "kcr", [64, S_LEN + 32], BF16)
                vcr = sb(stk, "vcr", [64, S_LEN + 32], BF16)
                vsa = sb(stk, "vsa", [128, 32, 65], BF16)
                vwa = sb(stk, "vwa", [128, 32, 65], BF16)
                vca = sb(stk, "vca", [128, 2, 129], BF16)
                gsig = sb(stk, "gsig", [128, 32, 12])
                h1k = sb(stk, "h1k", [65, 256], BF16)
                h1v = sb(stk, "h1v", [65, 256], BF16)
                b1k = sb(stk, "b1k", [64, 1])
                b1v = sb(stk, "b1v", [64, 1])
                es_ = [sb(stk, "es%d" % i, [128, 512], BF16) for i in range(4)]
                mT_ = [sb(stk, "mT%d" % i, [64, 128], BF16) for i in range(2)]
                impa = sb(stk, "impa", [128, 64])
                impt = sb(stk, "impt", [128, 64])
                selm = sb(stk, "selm", [128, 64])
                m8 = sb(stk, "m8", [128, 16])
                rd = sb(stk, "rd", [128, 3, 4])
                coef = sb(stk, "coef", [128, 3, 4])
                ya_ = [sb(stk, "ya%d" % i, [128, 256]) for i in range(2)]
                ytmp = sb(stk, "ytmp", [128, 256])
                gtmp = sb(stk, "gtmp", [128, 12])

                for g in range(2):
                    for cb in range(4):
                        csl = slice(cb * 1024, (cb + 1) * 1024)
                        ldcast(qTa, qTa[64:68, :, csl], c_qpos[:, 4 * g:4 * g + 4, csl])
                        ldcast(ksa, ksa[64:68, csl], c_kpos[:, csl])
                        ldcast(kwa, kwa[64:68, csl], c_kpos[:, csl])
                    ldcast(kca, kca[64:68, :], c_cpos)
                    S.op("pool", memset(kcr[:, S_LEN:S_LEN + 32], 0.0), writes=[kcr], join=True)
                    S.op("pool", memset(vcr[:, S_LEN:S_LEN + 32], 0.0), writes=[vcr], join=True)
                    S.op("pool", memset(vsa[:], 1.0), writes=[vsa])
                    S.op("pool", memset(vwa[:], 1.0), writes=[vwa])
                    S.op("pool", memset(vca[:], 1.0), writes=[vca])
                    ldcast(vca, vca[:, :, 65:129], c_selmat)
                    S.op("pool", memset(h1k[64:65, :], 1.0), writes=[h1k], join=True)
                    S.op("pool", memset(h1v[64:65, :], 1.0), writes=[h1v], join=True)
                    projs = []
                    for r4 in range(4):
                        hh = 4 * g + r4
                        projs.append((hh * 64, qTa, lambda sl, r4=r4: qTa[0:64, r4, sl], bq8[0:64, hh:hh + 1], 0.125, bq8))
                    projs.append((512 + g * 64, kcr, lambda sl: kcr[0:64, sl], pc[0:64, PC_OFF["bkc"] + g:PC_OFF["bkc"] + g + 1], 1.0, pc))
                    projs.append((640 + g * 64, vcr, lambda sl: vcr[0:64, sl], pc[0:64, PC_OFF["bvc"] + g:PC_OFF["bvc"] + g + 1], 1.0, pc))
                    projs.append((768 + g * 64, ksa, lambda sl: ksa[0:64, sl], pc[0:64, PC_OFF["bks"] + g:PC_OFF["bks"] + g + 1], 1.0, pc))
                    projs.append((1024 + g * 64, kwa, lambda sl: kwa[0:64, sl], pc[0:64, PC_OFF["bkw"] + g:PC_OFF["bkw"] + g + 1], 1.0, pc))
                    it = 0
                    for (coff, dt_, dfn, bias_ap, scl, bias_t) in projs:
                        for tb in range(8):
                            bk = BANK[it % 4]
                            it += 1
                            tsl = slice(tb * 512, (tb + 1) * 512)
                            for k in range(8):
                                S.op("pe", mm(bk.ap[0:64, :], wn[:, k, coff:coff + 64], xT[:, k, tsl], k == 0, k == 7), reads=[wn, xT], writes=[bk])
                            S.op("act", act(dfn(tsl), bk.ap[0:64, :], AF.Identity, bias=bias_ap, scale=scl), reads=[bk, bias_t], writes=[dt_], join=True)
                    for t_ in range(32):
                        tsl = slice(t_ * 128, (t_ + 1) * 128)
                        b1_, b2_, b3_ = BANK[4], BANK[5], BANK[6]
                        for k in range(8):
                            S.op("pe", mm(b1_.ap[:, 0:64], xT[:, k, tsl], wn[:, k, 896 + g * 64:896 + g * 64 + 64], k == 0, k == 7), reads=[wn, xT], writes=[b1_])
                        for k in range(8):
                            S.op("pe", mm(b2_.ap[:, 0:64], xT[:, k, tsl], wn[:, k, 1152 + g * 64:1152 + g * 64 + 64], k == 0, k == 7), reads=[wn, xT], writes=[b2_])
                        for k in range(8):
                            S.op("pe", mm(b3_.ap[:, 0:12], xT[:, k, tsl], wn[:, k, 1280 + g * 12:1280 + g * 12 + 12], k == 0, k == 7), reads=[wn, xT], writes=[b3_])
                        S.op("dve", tt(vsa[:, t_, 0:64], b1_.ap[:, 0:64], prn[:, g * 64:g * 64 + 64], ALU.add), reads=[b1_, prn], writes=[vsa], join=True)
                        S.op("dve", tt(vwa[:, t_, 0:64], b2_.ap[:, 0:64], prn[:, 128 + g * 64:128 + g * 64 + 64], ALU.add), reads=[b2_, prn], writes=[vwa], join=True)
                        S.op("dve", tt(gtmp[:], b3_.ap[:, 0:12], prn[:, 256 + g * 12:256 + g * 12 + 12], ALU.add), reads=[b3_, prn], writes=[gtmp])
                        S.op("act", act(gsig[:, t_, :], gtmp[:], AF.Sigmoid), reads=[gtmp], writes=[gsig], join=True)
                    for (raw, w1, pos, b1t, b1name, h1) in ((kcr, w1k, posk, b1k, "kb1", h1k), (vcr, w1v, posv, b1v, "vb1", h1v)):
                        bA, bB = BANK[0], BANK[1]
                        for l_ in range(32):
                            S.op("pe", mm(bA.ap[0:64, 0:256], w1[:, l_, :], raw[:, l_:l_ + 4096:16], l_ == 0, l_ == 31), reads=[w1, raw], writes=[bA])
                        for l_ in range(32):
                            S.op("pe", mm(bB.ap[0:64, 0:2], w1[:, l_, :], pos[:, l_:l_ + 2], l_ == 0, l_ == 31), reads=[w1, pos], writes=[bB])
                        o = PC_OFF[b1name]
                        S.op("dve", tt(b1t[:], bB.ap[0:64, 0:1], pc[0:64, o:o + 1], ALU.add), reads=[bB, pc], writes=[b1t])
                        S.op("act", act(h1[0:64, :], bA.ap[0:64, 0:256], AF.Gelu_apprx_tanh, bias=b1t[:]), reads=[bA, b1t], writes=[h1], join=True)
                    bA = BANK[2]
                    S.op("pe", mm(bA.ap[0:64, 0:256], w2k[:], h1k[0:64, :], True, True), reads=[w2k, h1k], writes=[bA])
                    o = PC_OFF["kb2"]
                    S.op("act", act(kca[0:64, :], bA.ap[0:64, 0:256], AF.Identity, bias=pc[0:64, o:o + 1]), reads=[bA, pc], writes=[kca], join=True)
                    for ch in range(2):
                        bB = BANK[3]
                        S.op("pe", mm(bB.ap[:, 0:64], h1v[0:65, ch * 128:(ch + 1) * 128], w2va[:], True, True), reads=[h1v, w2va], writes=[bB])
                        S.op("dve", copy(vca[:, ch, 0:64], bB.ap[:, 0:64]), reads=[bB], writes=[vca], join=True)

                    bOc, bI, bOs, bOw = BANK[2], BANK[3], BANK[5], BANK[6]
                    sbank = [BANK[0], BANK[1]]
                    mbank = [BANK[7], BANK[4]]
                    ctr = {"s": 0, "e": 0, "m": 0}

                    def mk_step(kind, qi, idx):
                        qsl = slice(qi * 128, (qi + 1) * 128)
                        rhsq = qTa[:, :, qsl]
                        nch = 1 if qi < 16 else 2
                        k0 = max(0, qi - 4)
                        st = {}

                        def s1():
                            bs = sbank[ctr["s"] % 2]
                            ctr["s"] += 1
                            es = es_[ctr["e"] % 4]
                            ctr["e"] += 1
                            st["es"] = es
                            esv = es[:].rearrange("p (r q) -> p r q", r=4)
                            if kind == "c":
                                ch = idx
                                S.op("pe", mm(bs.ap, kca[:, ch * 128:(ch + 1) * 128], rhsq, True, True), reads=[kca, qTa], writes=[bs])
                                S.op("act", act(es[:], bs.ap, AF.Exp), reads=[bs], writes=[es])
                                S.op("pool", lambda e: e.affine_select(out=esv, in_=esv, pattern=[[0, 4], [1, 128]], compare_op=ALU.is_ge, fill=0.0,
                                                                       base=128 * qi - 2048 * ch - 31, channel_multiplier=-16), reads=[es], writes=[es])
                            elif kind == "w":
                                kj = idx
                                S.op("pe", mm(bs.ap, kwa[:, kj * 128:(kj + 1) * 128], rhsq, True, True), reads=[kwa, qTa], writes=[bs])
                                S.op("act", act(es[:], bs.ap, AF.Exp), reads=[bs], writes=[es])
                                if kj == qi:
                                    S.op("pool", lambda e: e.affine_select(out=esv, in_=esv, pattern=[[0, 4], [1, 128]], compare_op=ALU.is_ge, fill=0.0,
                                                                           base=0, channel_multiplier=-1), reads=[es], writes=[es])
                                if kj == qi - 4:
                                    S.op("pool", lambda e: e.affine_select(out=esv, in_=esv, pattern=[[0, 4], [-1, 128]], compare_op=ALU.is_ge, fill=0.0,
                                                                           base=-1, channel_multiplier=1), reads=[es], writes=[es])
                            else:
                                kj = idx
                                mT = mT_[qi % 2]
                                bm = mbank[ctr["m"] % 2]
                                ctr["m"] += 1
                                S.op("pe", mm(bs.ap, ksa[:, kj * 128:(kj + 1) * 128], rhsq, True, True), reads=[ksa, qTa], writes=[bs])
                                S.op("act", act(es[:], bs.ap, AF.Exp), reads=[bs], writes=[es])
                                S.op("pe", mm(bm.ap[:, 0:128], expd[:, kj, :], mT[:], True, True), reads=[expd, mT], writes=[bm])
                                S.op("dve", tt(esv, esv, bm.ap[:, 0:128].unsqueeze(1).to_broadcast([128, 4, 128]), ALU.mult), reads=[es, bm], writes=[es])
                                if kj == qi:
                                    S.op("pool", lambda e: e.affine_select(out=esv, in_=esv, pattern=[[0, 4], [1, 128]], compare_op=ALU.is_ge, fill=0.0,
                                                                           base=0, channel_multiplier=-1), reads=[es], writes=[es])

                        def s2():
                            es = st["es"]
                            if kind == "c":
                                ch = idx
                                for r4 in range(4):
                                    S.op("pe", mm(bOc.ap[:, r4 * 65:(r4 + 1) * 65], es[:, r4 * 128:(r4 + 1) * 128], vca[:, ch, 0:65], ch == 0 and r4 == 0, ch == nch - 1 and r4 == 3),
                                         reads=[es, vca], writes=[bOc])
                                for r4 in range(4):
                                    S.op("pe", mm(bI.ap[:, r4 * 64:(r4 + 1) * 64], es[:, r4 * 128:(r4 + 1) * 128], vca[:, ch, 65:129], ch == 0 and r4 == 0, ch == nch - 1 and r4 == 3),
                                         reads=[es, vca], writes=[bI])
                                if ch == nch - 1:
                                    chain(qi)
                            elif kind == "w":
                                kj = idx
                                for r4 in range(4):
                                    S.op("pe", mm(bOw.ap[:, r4 * 65:(r4 + 1) * 65], es[:, r4 * 128:(r4 + 1) * 128], vwa[:, kj, :], kj == k0 and r4 == 0, kj == qi and r4 == 3),
                                         reads=[es, vwa], writes=[bOw])
                            else:
                                kj = idx
                                for r4 in range(4):
                                    S.op("pe", mm(bOs.ap[:, r4 * 65:(r4 + 1) * 65], es[:, r4 * 128:(r4 + 1) * 128], vsa[:, kj, :], kj == 0 and r4 == 0, kj == qi and r4 == 3),
                                         reads=[es, vsa], writes=[bOs])
                                if kj == qi:
                                    combine(qi)
                        return s1, s2

                    def chain(qi):
                        mT = mT_[qi % 2]
                        ocv = bOc.ap[:, 0:260].rearrange("p (r d) -> p r d", r=4)
                        S.op("dve", ts(rd[:, 0, :], ocv[:, :, 64], 1e-30, None, ALU.add), reads=[bOc], writes=[rd])
                        S.op("dve", lambda e: e.reciprocal(out=rd[:, 0, :], in_=rd[:, 0, :]), reads=[rd], writes=[rd])
                        S.op("dve", ts(impa[:], bI.ap[:, 0:64], rd[:, 0, 0:1], None, ALU.mult), reads=[bI, rd], writes=[impa])
                        for r4 in range(1, 4):
                            S.op("dve", stt(impa[:], bI.ap[:, r4 * 64:(r4 + 1) * 64], rd[:, 0, r4:r4 + 1], impa[:], ALU.mult, ALU.add), reads=[bI, rd, impa], writes=[impa])
                        S.op("dve", tt(impa[:], impa[:], fb[:, qi, :], ALU.add), reads=[impa, fb], writes=[impa])
                        S.op("dve", lambda e: e.max(out=m8[:, 0:8], in_=impa[:]), reads=[impa], writes=[m8])
                        S.op("dve", lambda e: e.match_replace(out=impt[:], in_to_replace=m8[:, 0:8], in_values=impa[:], imm_value=-3e38), reads=[impa, m8], writes=[impt])
                        S.op("dve", lambda e: e.max(out=m8[:, 8:16], in_=impt[:]), reads=[impt], writes=[m8])
                        S.op("dve", ts(selm[:], impa[:], m8[:, 15:16], None, ALU.is_ge), reads=[impa, m8], writes=[selm])
                        S.op("pe", lambda e: e.transpose(out=bI.ap[0:64, 0:128], in_=selm[:], identity=identf[:]), reads=[selm, identf], writes=[bI])
                        S.op("act", act(mT[:], bI.ap[0:64, 0:128], AF.Copy), reads=[bI], writes=[mT])

                    def combine(qi):
                        qsl = slice(qi * 128, (qi + 1) * 128)
                        ya = ya_[qi % 2]
                        ocv = bOc.ap[:, 0:260].rearrange("p (r d) -> p r d", r=4)
                        osv = bOs.ap[:, 0:260].rearrange("p (r d) -> p r d", r=4)
                        owv = bOw.ap[:, 0:260].rearrange("p (r d) -> p r d", r=4)
                        S.op("dve", ts(rd[:, 1, :], osv[:, :, 64], 1e-30, None, ALU.add), reads=[bOs], writes=[rd], join=True)
                        S.op("dve", ts(rd[:, 2, :], owv[:, :, 64], 1e-30, None, ALU.add), reads=[bOw], writes=[rd], join=True)
                        S.op("dve", lambda e: e.reciprocal(out=rd[:, 1:3, :], in_=rd[:, 1:3, :]), reads=[rd], writes=[rd])
                        S.op("dve", tt(coef[:], rd[:], gsig[:, qi, :].rearrange("p (r b) -> p b r", b=3), ALU.mult), reads=[rd, gsig], writes=[coef])
                        yav = ya[:].rearrange("p (r d) -> p r d", r=4)
                        ytv = ytmp[:].rearrange("p (r d) -> p r d", r=4)
                        S.op("dve", tt(yav, ocv[:, :, 0:64], coef[:, 0, :].unsqueeze(2).to_broadcast([128, 4, 64]), ALU.mult), reads=[bOc, coef], writes=[ya])
                        S.op("dve", tt(ytv, osv[:, :, 0:64], coef[:, 1, :].unsqueeze(2).to_broadcast([128, 4, 64]), ALU.mult), reads=[bOs, coef], writes=[ytmp])
                        S.op("dve", tt(ya[:], ya[:], ytmp[:], ALU.add), reads=[ya, ytmp], writes=[ya])
                        S.op("dve", tt(ytv, owv[:, :, 0:64], coef[:, 2, :].unsqueeze(2).to_broadcast([128, 4, 64]), ALU.mult), reads=[bOw, coef], writes=[ytmp])
                        S.op("dve", tt(ya[:], ya[:], ytmp[:], ALU.add), reads=[ya, ytmp], writes=[ya])
                        S.dma("sp", ynsa_d[qsl, g * 256:(g + 1) * 256], ya[:], reads=[ya])

                    steps = []
                    pump(64)
                    for qi in range(32):
                        nch = 1 if qi < 16 else 2
                        for ch in range(nch):
                            steps.append(mk_step("c", qi, ch))
                        for kj in range(max(0, qi - 4), qi + 1):
                            steps.append(mk_step("w", qi, kj))
                        for kj in range(qi + 1):
                            steps.append(mk_step("s", qi, kj))
                    steps[0][0]()
                    for i in range(len(steps)):
                        if i + 1 < len(steps):
                            steps[i + 1][0]()
                        steps[i][1]()
                S.barrier()

        def layer_norm(stk_tiles, t, gb, goff, boff, out):
            stats, mv, rs = stk_tiles
            S.op("dve", lambda e: e.bn_stats(out=stats[:, 0, :], in_=t[:, 0:512]), reads=[t], writes=[stats])
            S.op("dve", lambda e: e.bn_stats(out=stats[:, 1, :], in_=t[:, 512:1024]), reads=[t], writes=[stats], join=True)
            S.op("dve", lambda e: e.bn_aggr(out=mv[:], in_=stats[:].rearrange("p a b -> p (a b)")), reads=[stats], writes=[mv])
            S.op("dve", ts(rs[:], mv[:, 1:2], EPS, None, ALU.add), reads=[mv], writes=[rs])
            S.op("act", act(rs[:], rs[:], AF.Sqrt), reads=[rs], writes=[rs])
            S.op("dve", lambda e: e.reciprocal(out=rs[:], in_=rs[:]), reads=[rs], writes=[rs])
            S.op("dve", ts(out[:], t[:], mv[:, 0:1], rs[:], ALU.subtract, ALU.mult), reads=[t, mv, rs], writes=[out])
            S.op("pool", tt(out[:], out[:], gb[:, goff:goff + 1024], ALU.mult), reads=[out, gb], writes=[out])
            S.op("pool", tt(out[:], out[:], gb[:, boff:boff + 1024], ALU.add), reads=[out, gb], writes=[out])

        def stage_out(l, resid):
            with ExitStack() as stk:
                wo = sb(stk, "wo", [128, 8, 1024], BF16)
                for k in range(8):
                    ldcast(wo, wo[:, k, :], wout_d[l, k * 128:(k + 1) * 128, :])
                ylru = sb(stk, "ylru", [128, 4, S_LEN], BF16)
                S.dma("sp", ylru[:], ylruT_d.rearrange("(c p) t -> p c t", p=128), writes=[ylru])
                gb = sb(stk, "gb", [128, 2560])
                S.dma("sp", gb[:], pr_d[l, :, 280:2840], writes=[gb])
                yn_ = [sb(stk, "yn%d" % i, [128, 512]) for i in range(2)]
                ynn_ = [sb(stk, "ynn%d" % i, [128, 512]) for i in range(2)]
                nsaT_ = [sb(stk, "nsaT%d" % i, [128, 4, 128], BF16) for i in range(2)]
                xr_ = [sb(stk, "xr%d" % i, [128, 1024]) for i in range(2)]
                t_ = [sb(stk, "t%d" % i, [128, 1024]) for i in range(2)]
                x1_ = [sb(stk, "x1%d" % i, [128, 1024]) for i in range(2)]
                x1T_ = [sb(stk, "x1T%d" % i, [128, 8, 128], BF16) for i in range(2)]
                junk = sb(stk, "junk", [128, 512])
                ss_ = [sb(stk, "ss%d" % i, [128, 1]) for i in range(2)]
                stats = sb(stk, "stats", [128, 2, 6])
                mv = sb(stk, "mv", [128, 2])
                rs = sb(stk, "rs", [128, 1])
                x1Tv = x1T_d.rearrange("(k p) t -> p k t", p=128)
                def out_front(tt_):
                    p = tt_ % 2
                    yn, ynn, nsaT, xr, ss = yn_[p], ynn_[p], nsaT_[p], xr_[p], ss_[p]
                    tsl = slice(tt_ * 128, (tt_ + 1) * 128)
                    S.dma("sp", yn[:], ynsa_d[tsl, :], writes=[yn])
                    S.dma("sp", xr[:], resid[tsl, :], writes=[xr])
                    S.op("act", act(junk[:], yn[:], AF.Square, accum_out=ss[:]), reads=[yn], writes=[junk, ss])
                    S.op("dve", ts(ss[:], ss[:], 1.0 / 512, EPS, ALU.mult, ALU.add), reads=[ss], writes=[ss])
                    S.op("act", act(ss[:], ss[:], AF.Sqrt), reads=[ss], writes=[ss])
                    S.op("dve", lambda e: e.reciprocal(out=ss[:], in_=ss[:]), reads=[ss], writes=[ss])
                    S.op("dve", stt(ynn[:], yn[:], ss[:], gb[:, 0:512], ALU.mult, ALU.mult), reads=[yn, ss, gb], writes=[ynn])
                    bT = BANK[4 + p]
                    for k in range(4):
                        S.op("pe", lambda e: e.transpose(out=bT.ap[:, k * 128:(k + 1) * 128], in_=ynn[:, k * 128:(k + 1) * 128], identity=identf[:]),
                             reads=[ynn, identf], writes=[bT])
                    S.op("act", act(nsaT[:], bT.ap.rearrange("p (k t) -> p k t", k=4), AF.Copy), reads=[bT], writes=[nsaT])
                    for hf in range(2):
                        bk = BANK[hf + 2 * p]
                        hs = slice(hf * 512, (hf + 1) * 512)
                        for k in range(4):
                            S.op("pe", mm(bk.ap, ylru[:, k, tsl], wo[:, k, hs], k == 0, False), reads=[ylru, wo], writes=[bk])
                        for k in range(4):
                            S.op("pe", mm(bk.ap, nsaT[:, k, :], wo[:, 4 + k, hs], False, k == 3), reads=[nsaT, wo], writes=[bk])

                def out_back(tt_):
                    p = tt_ % 2
                    xr, t, x1, x1T = xr_[p], t_[p], x1_[p], x1T_[p]
                    tsl = slice(tt_ * 128, (tt_ + 1) * 128)
                    for hf in range(2):
                        bk = BANK[hf + 2 * p]
                        hs = slice(hf * 512, (hf + 1) * 512)
                        S.op("dve", stt(t[:, hs], xr[:, hs], ALPHA, bk.ap, ALU.mult, ALU.add), reads=[xr, bk], writes=[t], join=True)
                    layer_norm((stats, mv, rs), t, gb, 512, 1536, x1)
                    S.dma("sp", x1_d[tsl, :], x1[:], reads=[x1])
                    for hb in range(2):
                        bk = BANK[6 + hb]
                        for kk in range(4):
                            k = hb * 4 + kk
                            S.op("pe", lambda e: e.transpose(out=bk.ap[:, kk * 128:(kk + 1) * 128], in_=x1[:, k * 128:(k + 1) * 128], identity=identf[:]),
                                 reads=[x1, identf], writes=[bk])
                        S.op("act", act(x1T[:, hb * 4:(hb + 1) * 4, :], bk.ap.rearrange("p (k t) -> p k t", k=4), AF.Copy), reads=[bk], writes=[x1T], join=True)
                    S.dma("sp", x1Tv[:, :, tsl], x1T[:], reads=[x1T])

                out_front(0)
                for tt_ in range(32):
                    if tt_ + 1 < 32:
                        out_front(tt_ + 1)
                    out_back(tt_)
                S.barrier()

        conv_q = []

        def conv_fill():
            for l in range(nlayers):
                for kp in range(8):
                    for cb in range(16):
                        conv_q.append((uTb_l[l][kp * 128:(kp + 1) * 128, cb * 1024:(cb + 1) * 1024], uT_d[l, kp * 128:(kp + 1) * 128, cb * 1024:(cb + 1) * 1024]))
                for rb in range(128):
                    conv_q.append((vb_l[l][rb * 128:(rb + 1) * 128, :], v_d[l, rb * 128:(rb + 1) * 128, :]))

        def pump(n):
            for _ in range(n):
                if not conv_q:
                    return
                dst, src = conv_q.pop(0)
                S.dma("pool", dst, src)

        def stage_peer(l, dst):
            with ExitStack() as stk:
                wq = sb(stk, "wq", [128, 8, 2048], BF16)
                for k in range(8):
                    ldcast(wq, wq[:, k, 0:1024], wq_d[l, k * 128:(k + 1) * 128, 0:1024])
                    ldcast(wq, wq[:, k, 1024:2048], wq_d[l, k * 128:(k + 1) * 128, 1024:2048])
                skT = sb(stk, "skT", [128, 2, 128], BF16)
                ldcast(skT, skT[:], skT_d[l])
                gb = sb(stk, "gb", [128, 2048])
                S.dma("sp", gb[:], pr_d[l, :, 2840:4888], writes=[gb])
                x1T_ = [sb(stk, "x1T%d" % i, [128, 8, 256], BF16) for i in range(2)]
                qT = sb(stk, "qT", [128, 16, 256], BF16)
                sab_ = [[sb(stk, "sab%d_%d" % (j, i), [128, 8, 2, 128]) for i in range(2)] for j in range(2)]
                thr_ = [[sb(stk, "thr%d_%d" % (j, i), [128, 8]) for i in range(2)] for j in range(2)]
                bia_ = [[sb(stk, "bia%d_%d" % (j, i), [128, 8]) for i in range(2)] for j in range(2)]
                sv = sb(stk, "sv", [128, 2, 16])
                tmpk = sb(stk, "tmpk", [128, 128])
                cand = sb(stk, "cand", [128, 256])
                ctmp = sb(stk, "ctmp", [128, 256])
                cv = sb(stk, "cv", [128, 16])
                cex = sb(stk, "cex", [128, 16])
                negm = sb(stk, "negm", [128, 1])
                zz = sb(stk, "zz", [128, 1])
                uT_ = [sb(stk, "uTg%d" % i, [128, 8, 512], BF16) for i in range(2)]
                vg_ = [sb(stk, "vg%d" % i, [128, 4, 1024], BF16) for i in range(2)]
                abf_ = [sb(stk, "abf%d" % i, [128, 1024], BF16) for i in range(2)]
                hid_ = [sb(stk, "hid%d" % i, [128, 4, 256], BF16) for i in range(2)]
                NR = 48
                ee_ = [sb(stk, "EE%d" % i, [128, 128], BF16) for i in range(NR)]
                gh_ = [sb(stk, "GH%d" % i, [128, 128], BF16) for i in range(NR)]
                sabi_ = [[sb(stk, "sabi%d_%d" % (j, i), [128, 8, 128]) for i in range(2)] for j in range(2)]
                rneg_ = [[sb(stk, "rneg%d_%d" % (j, i), [128, 8, 128]) for i in range(2)] for j in range(2)]
                thrm = sb(stk, "thrm", [128, 1])
                negone = sb(stk, "negone", [128, 1])
                S.op("dve", memset(negone[:], -1.0), writes=[negone])
                xr_ = [sb(stk, "xr%d" % i, [128, 1024]) for i in range(1)] * 2
                t_ = [sb(stk, "t%d" % i, [128, 1024]) for i in range(1)] * 2
                xo_ = [sb(stk, "xo%d" % i, [128, 1024]) for i in range(1)] * 2
                stats = sb(stk, "stats", [128, 2, 6])
                mv = sb(stk, "mv", [128, 2])
                rs = sb(stk, "rs", [128, 1])
                x1Tv = x1T_d.rearrange("(k p) t -> p k t", p=128)
                uTb_d, vb_d = uTb_l[l], vb_l[l]
                uTv = uTb_d.rearrange("(k p) e -> p k e", p=128)
                bA = [BANK[0], BANK[1]]
                bG = [BANK[2], BANK[3]]
                bY = [[BANK[4], BANK[5]], [BANK[6], BANK[7]]]
                psA = P2[0]
                psG = P2[1]
                cnt = {"g": 0, "w": 0, "gh": 0}

                def ab_tasks(st_):
                    x1T = x1T_[st_ % 2]
                    tk = {"proj": [], "score": [], "chain": []}
                    tk["dma"] = lambda: S.dma("sp", x1T[:], x1Tv[:, :, st_ * 256:(st_ + 1) * 256], writes=[x1T])

                    def proj(hc):
                        bk = BANK[hc % 2]
                        for k in range(8):
                            S.op("pe", mm(bk.ap[:, 0:256], wq[:, k, hc * 128:(hc + 1) * 128], x1T[:, k, :], k == 0, k == 7), reads=[wq, x1T], writes=[bk])
                        S.op("act", act(qT[:, hc, :], bk.ap[:, 0:256], AF.Copy), reads=[bk], writes=[qT], join=True)

                    def score(t2, h):
                        sab = sab_[st_ % 2][t2]
                        bk = BANK[h % 2]
                        for c in range(2):
                            S.op("pe", mm(bk.ap[:, c * 128:(c + 1) * 128], qT[:, 2 * h + c, t2 * 128:(t2 + 1) * 128], skT[:, c, :], c == 0, c == 1), reads=[qT, skT], writes=[bk])
                        S.op("act", act(sab[:, h, :, :], bk.ap[:, 0:256].rearrange("p (c k) -> p c k", c=2), AF.Copy), reads=[bk], writes=[sab], join=True)

                    def chain(t2, h):
                        sab, thr, bia = sab_[st_ % 2][t2], thr_[st_ % 2][t2], bia_[st_ % 2][t2]
                        for c in range(2):
                            S.op("dve", lambda e: e.max(out=sv[:, c, 0:8], in_=sab[:, h, c, :]), reads=[sab], writes=[sv], join=True)
                            S.op("dve", lambda e: e.match_replace(out=tmpk[:], in_to_replace=sv[:, c, 0:8], in_values=sab[:, h, c, :], imm_value=-3e38), reads=[sab, sv], writes=[tmpk])
                            S.op("dve", lambda e: e.max(out=sv[:, c, 8:16], in_=tmpk[:]), reads=[tmpk], writes=[sv], join=True)
                        S.op("dve", tt(cand[:].rearrange("p (i j) -> p i j", i=16), sv[:, 0, :].unsqueeze(2).to_broadcast([128, 16, 16]),
                                       sv[:, 1, :].unsqueeze(1).to_broadcast([128, 16, 16]), ALU.add), reads=[sv], writes=[cand])
                        S.op("dve", lambda e: e.max(out=cv[:, 0:8], in_=cand[:]), reads=[cand], writes=[cv])
                        S.op("dve", lambda e: e.match_replace(out=ctmp[:], in_to_replace=cv[:, 0:8], in_values=cand[:], imm_value=-3e38), reads=[cand, cv], writes=[ctmp])
                        S.op("dve", lambda e: e.max(out=cv[:, 8:16], in_=ctmp[:]), reads=[ctmp], writes=[cv], join=True)
                        S.op("dve", ts(negm[:], cv[:, 0:1], -1.0, None, ALU.mult), reads=[cv], writes=[negm])
                        S.op("act", act(cex[:], cv[:], AF.Exp, bias=negm[:], accum_out=zz[:]), reads=[cv, negm], writes=[cex, zz])
                        S.op("act", act(zz[:], zz[:], AF.Ln), reads=[zz], writes=[zz])
                        S.op("dve", tt(bia[:, h:h + 1], negm[:], zz[:], ALU.subtract), reads=[negm, zz], writes=[bia], join=True)
                        S.op("dve", ts(thrm[:], cv[:, 15:16], -4e-6, None, ALU.add), reads=[cv], writes=[thrm])
                        sabi, rneg = sabi_[st_ % 2][t2], rneg_[st_ % 2][t2]
                        S.op("dve", ts(sabi[:, h, :], sab[:, h, 0, :], bia[:, h:h + 1], None, ALU.add), reads=[sab, bia], writes=[sabi], join=True)
                        S.op("dve", ts(rneg[:, h, :], sab[:, h, 0, :], negone[:], thrm[:], ALU.mult, ALU.add), reads=[sab, thrm, negone], writes=[rneg], join=True)
                    for hc in range(16):
                        tk["proj"].append(lambda hc=hc: proj(hc))
                    for t2 in range(2):
                        for h in range(8):
                            tk["score"].append(lambda t2=t2, h=h: score(t2, h))
                            tk["chain"].append(lambda t2=t2, h=h: chain(t2, h))
                    return tk

                def emit_ab(st_):
                    tk = ab_tasks(st_)
                    tk["dma"]()
                    for f in tk["proj"] + tk["score"] + tk["chain"]:
                        f()

                def emit_A(st_, ag):
                    x1T = x1T_[st_ % 2]
                    n = st_ * 32 + ag
                    uTg = uT_[n % 2]
                    S.dma("sp", uTg[:], uTv[:, :, ag * 512:(ag + 1) * 512], writes=[uTg])
                    for a4 in range(4):
                        bk = bA[a4 // 2]
                        for k in range(8):
                            S.op("pe", mm(psA[:, a4 * 256:(a4 + 1) * 256], uTg[:, k, a4 * 128:(a4 + 1) * 128], x1T[:, k, :], k == 0, k == 7), reads=[uTg, x1T], writes=[bk])

                def emit_gelu(st_, ag):
                    n = st_ * 32 + ag
                    abf = abf_[n % 2]
                    for hb in range(2):
                        S.op("act", act(abf[:, hb * 512:(hb + 1) * 512], bA[hb].ap, AF.Gelu_apprx_tanh), reads=[bA[hb]], writes=[abf], join=True)

                def emit_G(st_, ag, t2, first):
                    sab = sab_[st_ % 2][t2]
                    sabi, rneg = sabi_[st_ % 2][t2], rneg_[st_ % 2][t2]
                    for h in range(8):
                        for a4 in range(4):
                            g = cnt["g"]
                            cnt["g"] += 1
                            EE, GH = ee_[g % NR], gh_[g % NR]
                            a = ag * 4 + a4
                            S.op("act", act(EE[:], sab[:, h, 1, :], AF.Exp, bias=sabi[:, h, a:a + 1]), reads=[sab, sabi], writes=[EE])
                            S.op("dve", stt(GH[:], sab[:, h, 1, :], rneg[:, h, a:a + 1], EE[:], ALU.is_ge, ALU.mult), reads=[sab, rneg, EE], writes=[GH])
                            bk = bG[a4 // 2]
                            S.op("pe", mm(psG[:, a4 * 256 + t2 * 128:a4 * 256 + (t2 + 1) * 128], GH[:], identb[:], first[a4 // 2], (t2 == 1 and h == 7 and a4 % 2 == 1)),
                                 reads=[GH, identb], writes=[bk])
                            first[a4 // 2] = False

                def emit_HV(st_, ag):
                    n = st_ * 32 + ag
                    abf, hid, vg = abf_[n % 2], hid_[n % 2], vg_[n % 2]
                    for hb in range(2):
                        S.op("dve", tt(hid[:, 2 * hb:2 * hb + 2, :].rearrange("p a t -> p (a t)"), abf[:, hb * 512:(hb + 1) * 512], bG[hb].ap, ALU.mult),
                             reads=[abf, bG[hb]], writes=[hid], join=True)
                    for t2 in range(2):
                        for hf in range(2):
                            bk = bY[t2][hf]
                            for a4 in range(4):
                                S.op("pe", mm(bk.ap, hid[:, a4, t2 * 128:(t2 + 1) * 128], vg[:, a4, hf * 512:(hf + 1) * 512], ag == 0 and a4 == 0, ag == 31 and a4 == 3),
                                     reads=[hid, vg], writes=[bk])

                def emit_epi(st_):
                    for t2 in range(2):
                        tok = st_ * 256 + t2 * 128
                        xr, t, xo = xr_[t2], t_[t2], xo_[t2]
                        S.dma("sp", xr[:], x1_d[tok:tok + 128, :], writes=[xr])
                        for hf in range(2):
                            hs = slice(hf * 512, (hf + 1) * 512)
                            S.op("dve", stt(t[:, hs], xr[:, hs], ALPHA, bY[t2][hf].ap, ALU.mult, ALU.add), reads=[xr, bY[t2][hf]], writes=[t], join=True)
                        layer_norm((stats, mv, rs), t, gb, 0, 1024, xo)
                        S.dma("sp", dst[tok:tok + 128, :], xo[:], reads=[xo])

                emit_ab(0)
                emit_A(0, 0)
                emit_gelu(0, 0)
                nxt = None
                for n in range(512):
                    st_, ag = divmod(n, 32)
                    if st_ + 1 < 16:
                        if ag == 0:
                            nxt = ab_tasks(st_ + 1)
                        if ag == 12:
                            nxt["dma"]()
                            for f in nxt["proj"] + nxt["score"]:
                                f()
                        if 13 <= ag <= 28:
                            nxt["chain"][ag - 13]()
                    vg = vg_[n % 2]
                    S.dma("sp", vg[:], vb_d[ag * 512:(ag + 1) * 512, :].rearrange("(a p) d -> p a d", p=128), writes=[vg])
                    first = [True, True]
                    emit_G(st_, ag, 0, first)
                    if n + 1 < 512:
                        emit_A(*divmod(n + 1, 32))
                    emit_G(st_, ag, 1, first)
                    emit_HV(st_, ag)
                    if n + 1 < 512:
                        emit_gelu(*divmod(n + 1, 32))
                    if ag == 31:
                        emit_epi(st_)
                S.barrier()

        resid = x_in
        if "peer" in stages:
            conv_fill()
        for l in range(nlayers):
            last = (l == NL - 1)
            if "xT" in stages:
                stage_xT(resid, xT_d)
            if "lru" in stages:
                stage_lru(l)
            if "nsa" in stages:
                stage_nsa(l)
            if "out" in stages:
                stage_out(l, resid)
            if "peer" in stages:
                pump((2 - l) * 256 if l == 0 else 10 ** 6)
                stage_peer(l, y_out if last else resid_d)
            resid = resid_d
        S.barrier()
    return nc


def _consts():
    c = {}
    c["c_ident"] = np.eye(128, dtype=np.float32)
    t = np.arange(S_LEN)
    a_t = (t // 16).astype(np.float32)
    b_t = (t % 16).astype(np.float32)
    qpos = np.zeros((4, 8, S_LEN), np.float32)
    for h in range(8):
        sl = 2.0 ** (-(h + 1))
        qpos[0, h] = 16.0 * sl
        qpos[1, h] = sl
        qpos[2, h] = -sl * 16.0 * a_t
        qpos[3, h] = -sl * b_t
    c["c_qpos"] = qpos
    kpos = np.zeros((4, S_LEN), np.float32)
    kpos[0] = a_t
    kpos[1] = b_t
    kpos[2] = 1.0
    kpos[3] = 1.0
    c["c_kpos"] = kpos
    cc = np.arange(256)
    cpos = np.zeros((4, 256), np.float32)
    cpos[0] = cc + 1
    cpos[1] = 15.0
    cpos[2] = 1.0
    cpos[3] = 1.0
    c["c_cpos"] = cpos
    cs = np.arange(255)[:, None] * 16
    ss = np.arange(64)[None, :] * 64
    ov = np.clip(np.minimum(cs + 32, ss + 64) - np.maximum(cs, ss), 0, None)
    sm = np.zeros((256, 64), np.float32)
    sm[:255] = ov / 16.0
    c["c_selmat"] = np.ascontiguousarray(sm.reshape(2, 128, 64).transpose(1, 0, 2))
    fb = np.zeros((128, 32, 64), np.float32)
    for qi in range(32):
        tq = qi * 128 + np.arange(128)
        cur = tq // 64
        j = np.arange(64)[None, :]
        vblk = j <= cur[:, None]
        forced = (j == 0) | (j == cur[:, None]) | (j == cur[:, None] - 1)
        fb[:, qi, :] = np.where(vblk, np.where(forced, 1e4, 0.0), NEG)
    c["c_fb"] = fb
    ex = np.zeros((64, 32, 128), np.float32)
    for kj in range(32):
        ex[2 * kj, kj, 0:64] = 1.0
        ex[2 * kj + 1, kj, 64:128] = 1.0
    c["c_expand"] = ex
    return c


def _pack(inp):
    L = NL
    f = lambda k: np.asarray(inp[k], dtype=np.float32)
    b_in = f("b_in")
    pc = np.zeros((L, 128, NPC), np.float32)
    pr = np.zeros((L, 128, NPR), np.float32)

    def colpack(vec512):
        return vec512.reshape(4, 128).T
    for l in range(L):
        o = PC_OFF
        pc[l, :, o["bx"]:o["bx"] + 4] = colpack(b_in[l, 0:512])
        pc[l, :, o["bg"]:o["bg"] + 4] = colpack(b_in[l, 512:1024])
        cw = f("conv_w")[l]
        for j in range(4):
            pc[l, :, o["cw"] + j * 4:o["cw"] + j * 4 + 4] = colpack(cw[j])
        pc[l, :, o["cb"]:o["cb"] + 4] = colpack(f("conv_b")[l])
        pc[l, :, o["ba"]:o["ba"] + 4] = colpack(f("lru_ba")[l])
        pc[l, :, o["bxg"]:o["bxg"] + 4] = colpack(f("lru_bx")[l])
        pc[l, :, o["lam"]:o["lam"] + 4] = colpack(f("lru_lambda")[l])
        pc[l, :, o["gl"]:o["gl"] + 4] = colpack(f("gn_lru_g")[l])
        for h in range(8):
            pc[l, 0:64, o["bq"] + h] = b_in[l, 1024 + h * 64:1024 + (h + 1) * 64]
        for g in range(2):
            pc[l, 0:64, o["bkc"] + g] = b_in[l, 1536 + g * 64:1536 + (g + 1) * 64]
            pc[l, 0:64, o["bvc"] + g] = b_in[l, 1664 + g * 64:1664 + (g + 1) * 64]
            pc[l, 0:64, o["bks"] + g] = b_in[l, 1792 + g * 64:1792 + (g + 1) * 64]
            pc[l, 0:64, o["bkw"] + g] = b_in[l, 2048 + g * 64:2048 + (g + 1) * 64]
        pc[l, 0:64, o["kb1"]] = f("cmpk_b1")[l]
        pc[l, 0:64, o["kb2"]] = f("cmpk_b2")[l]
        pc[l, 0:64, o["vb1"]] = f("cmpv_b1")[l]
        r = PR_OFF
        pr[l, :, r["bvs"]:r["bvs"] + 128] = b_in[l, 1920:2048][None, :]
        pr[l, :, r["bvw"]:r["bvw"] + 128] = b_in[l, 2176:2304][None, :]
        pr[l, :, r["bgt"]:r["bgt"] + 24] = b_in[l, 2304:2328][None, :]
        pr[l, :, r["gn"]:r["gn"] + 512] = f("gn_nsa_g")[l][None, :]
        pr[l, :, r["l1g"]:r["l1g"] + 1024] = f("ln1_g")[l][None, :]
        pr[l, :, r["l1b"]:r["l1b"] + 1024] = f("ln1_b")[l][None, :]
        pr[l, :, r["l2g"]:r["l2g"] + 1024] = f("ln2_g")[l][None, :]
        pr[l, :, r["l2b"]:r["l2b"] + 1024] = f("ln2_b")[l][None, :]
    m = {"pc": pc, "pr": pr}
    wa = f("lru_wa")
    wx = f("lru_wx")
    wabd = np.zeros((L, 4, 128, 128), np.float32)
    wxbd = np.zeros((L, 4, 128, 128), np.float32)
    for l in range(L):
        for c in range(4):
            for j in range(2):
                wabd[l, c, j * 64:(j + 1) * 64, j * 64:(j + 1) * 64] = wa[l, 2 * c + j]
                wxbd[l, c, j * 64:(j + 1) * 64, j * 64:(j + 1) * 64] = wx[l, 2 * c + j]
    m["wa_bd"] = wabd
    m["wx_bd"] = wxbd
    m["w1k"] = np.ascontiguousarray(f("cmpk_w1").reshape(L, 32, 64, 64).transpose(0, 2, 1, 3))
    m["w1v"] = np.ascontiguousarray(f("cmpv_w1").reshape(L, 32, 64, 64).transpose(0, 2, 1, 3))
    m["w2k"] = f("cmpk_w2")
    m["w2va"] = np.ascontiguousarray(np.concatenate([f("cmpv_w2"), f("cmpv_b2")[:, None, :]], axis=1))
    pk = np.zeros((L, 64, 34), np.float32)
    pv = np.zeros((L, 64, 34), np.float32)
    pk[:, :, 0:32] = f("cmp_pos_k").transpose(0, 2, 1)
    pv[:, :, 0:32] = f("cmp_pos_v").transpose(0, 2, 1)
    m["posk"] = pk
    m["posv"] = pv
    m["w_in"] = f("w_in")
    m["w_out"] = f("w_out")
    m["wq"] = f("peer_wq")
    m["skT"] = np.ascontiguousarray(f("peer_subkeys").transpose(0, 3, 1, 2))
    m["uT"] = np.ascontiguousarray(f("peer_u").transpose(0, 2, 1))
    m["vtab"] = f("peer_v")
    m.update(_consts())
    return m


def kernel(**inputs):
    x = np.asarray(inputs["x"], dtype=np.float32)
    shared = _pack(inputs)
    nc = build()
    in_maps = []
    for b in range(8):
        d = dict(shared)
        d["x"] = np.ascontiguousarray(x[b])
        in_maps.append(d)
    res = run_bass_kernel_spmd(nc, in_maps, core_ids=list(range(8)))
    return np.stack([np.asarray(r["y"], dtype=np.float32) for r in res.results], axis=0)
```

```python
import numpy as np
from contextlib import ExitStack
import concourse.bass as bass
import concourse.mybir as mybir
from concourse.bass_utils import run_bass_kernel_spmd

F32 = mybir.dt.float32
BF16 = mybir.dt.bfloat16
AF = mybir.ActivationFunctionType
ALU = mybir.AluOpType

S_LEN = 4096
DM = 1024
NL = 2
ALPHA = (2 * NL) ** 0.25
EPS = 1e-5
NEG = -1e30

PC_SPEC = [("bx", 4), ("bg", 4), ("cw", 16), ("cb", 4), ("ba", 4), ("bxg", 4), ("lam", 4), ("gl", 4),
           ("bq", 8), ("bkc", 2), ("bvc", 2), ("bks", 2), ("bkw", 2), ("kb1", 1), ("kb2", 1), ("vb1", 1)]
PC_OFF = {}
_o = 0
for _n, _c in PC_SPEC:
    PC_OFF[_n] = _o
    _o += _c
NPC = _o
PR_SPEC = [("bvs", 128), ("bvw", 128), ("bgt", 24), ("gn", 512), ("l1g", 1024), ("l1b", 1024), ("l2g", 1024), ("l2b", 1024)]
PR_OFF = {}
_o = 0
for _n, _c in PR_SPEC:
    PR_OFF[_n] = _o
    _o += _c
NPR = _o


class Buf:
    __slots__ = ("w", "r")

    def __init__(self):
        self.w = {}
        self.r = {}


class T:
    def __init__(self, t):
        self.t = t
        self.b = Buf()

    def __getitem__(self, k):
        return self.t[k]


class V:
    def __init__(self, ap):
        self.ap = ap
        self.b = Buf()


class Sched:
    def __init__(self, nc, stack, ndma=32):
        self.nc = nc
        self.eng = {"pe": nc.tensor, "dve": nc.vector, "act": nc.scalar, "pool": nc.gpsimd, "sp": nc.sync}
        self.sem = {}
        self.cnt = {}
        for k in self.eng:
            self.sem[k] = stack.enter_context(nc.semaphore("s_" + k))
            self.cnt[k] = 0
        self.ndma = ndma
        for i in range(ndma):
            k = "d%d" % i
            self.sem[k] = stack.enter_context(nc.semaphore("s_" + k))
            self.cnt[k] = 0
        self.seen = {e: {} for e in self.eng}
        self.rr = 0

    def _deps(self, reads, writes, join):
        deps = {}

        def add(d):
            for s, v in d.items():
                if deps.get(s, 0) < v:
                    deps[s] = v
        for t in reads:
            add(t.b.w)
        for t in writes:
            if not join:
                add(t.b.w)
            add(t.b.r)
        return deps

    def _wait(self, e, deps):
        for s, v in deps.items():
            if s == "pe" and e == "pe":
                continue
            if self.seen[e].get(s, 0) >= v:
                continue
            self.eng[e].wait_ge(self.sem[s], v)
            self.seen[e][s] = v

    def _mark(self, tok, reads, writes, join):
        s, v = tok
        for t in reads:
            t.b.r[s] = v
        for t in writes:
            if join:
                t.b.w[s] = v
            else:
                t.b.w = {s: v}
                t.b.r = {}

    def op(self, e, fn, reads=(), writes=(), join=False):
        self._wait(e, self._deps(reads, writes, join))
        ins = fn(self.eng[e])
        ins.then_inc(self.sem[e], 1)
        self.cnt[e] += 1
        self._mark((e, self.cnt[e]), reads, writes, join)

    def dma(self, q, out, in_, reads=(), writes=(), join=False, **kw):
        k = "d%d" % self.rr
        self.rr = (self.rr + 1) % self.ndma
        deps = self._deps(reads, writes, join)
        if self.cnt[k] > 0:
            deps[k] = max(deps.get(k, 0), self.cnt[k])
        self._wait(q, deps)
        self.eng[q].dma_start(out=out, in_=in_, **kw).then_inc(self.sem[k], 16)
        self.cnt[k] += 16
        self._mark((k, self.cnt[k]), reads, writes, join)

    def barrier(self):
        for e in self.eng:
            for s, v in self.cnt.items():
                if v > 0 and self.seen[e].get(s, 0) < v:
                    self.eng[e].wait_ge(self.sem[s], v)
                    self.seen[e][s] = v


def build(nlayers=NL, debug=False, stages=("xT", "lru", "nsa", "out", "peer")):
    nc = bass.Bass("TRN2", target_bir_lowering=False)

    def din(name, shape, dt=F32):
        return nc.dram_tensor(name, list(shape), dt, kind="ExternalInput").ap()

    dbg = set(debug) if debug else set()

    def dscr(name, shape, dt=F32):
        return nc.dram_tensor(name, list(shape), dt, kind=("ExternalOutput" if name in dbg else "Internal")).ap()

    L = NL
    x_in = din("x", [S_LEN, DM])
    w_in = din("w_in", [L, DM, 2328])
    pc_d = din("pc", [L, 128, NPC])
    pr_d = din("pr", [L, 128, NPR])
    wa_bd = din("wa_bd", [L, 4, 128, 128])
    wx_bd = din("wx_bd", [L, 4, 128, 128])
    w1k_d = din("w1k", [L, 64, 32, 64])
    w1v_d = din("w1v", [L, 64, 32, 64])
    w2k_d = din("w2k", [L, 64, 64])
    w2va_d = din("w2va", [L, 65, 64])
    posk_d = din("posk", [L, 64, 34])
    posv_d = din("posv", [L, 64, 34])
    wout_d = din("w_out", [L, DM, DM])
    wq_d = din("wq", [L, DM, 2048])
    skT_d = din("skT", [L, 128, 2, 128])
    big = "peer" in stages
    uT_d = din("uT", [L, DM, 16384] if big else [L, 8, 8])
    v_d = din("vtab", [L, 16384, DM] if big else [L, 8, 8])
    c_ident = din("c_ident", [128, 128])
    c_qpos = din("c_qpos", [4, 8, S_LEN])
    c_kpos = din("c_kpos", [4, S_LEN])
    c_cpos = din("c_cpos", [4, 256])
    c_selmat = din("c_selmat", [128, 2, 64])
    c_fb = din("c_fb", [128, 32, 64])
    c_expand = din("c_expand", [64, 32, 128])
    y_out = nc.dram_tensor("y", [S_LEN, DM], F32, kind="ExternalOutput").ap()

    xT_d = dscr("xT_d", [DM, S_LEN], BF16)
    ylruT_d = dscr("ylruT_d", [512, S_LEN], BF16)
    ynsa_d = dscr("ynsa_d", [S_LEN, 512])
    x1_d = dscr("x1_d", [S_LEN, DM])
    x1T_d = dscr("x1T_d", [DM, S_LEN], BF16)
    resid_d = dscr("resid_d", [S_LEN, DM])
    uTb_l = [dscr("uTb_d%d" % i, [DM, 16384], BF16) for i in range(NL)]
    vb_l = [dscr("vb_d%d" % i, [16384, DM], BF16) for i in range(NL)]

    with ExitStack() as top:
        S = Sched(nc, top)

        uid = [0]

        def sb(stk, name, shape, dt=F32):
            uid[0] += 1
            return T(stk.enter_context(nc.sbuf_tensor("sb%d_%s" % (uid[0], name), list(shape), dt)))

        P2 = [top.enter_context(nc.psum_tensor("ps%d" % i, [128, 1024], F32)) for i in range(4)]
        BANK = [V(P2[i // 2][:, (i % 2) * 512:(i % 2 + 1) * 512]) for i in range(8)]

        def act(out, in_, func, bias=None, scale=None, accum_out=None):
            kw = {}
            if bias is not None:
                kw["bias"] = bias
            if scale is not None:
                kw["scale"] = scale
            if accum_out is not None:
                kw["accum_out"] = accum_out
            return lambda e: e.activation(out=out, in_=in_, func=func, **kw)

        def copy(out, in_):
            return lambda e: e.tensor_copy(out=out, in_=in_)

        def mm(out, lhsT, rhs, start, stop):
            return lambda e: e.matmul(out, lhsT, rhs, start=start, stop=stop)

        def tt(out, in0, in1, op):
            return lambda e: e.tensor_tensor(out=out, in0=in0, in1=in1, op=op)

        def ts(out, in0, s1, s2, op0, op1=None):
            if op1 is None:
                return lambda e: e.tensor_scalar(out=out, in0=in0, scalar1=s1, scalar2=None, op0=op0)
            return lambda e: e.tensor_scalar(out=out, in0=in0, scalar1=s1, scalar2=s2, op0=op0, op1=op1)

        def stt(out, in0, scalar, in1, op0, op1):
            return lambda e: e.scalar_tensor_tensor(out=out, in0=in0, scalar=scalar, in1=in1, op0=op0, op1=op1)

        def memset(ap, v):
            return lambda e: e.memset(ap, v)

        identf = sb(top, "identf", [128, 128])
        identb = sb(top, "identb", [128, 128], BF16)
        onesf = sb(top, "onesf", [128, 128])
        S.dma("sp", identf[:], c_ident, writes=[identf])
        S.op("dve", copy(identb[:], identf[:]), reads=[identf], writes=[identb])
        S.op("dve", memset(onesf[:], 1.0), writes=[onesf])

        def ldcast(dst_t, dst_ap, src_ap):
            S.dma("pool", dst_ap, src_ap, writes=[dst_t], join=True)

        def stage_xT(src, dst):
            with ExitStack() as stk:
                xin = [sb(stk, "xin%d" % i, [128, DM]) for i in range(2)]
                xo = [sb(stk, "xo%d" % i, [128, 8, 512], BF16) for i in range(2)]
                dstv = dst.rearrange("(k p) t -> p k t", p=128)
                for t_ in range(32):
                    xi = xin[t_ % 2]
                    S.dma("sp", xi[:], src[t_ * 128:(t_ + 1) * 128, :], writes=[xi])
                    g = t_ // 4
                    o = xo[g % 2]
                    j = t_ % 4
                    for hb in range(2):
                        bk = BANK[hb + 2 * (t_ % 2)]
                        for kk in range(4):
                            k = hb * 4 + kk
                            S.op("pe", lambda e: e.transpose(out=bk.ap[:, kk * 128:(kk + 1) * 128], in_=xi[:, k * 128:(k + 1) * 128], identity=identf[:]),
                                 reads=[xi, identf], writes=[bk])
                        S.op("dve" if hb == 0 else "act",
                             copy(o[:, hb * 4:(hb + 1) * 4, j * 128:(j + 1) * 128], bk.ap.rearrange("p (k t) -> p k t", k=4)) if hb == 0 else
                             act(o[:, hb * 4:(hb + 1) * 4, j * 128:(j + 1) * 128], bk.ap.rearrange("p (k t) -> p k t", k=4), AF.Copy),
                             reads=[bk], writes=[o], join=True)
                    if j == 3:
                        S.dma("sp", dstv[:, :, g * 512:(g + 1) * 512], o[:], reads=[o])
                    pump(4)
                S.barrier()

        def stage_lru(l):
            with ExitStack() as stk:
                xT = sb(stk, "xT", [128, 8, S_LEN], BF16)
                S.dma("sp", xT[:], xT_d.rearrange("(k p) t -> p k t", p=128), writes=[xT])
                wl = sb(stk, "wl", [128, 8, 1024], BF16)
                for k in range(8):
                    ldcast(wl, wl[:, k, :], w_in[l, k * 128:(k + 1) * 128, 0:1024])
                pc = sb(stk, "pc", [128, NPC])
                S.dma("sp", pc[:], pc_d[l], writes=[pc])
                wab = sb(stk, "wab", [128, 4, 128], BF16)
                wxb = sb(stk, "wxb", [128, 4, 128], BF16)
                for c in range(4):
                    ldcast(wab, wab[:, c, :], wa_bd[l, c])
                    ldcast(wxb, wxb[:, c, :], wx_bd[l, c])
                cch = sb(stk, "cch", [128, 4])
                cch2 = sb(stk, "cch2", [128, 4])
                tmpc = sb(stk, "tmpc", [128, 4])
                lam = pc[:, PC_OFF["lam"]:PC_OFF["lam"] + 4]
                S.op("act", act(tmpc[:], lam, AF.Exp, scale=-1.0), reads=[pc], writes=[tmpc])
                S.op("act", act(tmpc[:], tmpc[:], AF.Ln, bias=1.0), reads=[tmpc], writes=[tmpc])
                S.op("dve", ts(cch[:], tmpc[:], -8.0, None, ALU.mult), reads=[tmpc], writes=[cch])
                S.op("dve", ts(cch2[:], tmpc[:], -16.0, None, ALU.mult), reads=[tmpc], writes=[cch2])

                ylru = sb(stk, "ylru", [128, 4, S_LEN], BF16)
                ssq = sb(stk, "ssq", [128, S_LEN])

                def rot(name, shape, dt=F32, n=2):
                    return [sb(stk, "%s%d" % (name, i), shape, dt) for i in range(n)]
                xb_ = rot("xb", [128, 515])
                gg_ = rot("gg", [128, 512])
                xc_ = rot("xc", [128, 512])
                xcb_ = rot("xcb", [128, 512], BF16)
                r_ = rot("r", [128, 512])
                i_ = rot("i", [128, 512])
                a_ = rot("a", [128, 512])
                s_ = rot("s", [128, 512])
                u_ = rot("u", [128, 512])
                h_ = rot("h", [128, 512])
                y_ = rot("y", [128, 512])
                q_ = rot("ysq", [128, 512])
                it = 0
                for c in range(4):
                    def col(nm, j=0):
                        o = PC_OFF[nm] + j
                        return pc[:, o:o + 1]
                    for tb in range(8):
                        p = it % 2
                        xb, gg, xc, xcb, r, i, a, s, u, h, y, ysq = (xb_[p], gg_[p], xc_[p], xcb_[p], r_[p], i_[p], a_[p], s_[p], u_[p], h_[p], y_[p], q_[p])
                        xbp, hp = xb_[1 - p], h_[1 - p]
                        bx, bgk, br, bi = BANK[0 + 4 * p], BANK[1 + 4 * p], BANK[2 + 4 * p], BANK[3 + 4 * p]
                        tsl = slice(tb * 512, (tb + 1) * 512)
                        for k in range(8):
                            S.op("pe", mm(bx.ap, wl[:, k, c * 128:(c + 1) * 128], xT[:, k, tsl], k == 0, k == 7), reads=[wl, xT], writes=[bx])
                        for k in range(8):
                            S.op("pe", mm(bgk.ap, wl[:, k, 512 + c * 128:512 + (c + 1) * 128], xT[:, k, tsl], k == 0, k == 7), reads=[wl, xT], writes=[bgk])
                        if tb == 0:
                            S.op("dve", memset(xb[:, 0:3], 0.0), writes=[xb])
                        else:
                            S.op("dve", copy(xb[:, 0:3], xbp[:, 512:515]), reads=[xbp], writes=[xb])
                        S.op("act", act(xb[:, 3:515], bx.ap, AF.Identity, bias=col("bx", c)), reads=[bx, pc], writes=[xb], join=True)
                        S.op("act", act(gg[:], bgk.ap, AF.Gelu_apprx_tanh, bias=col("bg", c)), reads=[bgk, pc], writes=[gg])
                        S.op("dve", ts(xc[:], xb[:, 0:512], col("cw", 0 * 4 + c), col("cb", c), ALU.mult, ALU.add), reads=[xb, pc], writes=[xc])
                        for j in range(1, 4):
                            S.op("dve", stt(xc[:], xb[:, j:j + 512], col("cw", j * 4 + c), xc[:], ALU.mult, ALU.add), reads=[xb, pc, xc], writes=[xc])
                        S.op("pool", copy(xcb[:], xc[:]), reads=[xc], writes=[xcb])
                        S.op("pe", mm(br.ap, wab[:, c, :], xcb[:], True, True), reads=[wab, xcb], writes=[br])
                        S.op("pe", mm(bi.ap, wxb[:, c, :], xcb[:], True, True), reads=[wxb, xcb], writes=[bi])
                        S.op("act", act(r[:], br.ap, AF.Sigmoid, bias=col("ba", c)), reads=[br, pc], writes=[r])
                        S.op("act", act(i[:], bi.ap, AF.Sigmoid, bias=col("bxg", c)), reads=[bi, pc], writes=[i])
                        S.op("act", act(a[:], r[:], AF.Exp, scale=cch[:, c:c + 1]), reads=[r, cch], writes=[a])
                        S.op("act", act(s[:], r[:], AF.Exp, scale=cch2[:, c:c + 1]), reads=[r, cch2], writes=[s])
                        S.op("dve", ts(s[:], s[:], -1.0, 1.0, ALU.mult, ALU.add), reads=[s], writes=[s])
                        S.op("act", act(s[:], s[:], AF.Sqrt), reads=[s], writes=[s])
                        S.op("pool", tt(u[:], i[:], xc[:], ALU.mult), reads=[i, xc], writes=[u])
                        S.op("dve", tt(u[:], u[:], s[:], ALU.mult), reads=[u, s], writes=[u])
                        init = 0.0 if tb == 0 else hp[:, 511:512]
                        S.op("dve", lambda e: e.tensor_tensor_scan(out=h[:], data0=a[:], data1=u[:], initial=init, op0=ALU.mult, op1=ALU.add),
                             reads=[a, u] + ([] if tb == 0 else [hp]), writes=[h])
                        S.op("dve", tt(y[:], h[:], gg[:], ALU.mult), reads=[h, gg], writes=[y])
                        S.op("act", act(ysq[:], y[:], AF.Square), reads=[y], writes=[ysq])
                        S.op("pool", copy(ylru[:, c, tsl], y[:]), reads=[y], writes=[ylru], join=True)
                        S.op("pe", mm(bx.ap, onesf[:], ysq[:], True, True), reads=[onesf, ysq], writes=[bx])
                        if c == 0:
                            S.op("dve", copy(ssq[:, tsl], bx.ap), reads=[bx], writes=[ssq], join=True)
                        else:
                            S.op("dve", tt(ssq[:, tsl], ssq[:, tsl], bx.ap, ALU.add), reads=[bx, ssq], writes=[ssq])
                        it += 1
                S.op("dve", ts(ssq[:], ssq[:], 1.0 / 512, EPS, ALU.mult, ALU.add), reads=[ssq], writes=[ssq])
                S.op("act", act(ssq[:], ssq[:], AF.Sqrt), reads=[ssq], writes=[ssq])
                S.op("dve", lambda e: e.reciprocal(out=ssq[:], in_=ssq[:]), reads=[ssq], writes=[ssq])
                for c in range(4):
                    o = PC_OFF["gl"] + c
                    S.op("dve", stt(ylru[:, c, :], ylru[:, c, :], pc[:, o:o + 1], ssq[:], ALU.mult, ALU.mult), reads=[ylru, pc, ssq], writes=[ylru])
                S.dma("sp", ylruT_d.rearrange("(c p) t -> p c t", p=128), ylru[:], reads=[ylru])
                S.barrier()

        def stage_nsa(l):
            with ExitStack() as stk:
                xT = sb(stk, "xT", [128, 8, S_LEN], BF16)
                S.dma("sp", xT[:], xT_d.rearrange("(k p) t -> p k t", p=128), writes=[xT])
                wn = sb(stk, "wn", [128, 8, 1304], BF16)
                for k in range(8):
                    ldcast(wn, wn[:, k, 0:1024], w_in[l, k * 128:(k + 1) * 128, 1024:2048])
                    ldcast(wn, wn[:, k, 1024:1304], w_in[l, k * 128:(k + 1) * 128, 2048:2328])
                pc = sb(stk, "pc", [128, NPC])
                S.dma("sp", pc[:], pc_d[l], writes=[pc])
                prn = sb(stk, "prn", [128, 280])
                S.dma("sp", prn[:], pr_d[l, :, 0:280], writes=[prn])
                bq8 = sb(stk, "bq8", [128, 8])
                S.op("dve", ts(bq8[:], pc[:, PC_OFF["bq"]:PC_OFF["bq"] + 8], 0.125, None, ALU.mult), reads=[pc], writes=[bq8])
                fb = sb(stk, "fb", [128, 32, 64])
                S.dma("sp", fb[:], c_fb, writes=[fb])
                expd = sb(stk, "expd", [64, 32, 128], BF16)
                for j in range(4):
                    ldcast(expd, expd[:, j * 8:(j + 1) * 8, :], c_expand[:, j * 8:(j + 1) * 8, :])
                w1k = sb(stk, "w1k", [64, 32, 64], BF16)
                w1v = sb(stk, "w1v", [64, 32, 64], BF16)
                for j in range(2):
                    ldcast(w1k, w1k[:, j * 16:(j + 1) * 16, :], w1k_d[l, :, j * 16:(j + 1) * 16, :])
                    ldcast(w1v, w1v[:, j * 16:(j + 1) * 16, :], w1v_d[l, :, j * 16:(j + 1) * 16, :])
                posk = sb(stk, "posk", [64, 34], BF16)
                posv = sb(stk, "posv", [64, 34], BF16)
                ldcast(posk, posk[:], posk_d[l])
                ldcast(posv, posv[:], posv_d[l])
                w2k = sb(stk, "w2k", [64, 64], BF16)
                w2va = sb(stk, "w2va", [65, 64], BF16)
                ldcast(w2k, w2k[:], w2k_d[l])
                ldcast(w2va, w2va[:], w2va_d[l])

                qTa = sb(stk, "qTa", [68, 4, S_LEN], BF16)
                ksa = sb(stk, "ksa", [68, S_LEN], BF16)
                kwa = sb(stk, "kwa", [68, S_LEN], BF16)
                kca = sb(stk, "kca", [68, 256], BF16)
                kcr = sb(stk, "kcr", [64, S_LEN + 32], BF16)
                vcr = sb(stk, "vcr", [64, S_LEN + 32], BF16)
                vsa = sb(stk, "vsa", [128, 32, 65], BF16)
                vwa = sb(stk, "vwa", [128, 32, 65], BF16)
                vca = sb(stk, "vca", [128, 2, 129], BF16)
                gsig = sb(stk, "gsig", [128, 32, 12])
                h1k = sb(stk, "h1k", [65, 256], BF16)
                h1v = sb(stk, "h1v", [65, 256], BF16)
                b1k = sb(stk, "b1k", [64, 1])
                b1v = sb(stk, "b1v", [64, 1])
                es_ = [sb(stk, "es%d" % i, [128, 512], BF16) for i in range(4)]
                mT_ = [sb(stk, "mT%d" % i, [64, 128], BF16) for i in range(2)]
                impa = sb(stk, "impa", [128, 64])
                impt = sb(stk, "impt", [128, 64])
                selm = sb(stk, "selm", [128, 64])
                m8 = sb(stk, "m8", [128, 16])
                rd = sb(stk, "rd", [128, 3, 4])
                coef = sb(stk, "coef", [128, 3, 4])
                ya_ = [sb(stk, "ya%d" % i, [128, 256]) for i in range(2)]
                ytmp = sb(stk, "ytmp", [128, 256])
                gtmp = sb(stk, "gtmp", [128, 12])

                for g in range(2):
                    for cb in range(4):
                        csl = slice(cb * 1024, (cb + 1) * 1024)
                        ldcast(qTa, qTa[64:68, :, csl], c_qpos[:, 4 * g:4 * g + 4, csl])
                        ldcast(ksa, ksa[64:68, csl], c_kpos[:, csl])
                        ldcast(kwa, kwa[64:68, csl], c_kpos[:, csl])
                    ldcast(kca, kca[64:68, :], c_cpos)
                    S.op("pool", memset(kcr[:, S_LEN:S_LEN + 32], 0.0), writes=[kcr], join=True)
                    S.op("pool", memset(vcr[:, S_LEN:S_LEN + 32], 0.0), writes=[vcr], join=True)
                    S.op("pool", memset(vsa[:], 1.0), writes=[vsa])
                    S.op("pool", memset(vwa[:], 1.0), writes=[vwa])
                    S.op("pool", memset(vca[:], 1.0), writes=[vca])
                    ldcast(vca, vca[:, :, 65:129], c_selmat)
                    S.op("pool", memset(h1k[64:65, :], 1.0), writes=[h1k], join=True)
                    S.op("pool", memset(h1v[64:65, :], 1.0), writes=[h1v], join=True)
                    projs = []
                    for r4 in range(4):
                        hh = 4 * g + r4
                        projs.append((hh * 64, qTa, lambda sl, r4=r4: qTa[0:64, r4, sl], bq8[0:64, hh:hh + 1], 0.125, bq8))
                    projs.append((512 + g * 64, kcr, lambda sl: kcr[0:64, sl], pc[0:64, PC_OFF["bkc"] + g:PC_OFF["bkc"] + g + 1], 1.0, pc))
                    projs.append((640 + g * 64, vcr, lambda sl: vcr[0:64, sl], pc[0:64, PC_OFF["bvc"] + g:PC_OFF["bvc"] + g + 1], 1.0, pc))
                    projs.append((768 + g * 64, ksa, lambda sl: ksa[0:64, sl], pc[0:64, PC_OFF["bks"] + g:PC_OFF["bks"] + g + 1], 1.0, pc))
                    projs.append((1024 + g * 64, kwa, lambda sl: kwa[0:64, sl], pc[0:64, PC_OFF["bkw"] + g:PC_OFF["bkw"] + g + 1], 1.0, pc))
                    it = 0
                    for (coff, dt_, dfn, bias_ap, scl, bias_t) in projs:
                        for tb in range(8):
                            bk = BANK[it % 4]
                            it += 1
                            tsl = slice(tb * 512, (tb + 1) * 512)
                            for k in range(8):
                                S.op("pe", mm(bk.ap[0:64, :], wn[:, k, coff:coff + 64], xT[:, k, tsl], k == 0, k == 7), reads=[wn, xT], writes=[bk])
                            S.op("act", act(dfn(tsl), bk.ap[0:64, :], AF.Identity, bias=bias_ap, scale=scl), reads=[bk, bias_t], writes=[dt_], join=True)
                    for t_ in range(32):
                        tsl = slice(t_ * 128, (t_ + 1) * 128)
                        b1_, b2_, b3_ = (BANK[4], BANK[5], BANK[6]) if t_ % 2 == 0 else (BANK[7], BANK[0], BANK[1])
                        for k in range(8):
                            S.op("pe", mm(b1_.ap[:, 0:64], xT[:, k, tsl], wn[:, k, 896 + g * 64:896 + g * 64 + 64], k == 0, k == 7), reads=[wn, xT], writes=[b1_])
                        for k in range(8):
                            S.op("pe", mm(b2_.ap[:, 0:64], xT[:, k, tsl], wn[:, k, 1152 + g * 64:1152 + g * 64 + 64], k == 0, k == 7), reads=[wn, xT], writes=[b2_])
                        for k in range(8):
                            S.op("pe", mm(b3_.ap[:, 0:12], xT[:, k, tsl], wn[:, k, 1280 + g * 12:1280 + g * 12 + 12], k == 0, k == 7), reads=[wn, xT], writes=[b3_])
                        S.op("dve", tt(vsa[:, t_, 0:64], b1_.ap[:, 0:64], prn[:, g * 64:g * 64 + 64], ALU.add), reads=[b1_, prn], writes=[vsa], join=True)
                        S.op("dve", tt(vwa[:, t_, 0:64], b2_.ap[:, 0:64], prn[:, 128 + g * 64:128 + g * 64 + 64], ALU.add), reads=[b2_, prn], writes=[vwa], join=True)
                        S.op("dve", tt(gtmp[:], b3_.ap[:, 0:12], prn[:, 256 + g * 12:256 + g * 12 + 12], ALU.add), reads=[b3_, prn], writes=[gtmp])
                        S.op("act", act(gsig[:, t_, :], gtmp[:], AF.Sigmoid), reads=[gtmp], writes=[gsig], join=True)
                    for (raw, w1, pos, b1t, b1name, h1) in ((kcr, w1k, posk, b1k, "kb1", h1k), (vcr, w1v, posv, b1v, "vb1", h1v)):
                        bA, bB = BANK[0], BANK[1]
                        for l_ in range(32):
                            S.op("pe", mm(bA.ap[0:64, 0:256], w1[:, l_, :], raw[:, l_:l_ + 4096:16], l_ == 0, l_ == 31), reads=[w1, raw], writes=[bA])
                        for l_ in range(32):
                            S.op("pe", mm(bB.ap[0:64, 0:2], w1[:, l_, :], pos[:, l_:l_ + 2], l_ == 0, l_ == 31), reads=[w1, pos], writes=[bB])
                        o = PC_OFF[b1name]
                        S.op("dve", tt(b1t[:], bB.ap[0:64, 0:1], pc[0:64, o:o + 1], ALU.add), reads=[bB, pc], writes=[b1t])
                        S.op("act", act(h1[0:64, :], bA.ap[0:64, 0:256], AF.Gelu_apprx_tanh, bias=b1t[:]), reads=[bA, b1t], writes=[h1], join=True)
                    bA = BANK[2]
                    S.op("pe", mm(bA.ap[0:64, 0:256], w2k[:], h1k[0:64, :], True, True), reads=[w2k, h1k], writes=[bA])
                    o = PC_OFF["kb2"]
                    S.op("act", act(kca[0:64, :], bA.ap[0:64, 0:256], AF.Identity, bias=pc[0:64, o:o + 1]), reads=[bA, pc], writes=[kca], join=True)
                    for ch in range(2):
                        bB = BANK[3]
                        S.op("pe", mm(bB.ap[:, 0:64], h1v[0:65, ch * 128:(ch + 1) * 128], w2va[:], True, True), reads=[h1v, w2va], writes=[bB])
                        S.op("dve", copy(vca[:, ch, 0:64], bB.ap[:, 0:64]), reads=[bB], writes=[vca], join=True)

                    bOc, bI, bOs, bOw = BANK[2], BANK[3], BANK[5], BANK[6]
                    sbank = [BANK[0], BANK[1]]
                    mbank = [BANK[7], BANK[4]]
                    ctr = {"s": 0, "e": 0, "m": 0}

                    def mk_step(kind, qi, idx):
                        qsl = slice(qi * 128, (qi + 1) * 128)
                        rhsq = qTa[:, :, qsl]
                        nch = 1 if qi < 16 else 2
                        k0 = max(0, qi - 4)
                        st = {}

                        def s1():
                            bs = sbank[ctr["s"] % 2]
                            ctr["s"] += 1
                            es = es_[ctr["e"] % 4]
                            ctr["e"] += 1
                            st["es"] = es
                            esv = es[:].rearrange("p (r q) -> p r q", r=4)
                            if kind == "c":
                                ch = idx
                                S.op("pe", mm(bs.ap, kca[:, ch * 128:(ch + 1) * 128], rhsq, True, True), reads=[kca, qTa], writes=[bs])
                                S.op("act", act(es[:], bs.ap, AF.Exp), reads=[bs], writes=[es])
                                S.op("pool", lambda e: e.affine_select(out=esv, in_=esv, pattern=[[0, 4], [1, 128]], compare_op=ALU.is_ge, fill=0.0,
                                                                       base=128 * qi - 2048 * ch - 31, channel_multiplier=-16), reads=[es], writes=[es])
                            elif kind == "w":
                                kj = idx
                                S.op("pe", mm(bs.ap, kwa[:, kj * 128:(kj + 1) * 128], rhsq, True, True), reads=[kwa, qTa], writes=[bs])
                                S.op("act", act(es[:], bs.ap, AF.Exp), reads=[bs], writes=[es])
                                if kj == qi:
                                    S.op("pool", lambda e: e.affine_select(out=esv, in_=esv, pattern=[[0, 4], [1, 128]], compare_op=ALU.is_ge, fill=0.0,
                                                                           base=0, channel_multiplier=-1), reads=[es], writes=[es])
                                if kj == qi - 4:
                                    S.op("pool", lambda e: e.affine_select(out=esv, in_=esv, pattern=[[0, 4], [-1, 128]], compare_op=ALU.is_ge, fill=0.0,
                                                                           base=-1, channel_multiplier=1), reads=[es], writes=[es])
                            else:
                                kj = idx
                                mT = mT_[qi % 2]
                                bm = mbank[ctr["m"] % 2]
                                ctr["m"] += 1
                                S.op("pe", mm(bs.ap, ksa[:, kj * 128:(kj + 1) * 128], rhsq, True, True), reads=[ksa, qTa], writes=[bs])
                                S.op("act", act(es[:], bs.ap, AF.Exp), reads=[bs], writes=[es])
                                S.op("pe", mm(bm.ap[:, 0:128], expd[:, kj, :], mT[:], True, True), reads=[expd, mT], writes=[bm])
                                S.op("dve", tt(esv, esv, bm.ap[:, 0:128].unsqueeze(1).to_broadcast([128, 4, 128]), ALU.mult), reads=[es, bm], writes=[es])
                                if kj == qi:
                                    S.op("pool", lambda e: e.affine_select(out=esv, in_=esv, pattern=[[0, 4], [1, 128]], compare_op=ALU.is_ge, fill=0.0,
                                                                           base=0, channel_multiplier=-1), reads=[es], writes=[es])

                        def s2():
                            es = st["es"]
                            if kind == "c":
                                ch = idx
                                for r4 in range(4):
                                    S.op("pe", mm(bOc.ap[:, r4 * 65:(r4 + 1) * 65], es[:, r4 * 128:(r4 + 1) * 128], vca[:, ch, 0:65], ch == 0 and r4 == 0, ch == nch - 1 and r4 == 3),
                                         reads=[es, vca], writes=[bOc])
                                for r4 in range(4):
                                    S.op("pe", mm(bI.ap[:, r4 * 64:(r4 + 1) * 64], es[:, r4 * 128:(r4 + 1) * 128], vca[:, ch, 65:129], ch == 0 and r4 == 0, ch == nch - 1 and r4 == 3),
                                         reads=[es, vca], writes=[bI])
                                if ch == nch - 1:
                                    chain(qi)
                            elif kind == "w":
                                kj = idx
                                for r4 in range(4):
                                    S.op("pe", mm(bOw.ap[:, r4 * 65:(r4 + 1) * 65], es[:, r4 * 128:(r4 + 1) * 128], vwa[:, kj, :], kj == k0 and r4 == 0, kj == qi and r4 == 3),
                                         reads=[es, vwa], writes=[bOw])
                            else:
                                kj = idx
                                for r4 in range(4):
                                    S.op("pe", mm(bOs.ap[:, r4 * 65:(r4 + 1) * 65], es[:, r4 * 128:(r4 + 1) * 128], vsa[:, kj, :], kj == 0 and r4 == 0, kj == qi and r4 == 3),
                                         reads=[es, vsa], writes=[bOs])
                                if kj == qi:
                                    combine(qi)
                        return s1, s2

                    def chain(qi):
                        mT = mT_[qi % 2]
                        ocv = bOc.ap[:, 0:260].rearrange("p (r d) -> p r d", r=4)
                        S.op("dve", ts(rd[:, 0, :], ocv[:, :, 64], 1e-30, None, ALU.add), reads=[bOc], writes=[rd])
                        S.op("dve", lambda e: e.reciprocal(out=rd[:, 0, :], in_=rd[:, 0, :]), reads=[rd], writes=[rd])
                        S.op("dve", ts(impa[:], bI.ap[:, 0:64], rd[:, 0, 0:1], None, ALU.mult), reads=[bI, rd], writes=[impa])
                        for r4 in range(1, 4):
                            S.op("dve", stt(impa[:], bI.ap[:, r4 * 64:(r4 + 1) * 64], rd[:, 0, r4:r4 + 1], impa[:], ALU.mult, ALU.add), reads=[bI, rd, impa], writes=[impa])
                        S.op("dve", tt(impa[:], impa[:], fb[:, qi, :], ALU.add), reads=[impa, fb], writes=[impa])
                        S.op("dve", lambda e: e.max(out=m8[:, 0:8], in_=impa[:]), reads=[impa], writes=[m8])
                        S.op("dve", lambda e: e.match_replace(out=impt[:], in_to_replace=m8[:, 0:8], in_values=impa[:], imm_value=-3e38), reads=[impa, m8], writes=[impt])
                        S.op("dve", lambda e: e.max(out=m8[:, 8:16], in_=impt[:]), reads=[impt], writes=[m8])
                        S.op("dve", ts(selm[:], impa[:], m8[:, 15:16], None, ALU.is_ge), reads=[impa, m8], writes=[selm])
                        S.op("pe", lambda e: e.transpose(out=bI.ap[0:64, 0:128], in_=selm[:], identity=identf[:]), reads=[selm, identf], writes=[bI])
                        S.op("act", act(mT[:], bI.ap[0:64, 0:128], AF.Copy), reads=[bI], writes=[mT])

                    def combine(qi):
                        qsl = slice(qi * 128, (qi + 1) * 128)
                        ya = ya_[qi % 2]
                        ocv = bOc.ap[:, 0:260].rearrange("p (r d) -> p r d", r=4)
                        osv = bOs.ap[:, 0:260].rearrange("p (r d) -> p r d", r=4)
                        owv = bOw.ap[:, 0:260].rearrange("p (r d) -> p r d", r=4)
                        S.op("dve", ts(rd[:, 1, :], osv[:, :, 64], 1e-30, None, ALU.add), reads=[bOs], writes=[rd], join=True)
                        S.op("dve", ts(rd[:, 2, :], owv[:, :, 64], 1e-30, None, ALU.add), reads=[bOw], writes=[rd], join=True)
                        S.op("dve", lambda e: e.reciprocal(out=rd[:, 1:3, :], in_=rd[:, 1:3, :]), reads=[rd], writes=[rd])
                        S.op("dve", tt(coef[:], rd[:], gsig[:, qi, :].rearrange("p (r b) -> p b r", b=3), ALU.mult), reads=[rd, gsig], writes=[coef])
                        yav = ya[:].rearrange("p (r d) -> p r d", r=4)
                        ytv = ytmp[:].rearrange("p (r d) -> p r d", r=4)
                        S.op("dve", tt(yav, ocv[:, :, 0:64], coef[:, 0, :].unsqueeze(2).to_broadcast([128, 4, 64]), ALU.mult), reads=[bOc, coef], writes=[ya])
                        S.op("dve", tt(ytv, osv[:, :, 0:64], coef[:, 1, :].unsqueeze(2).to_broadcast([128, 4, 64]), ALU.mult), reads=[bOs, coef], writes=[ytmp])
                        S.op("dve", tt(ya[:], ya[:], ytmp[:], ALU.add), reads=[ya, ytmp], writes=[ya])
                        S.op("dve", tt(ytv, owv[:, :, 0:64], coef[:, 2, :].unsqueeze(2).to_broadcast([128, 4, 64]), ALU.mult), reads=[bOw, coef], writes=[ytmp])
                        S.op("dve", tt(ya[:], ya[:], ytmp[:], ALU.add), reads=[ya, ytmp], writes=[ya])
                        S.dma("sp", ynsa_d[qsl, g * 256:(g + 1) * 256], ya[:], reads=[ya])

                    steps = []
                    pump(64)
                    for qi in range(32):
                        nch = 1 if qi < 16 else 2
                        for ch in range(nch):
                            steps.append(mk_step("c", qi, ch))
                        for kj in range(max(0, qi - 4), qi + 1):
                            steps.append(mk_step("w", qi, kj))
                        for kj in range(qi + 1):
                            steps.append(mk_step("s", qi, kj))
                    steps[0][0]()
                    for i in range(len(steps)):
                        if i + 1 < len(steps):
                            steps[i + 1][0]()
                        steps[i][1]()
                S.barrier()

        def layer_norm(stk_tiles, t, gb, goff, boff, out):
            stats, mv, rs = stk_tiles
            S.op("dve", lambda e: e.bn_stats(out=stats[:, 0, :], in_=t[:, 0:512]), reads=[t], writes=[stats])
            S.op("dve", lambda e: e.bn_stats(out=stats[:, 1, :], in_=t[:, 512:1024]), reads=[t], writes=[stats], join=True)
            S.op("dve", lambda e: e.bn_aggr(out=mv[:], in_=stats[:].rearrange("p a b -> p (a b)")), reads=[stats], writes=[mv])
            S.op("dve", ts(rs[:], mv[:, 1:2], EPS, None, ALU.add), reads=[mv], writes=[rs])
            S.op("act", act(rs[:], rs[:], AF.Sqrt), reads=[rs], writes=[rs])
            S.op("dve", lambda e: e.reciprocal(out=rs[:], in_=rs[:]), reads=[rs], writes=[rs])
            S.op("dve", ts(out[:], t[:], mv[:, 0:1], rs[:], ALU.subtract, ALU.mult), reads=[t, mv, rs], writes=[out])
            S.op("pool", tt(out[:], out[:], gb[:, goff:goff + 1024], ALU.mult), reads=[out, gb], writes=[out])
            S.op("pool", tt(out[:], out[:], gb[:, boff:boff + 1024], ALU.add), reads=[out, gb], writes=[out])

        def stage_out(l, resid):
            with ExitStack() as stk:
                wo = sb(stk, "wo", [128, 8, 1024], BF16)
                for k in range(8):
                    ldcast(wo, wo[:, k, :], wout_d[l, k * 128:(k + 1) * 128, :])
                ylru = sb(stk, "ylru", [128, 4, S_LEN], BF16)
                S.dma("sp", ylru[:], ylruT_d.rearrange("(c p) t -> p c t", p=128), writes=[ylru])
                gb = sb(stk, "gb", [128, 2560])
                S.dma("sp", gb[:], pr_d[l, :, 280:2840], writes=[gb])
                yn_ = [sb(stk, "yn%d" % i, [128, 512]) for i in range(2)]
                ynn_ = [sb(stk, "ynn%d" % i, [128, 512]) for i in range(2)]
                nsaT_ = [sb(stk, "nsaT%d" % i, [128, 4, 128], BF16) for i in range(2)]
                xr_ = [sb(stk, "xr%d" % i, [128, 1024]) for i in range(2)]
                t_ = [sb(stk, "t%d" % i, [128, 1024]) for i in range(2)]
                x1_ = [sb(stk, "x1%d" % i, [128, 1024]) for i in range(2)]
                x1T_ = [sb(stk, "x1T%d" % i, [128, 8, 128], BF16) for i in range(2)]
                junk = sb(stk, "junk", [128, 512])
                ss_ = [sb(stk, "ss%d" % i, [128, 1]) for i in range(2)]
                stats = sb(stk, "stats", [128, 2, 6])
                mv = sb(stk, "mv", [128, 2])
                rs = sb(stk, "rs", [128, 1])
                x1Tv = x1T_d.rearrange("(k p) t -> p k t", p=128)
                def out_front(tt_):
                    p = tt_ % 2
                    yn, ynn, nsaT, xr, ss = yn_[p], ynn_[p], nsaT_[p], xr_[p], ss_[p]
                    tsl = slice(tt_ * 128, (tt_ + 1) * 128)
                    S.dma("sp", yn[:], ynsa_d[tsl, :], writes=[yn])
                    S.dma("sp", xr[:], resid[tsl, :], writes=[xr])
                    S.op("act", act(junk[:], yn[:], AF.Square, accum_out=ss[:]), reads=[yn], writes=[junk, ss])
                    S.op("dve", ts(ss[:], ss[:], 1.0 / 512, EPS, ALU.mult, ALU.add), reads=[ss], writes=[ss])
                    S.op("act", act(ss[:], ss[:], AF.Sqrt), reads=[ss], writes=[ss])
                    S.op("dve", lambda e: e.reciprocal(out=ss[:], in_=ss[:]), reads=[ss], writes=[ss])
                    S.op("dve", stt(ynn[:], yn[:], ss[:], gb[:, 0:512], ALU.mult, ALU.mult), reads=[yn, ss, gb], writes=[ynn])
                    bT = BANK[4 + p]
                    for k in range(4):
                        S.op("pe", lambda e: e.transpose(out=bT.ap[:, k * 128:(k + 1) * 128], in_=ynn[:, k * 128:(k + 1) * 128], identity=identf[:]),
                             reads=[ynn, identf], writes=[bT])
                    S.op("act", act(nsaT[:], bT.ap.rearrange("p (k t) -> p k t", k=4), AF.Copy), reads=[bT], writes=[nsaT])
                    for hf in range(2):
                        bk = BANK[hf + 2 * p]
                        hs = slice(hf * 512, (hf + 1) * 512)
                        for k in range(4):
                            S.op("pe", mm(bk.ap, ylru[:, k, tsl], wo[:, k, hs], k == 0, False), reads=[ylru, wo], writes=[bk])
                        for k in range(4):
                            S.op("pe", mm(bk.ap, nsaT[:, k, :], wo[:, 4 + k, hs], False, k == 3), reads=[nsaT, wo], writes=[bk])

                def out_back(tt_):
                    p = tt_ % 2
                    xr, t, x1, x1T = xr_[p], t_[p], x1_[p], x1T_[p]
                    tsl = slice(tt_ * 128, (tt_ + 1) * 128)
                    for hf in range(2):
                        bk = BANK[hf + 2 * p]
                        hs = slice(hf * 512, (hf + 1) * 512)
                        S.op("dve", stt(t[:, hs], xr[:, hs], ALPHA, bk.ap, ALU.mult, ALU.add), reads=[xr, bk], writes=[t], join=True)
                    layer_norm((stats, mv, rs), t, gb, 512, 1536, x1)
                    S.dma("sp", x1_d[tsl, :], x1[:], reads=[x1])
                    for hb in range(2):
                        bk = BANK[6 + hb]
                        for kk in range(4):
                            k = hb * 4 + kk
                            S.op("pe", lambda e: e.transpose(out=bk.ap[:, kk * 128:(kk + 1) * 128], in_=x1[:, k * 128:(k + 1) * 128], identity=identf[:]),
                                 reads=[x1, identf], writes=[bk])
                        S.op("act", act(x1T[:, hb * 4:(hb + 1) * 4, :], bk.ap.rearrange("p (k t) -> p k t", k=4), AF.Copy), reads=[bk], writes=[x1T], join=True)
                    S.dma("sp", x1Tv[:, :, tsl], x1T[:], reads=[x1T])

                out_front(0)
                for tt_ in range(32):
                    if tt_ + 1 < 32:
                        out_front(tt_ + 1)
                    out_back(tt_)
                S.barrier()

        conv_q = []

        def conv_fill():
            for l in range(nlayers):
                for kp in range(8):
                    for cb in range(16):
                        conv_q.append((uTb_l[l][kp * 128:(kp + 1) * 128, cb * 1024:(cb + 1) * 1024], uT_d[l, kp * 128:(kp + 1) * 128, cb * 1024:(cb + 1) * 1024]))
                for rb in range(128):
                    conv_q.append((vb_l[l][rb * 128:(rb + 1) * 128, :], v_d[l, rb * 128:(rb + 1) * 128, :]))

        def pump(n):
            for _ in range(n):
                if not conv_q:
                    return
                dst, src = conv_q.pop(0)
                S.dma("pool", dst, src)

        def stage_peer(l, dst):
            with ExitStack() as stk:
                wq = sb(stk, "wq", [128, 8, 2048], BF16)
                for k in range(8):
                    ldcast(wq, wq[:, k, 0:1024], wq_d[l, k * 128:(k + 1) * 128, 0:1024])
                    ldcast(wq, wq[:, k, 1024:2048], wq_d[l, k * 128:(k + 1) * 128, 1024:2048])
                skT = sb(stk, "skT", [128, 2, 128], BF16)
                ldcast(skT, skT[:], skT_d[l])
                gb = sb(stk, "gb", [128, 2048])
                S.dma("sp", gb[:], pr_d[l, :, 2840:4888], writes=[gb])
                x1T_ = [sb(stk, "x1T%d" % i, [128, 8, 256], BF16) for i in range(2)]
                qT = sb(stk, "qT", [128, 16, 256], BF16)
                sab_ = [[sb(stk, "sab%d_%d" % (j, i), [128, 8, 2, 128]) for i in range(2)] for j in range(2)]
                thr_ = [[sb(stk, "thr%d_%d" % (j, i), [128, 8]) for i in range(2)] for j in range(2)]
                bia_ = [[sb(stk, "bia%d_%d" % (j, i), [128, 8]) for i in range(2)] for j in range(2)]
                sv = sb(stk, "sv", [128, 2, 16])
                tmpk = sb(stk, "tmpk", [128, 128])
                cand = sb(stk, "cand", [128, 256])
                ctmp = sb(stk, "ctmp", [128, 256])
                cv = sb(stk, "cv", [128, 16])
                cex = sb(stk, "cex", [128, 16])
                negm = sb(stk, "negm", [128, 1])
                zz = sb(stk, "zz", [128, 1])
                uT_ = [sb(stk, "uTg%d" % i, [128, 8, 512], BF16) for i in range(2)]
                vg_ = [sb(stk, "vg%d" % i, [128, 4, 1024], BF16) for i in range(2)]
                abf_ = [sb(stk, "abf%d" % i, [128, 1024], BF16) for i in range(2)]
                hid_ = [sb(stk, "hid%d" % i, [128, 4, 256], BF16) for i in range(2)]
                NR = 48
                ee_ = [sb(stk, "EE%d" % i, [128, 128], BF16) for i in range(NR)]
                gh_ = [sb(stk, "GH%d" % i, [128, 128], BF16) for i in range(NR)]
                sabi_ = [[sb(stk, "sabi%d_%d" % (j, i), [128, 8, 128]) for i in range(2)] for j in range(2)]
                rneg_ = [[sb(stk, "rneg%d_%d" % (j, i), [128, 8, 128]) for i in range(2)] for j in range(2)]
                thrm = sb(stk, "thrm", [128, 1])
                negone = sb(stk, "negone", [128, 1])
                S.op("dve", memset(negone[:], -1.0), writes=[negone])
                xr_ = [sb(stk, "xr%d" % i, [128, 1024]) for i in range(1)] * 2
                t_ = [sb(stk, "t%d" % i, [128, 1024]) for i in range(1)] * 2
                xo_ = [sb(stk, "xo%d" % i, [128, 1024]) for i in range(1)] * 2
                stats = sb(stk, "stats", [128, 2, 6])
                mv = sb(stk, "mv", [128, 2])
                rs = sb(stk, "rs", [128, 1])
                x1Tv = x1T_d.rearrange("(k p) t -> p k t", p=128)
                uTb_d, vb_d = uTb_l[l], vb_l[l]
                uTv = uTb_d.rearrange("(k p) e -> p k e", p=128)
                bA = [BANK[0], BANK[1]]
                bG = [BANK[2], BANK[3]]
                bY = [[BANK[4], BANK[5]], [BANK[6], BANK[7]]]
                psA = P2[0]
                psG = P2[1]
                cnt = {"g": 0, "w": 0, "gh": 0}

                def ab_tasks(st_):
                    x1T = x1T_[st_ % 2]
                    tk = {"proj": [], "score": [], "chain": []}
                    tk["dma"] = lambda: S.dma("sp", x1T[:], x1Tv[:, :, st_ * 256:(st_ + 1) * 256], writes=[x1T])

                    def proj(hc):
                        bk = BANK[hc % 2]
                        for k in range(8):
                            S.op("pe", mm(bk.ap[:, 0:256], wq[:, k, hc * 128:(hc + 1) * 128], x1T[:, k, :], k == 0, k == 7), reads=[wq, x1T], writes=[bk])
                        S.op("act", act(qT[:, hc, :], bk.ap[:, 0:256], AF.Copy), reads=[bk], writes=[qT], join=True)

                    def score(t2, h):
                        sab = sab_[st_ % 2][t2]
                        bk = BANK[h % 2]
                        for c in range(2):
                            S.op("pe", mm(bk.ap[:, c * 128:(c + 1) * 128], qT[:, 2 * h + c, t2 * 128:(t2 + 1) * 128], skT[:, c, :], c == 0, c == 1), reads=[qT, skT], writes=[bk])
                        S.op("act", act(sab[:, h, :, :], bk.ap[:, 0:256].rearrange("p (c k) -> p c k", c=2), AF.Copy), reads=[bk], writes=[sab], join=True)

                    def chain(t2, h):
                        sab, thr, bia = sab_[st_ % 2][t2], thr_[st_ % 2][t2], bia_[st_ % 2][t2]
                        for c in range(2):
                            S.op("dve", lambda e: e.max(out=sv[:, c, 0:8], in_=sab[:, h, c, :]), reads=[sab], writes=[sv], join=True)
                            S.op("dve", lambda e: e.match_replace(out=tmpk[:], in_to_replace=sv[:, c, 0:8], in_values=sab[:, h, c, :], imm_value=-3e38), reads=[sab, sv], writes=[tmpk])
                            S.op("dve", lambda e: e.max(out=sv[:, c, 8:16], in_=tmpk[:]), reads=[tmpk], writes=[sv], join=True)
                        S.op("dve", tt(cand[:].rearrange("p (i j) -> p i j", i=16), sv[:, 0, :].unsqueeze(2).to_broadcast([128, 16, 16]),
                                       sv[:, 1, :].unsqueeze(1).to_broadcast([128, 16, 16]), ALU.add), reads=[sv], writes=[cand])
                        S.op("dve", lambda e: e.max(out=cv[:, 0:8], in_=cand[:]), reads=[cand], writes=[cv])
                        S.op("dve", lambda e: e.match_replace(out=ctmp[:], in_to_replace=cv[:, 0:8], in_values=cand[:], imm_value=-3e38), reads=[cand, cv], writes=[ctmp])
                        S.op("dve", lambda e: e.max(out=cv[:, 8:16], in_=ctmp[:]), reads=[ctmp], writes=[cv], join=True)
                        S.op("dve", ts(negm[:], cv[:, 0:1], -1.0, None, ALU.mult), reads=[cv], writes=[negm])
                        S.op("act", act(cex[:], cv[:], AF.Exp, bias=negm[:], accum_out=zz[:]), reads=[cv, negm], writes=[cex, zz])
                        S.op("act", act(zz[:], zz[:], AF.Ln), reads=[zz], writes=[zz])
                        S.op("dve", tt(bia[:, h:h + 1], negm[:], zz[:], ALU.subtract), reads=[negm, zz], writes=[bia], join=True)
                        S.op("dve", ts(thrm[:], cv[:, 15:16], -4e-6, None, ALU.add), reads=[cv], writes=[thrm])
                        sabi, rneg = sabi_[st_ % 2][t2], rneg_[st_ % 2][t2]
                        S.op("dve", ts(sabi[:, h, :], sab[:, h, 0, :], bia[:, h:h + 1], None, ALU.add), reads=[sab, bia], writes=[sabi], join=True)
                        S.op("dve", ts(rneg[:, h, :], sab[:, h, 0, :], negone[:], thrm[:], ALU.mult, ALU.add), reads=[sab, thrm, negone], writes=[rneg], join=True)
                    for hc in range(16):
                        tk["proj"].append(lambda hc=hc: proj(hc))
                    for t2 in range(2):
                        for h in range(8):
                            tk["score"].append(lambda t2=t2, h=h: score(t2, h))
                            tk["chain"].append(lambda t2=t2, h=h: chain(t2, h))
                    return tk

                def emit_ab(st_):
                    tk = ab_tasks(st_)
                    tk["dma"]()
                    for f in tk["proj"] + tk["score"] + tk["chain"]:
                        f()

                def emit_A(st_, ag):
                    x1T = x1T_[st_ % 2]
                    n = st_ * 32 + ag
                    uTg = uT_[n % 2]
                    S.dma("sp", uTg[:], uTv[:, :, ag * 512:(ag + 1) * 512], writes=[uTg])
                    for a4 in range(4):
                        bk = bA[a4 // 2]
                        for k in range(8):
                            S.op("pe", mm(psA[:, a4 * 256:(a4 + 1) * 256], uTg[:, k, a4 * 128:(a4 + 1) * 128], x1T[:, k, :], k == 0, k == 7), reads=[uTg, x1T], writes=[bk])

                def emit_gelu(st_, ag):
                    n = st_ * 32 + ag
                    abf = abf_[n % 2]
                    for hb in range(2):
                        S.op("act", act(abf[:, hb * 512:(hb + 1) * 512], bA[hb].ap, AF.Gelu_apprx_tanh), reads=[bA[hb]], writes=[abf], join=True)

                def emit_G(st_, ag, t2, first):
                    sab = sab_[st_ % 2][t2]
                    sabi, rneg = sabi_[st_ % 2][t2], rneg_[st_ % 2][t2]
                    for h in range(8):
                        for a4 in range(4):
                            g = cnt["g"]
                            cnt["g"] += 1
                            EE, GH = ee_[g % NR], gh_[g % NR]
                            a = ag * 4 + a4
                            S.op("act", act(EE[:], sab[:, h, 1, :], AF.Exp, bias=sabi[:, h, a:a + 1]), reads=[sab, sabi], writes=[EE])
                            S.op("dve", stt(GH[:], sab[:, h, 1, :], rneg[:, h, a:a + 1], EE[:], ALU.is_ge, ALU.mult), reads=[sab, rneg, EE], writes=[GH])
                            bk = bG[a4 // 2]
                            S.op("pe", mm(psG[:, a4 * 256 + t2 * 128:a4 * 256 + (t2 + 1) * 128], GH[:], identb[:], first[a4 // 2], (t2 == 1 and h == 7 and a4 % 2 == 1)),
                                 reads=[GH, identb], writes=[bk])
                            first[a4 // 2] = False

                def emit_HV(st_, ag):
                    n = st_ * 32 + ag
                    abf, hid, vg = abf_[n % 2], hid_[n % 2], vg_[n % 2]
                    for hb in range(2):
                        S.op("dve", tt(hid[:, 2 * hb:2 * hb + 2, :].rearrange("p a t -> p (a t)"), abf[:, hb * 512:(hb + 1) * 512], bG[hb].ap, ALU.mult),
                             reads=[abf, bG[hb]], writes=[hid], join=True)
                    for t2 in range(2):
                        for hf in range(2):
                            bk = bY[t2][hf]
                            for a4 in range(4):
                                S.op("pe", mm(bk.ap, hid[:, a4, t2 * 128:(t2 + 1) * 128], vg[:, a4, hf * 512:(hf + 1) * 512], ag == 0 and a4 == 0, ag == 31 and a4 == 3),
                                     reads=[hid, vg], writes=[bk])

                def emit_epi(st_):
                    for t2 in range(2):
                        tok = st_ * 256 + t2 * 128
                        xr, t, xo = xr_[t2], t_[t2], xo_[t2]
                        S.dma("sp", xr[:], x1_d[tok:tok + 128, :], writes=[xr])
                        for hf in range(2):
                            hs = slice(hf * 512, (hf + 1) * 512)
                            S.op("dve", stt(t[:, hs], xr[:, hs], ALPHA, bY[t2][hf].ap, ALU.mult, ALU.add), reads=[xr, bY[t2][hf]], writes=[t], join=True)
                        layer_norm((stats, mv, rs), t, gb, 0, 1024, xo)
                        S.dma("sp", dst[tok:tok + 128, :], xo[:], reads=[xo])

                emit_ab(0)
                emit_A(0, 0)
                emit_gelu(0, 0)
                nxt = None
                for n in range(512):
                    st_, ag = divmod(n, 32)
                    if st_ + 1 < 16:
                        if ag == 0:
                            nxt = ab_tasks(st_ + 1)
                        if ag == 12:
                            nxt["dma"]()
                            for f in nxt["proj"] + nxt["score"]:
                                f()
                        if 13 <= ag <= 28:
                            nxt["chain"][ag - 13]()
                    vg = vg_[n % 2]
                    S.dma("sp", vg[:], vb_d[ag * 512:(ag + 1) * 512, :].rearrange("(a p) d -> p a d", p=128), writes=[vg])
                    first = [True, True]
                    emit_G(st_, ag, 0, first)
                    if n + 1 < 512:
                        emit_A(*divmod(n + 1, 32))
                    emit_G(st_, ag, 1, first)
                    emit_HV(st_, ag)
                    if n + 1 < 512:
                        emit_gelu(*divmod(n + 1, 32))
                    if ag == 31:
                        emit_epi(st_)
                S.barrier()

        resid = x_in
        if "peer" in stages:
            conv_fill()
        for l in range(nlayers):
            last = (l == NL - 1)
            if "xT" in stages:
                stage_xT(resid, xT_d)
            if "lru" in stages:
                stage_lru(l)
            if "nsa" in stages:
                stage_nsa(l)
            if "out" in stages:
                stage_out(l, resid)
            if "peer" in stages:
                pump((2 - l) * 256 if l == 0 else 10 ** 6)
                stage_peer(l, y_out if last else resid_d)
            resid = resid_d
        S.barrier()
    return nc


def _consts():
    c = {}
    c["c_ident"] = np.eye(128, dtype=np.float32)
    t = np.arange(S_LEN)
    a_t = (t // 16).astype(np.float32)
    b_t = (t % 16).astype(np.float32)
    qpos = np.zeros((4, 8, S_LEN), np.float32)
    for h in range(8):
        sl = 2.0 ** (-(h + 1))
        qpos[0, h] = 16.0 * sl
        qpos[1, h] = sl
        qpos[2, h] = -sl * 16.0 * a_t
        qpos[3, h] = -sl * b_t
    c["c_qpos"] = qpos
    kpos = np.zeros((4, S_LEN), np.float32)
    kpos[0] = a_t
    kpos[1] = b_t
    kpos[2] = 1.0
    kpos[3] = 1.0
    c["c_kpos"] = kpos
    cc = np.arange(256)
    cpos = np.zeros((4, 256), np.float32)
    cpos[0] = cc + 1
    cpos[1] = 15.0
    cpos[2] = 1.0
    cpos[3] = 1.0
    c["c_cpos"] = cpos
    cs = np.arange(255)[:, None] * 16
    ss = np.arange(64)[None, :] * 64
    ov = np.clip(np.minimum(cs + 32, ss + 64) - np.maximum(cs, ss), 0, None)
    sm = np.zeros((256, 64), np.float32)
    sm[:255] = ov / 16.0
    c["c_selmat"] = np.ascontiguousarray(sm.reshape(2, 128, 64).transpose(1, 0, 2))
    fb = np.zeros((128, 32, 64), np.float32)
    for qi in range(32):
        tq = qi * 128 + np.arange(128)
        cur = tq // 64
        j = np.arange(64)[None, :]
        vblk = j <= cur[:, None]
        forced = (j == 0) | (j == cur[:, None]) | (j == cur[:, None] - 1)
        fb[:, qi, :] = np.where(vblk, np.where(forced, 1e4, 0.0), NEG)
    c["c_fb"] = fb
    ex = np.zeros((64, 32, 128), np.float32)
    for kj in range(32):
        ex[2 * kj, kj, 0:64] = 1.0
        ex[2 * kj + 1, kj, 64:128] = 1.0
    c["c_expand"] = ex
    return c


def _pack(inp):
    L = NL
    f = lambda k: np.asarray(inp[k], dtype=np.float32)
    b_in = f("b_in")
    pc = np.zeros((L, 128, NPC), np.float32)
    pr = np.zeros((L, 128, NPR), np.float32)

    def colpack(vec512):
        return vec512.reshape(4, 128).T
    for l in range(L):
        o = PC_OFF
        pc[l, :, o["bx"]:o["bx"] + 4] = colpack(b_in[l, 0:512])
        pc[l, :, o["bg"]:o["bg"] + 4] = colpack(b_in[l, 512:1024])
        cw = f("conv_w")[l]
        for j in range(4):
            pc[l, :, o["cw"] + j * 4:o["cw"] + j * 4 + 4] = colpack(cw[j])
        pc[l, :, o["cb"]:o["cb"] + 4] = colpack(f("conv_b")[l])
        pc[l, :, o["ba"]:o["ba"] + 4] = colpack(f("lru_ba")[l])
        pc[l, :, o["bxg"]:o["bxg"] + 4] = colpack(f("lru_bx")[l])
        pc[l, :, o["lam"]:o["lam"] + 4] = colpack(f("lru_lambda")[l])
        pc[l, :, o["gl"]:o["gl"] + 4] = colpack(f("gn_lru_g")[l])
        for h in range(8):
            pc[l, 0:64, o["bq"] + h] = b_in[l, 1024 + h * 64:1024 + (h + 1) * 64]
        for g in range(2):
            pc[l, 0:64, o["bkc"] + g] = b_in[l, 1536 + g * 64:1536 + (g + 1) * 64]
            pc[l, 0:64, o["bvc"] + g] = b_in[l, 1664 + g * 64:1664 + (g + 1) * 64]
            pc[l, 0:64, o["bks"] + g] = b_in[l, 1792 + g * 64:1792 + (g + 1) * 64]
            pc[l, 0:64, o["bkw"] + g] = b_in[l, 2048 + g * 64:2048 + (g + 1) * 64]
        pc[l, 0:64, o["kb1"]] = f("cmpk_b1")[l]
        pc[l, 0:64, o["kb2"]] = f("cmpk_b2")[l]
        pc[l, 0:64, o["vb1"]] = f("cmpv_b1")[l]
        r = PR_OFF
        pr[l, :, r["bvs"]:r["bvs"] + 128] = b_in[l, 1920:2048][None, :]
        pr[l, :, r["bvw"]:r["bvw"] + 128] = b_in[l, 2176:2304][None, :]
        pr[l, :, r["bgt"]:r["bgt"] + 24] = b_in[l, 2304:2328][None, :]
        pr[l, :, r["gn"]:r["gn"] + 512] = f("gn_nsa_g")[l][None, :]
        pr[l, :, r["l1g"]:r["l1g"] + 1024] = f("ln1_g")[l][None, :]
        pr[l, :, r["l1b"]:r["l1b"] + 1024] = f("ln1_b")[l][None, :]
        pr[l, :, r["l2g"]:r["l2g"] + 1024] = f("ln2_g")[l][None, :]
        pr[l, :, r["l2b"]:r["l2b"] + 1024] = f("ln2_b")[l][None, :]
    m = {"pc": pc, "pr": pr}
    wa = f("lru_wa")
    wx = f("lru_wx")
    wabd = np.zeros((L, 4, 128, 128), np.float32)
    wxbd = np.zeros((L, 4, 128, 128), np.float32)
    for l in range(L):
        for c in range(4):
            for j in range(2):
                wabd[l, c, j * 64:(j + 1) * 64, j * 64:(j + 1) * 64] = wa[l, 2 * c + j]
                wxbd[l, c, j * 64:(j + 1) * 64, j * 64:(j + 1) * 64] = wx[l, 2 * c + j]
    m["wa_bd"] = wabd
    m["wx_bd"] = wxbd
    m["w1k"] = np.ascontiguousarray(f("cmpk_w1").reshape(L, 32, 64, 64).transpose(0, 2, 1, 3))
    m["w1v"] = np.ascontiguousarray(f("cmpv_w1").reshape(L, 32, 64, 64).transpose(0, 2, 1, 3))
    m["w2k"] = f("cmpk_w2")
    m["w2va"] = np.ascontiguousarray(np.concatenate([f("cmpv_w2"), f("cmpv_b2")[:, None, :]], axis=1))
    pk = np.zeros((L, 64, 34), np.float32)
    pv = np.zeros((L, 64, 34), np.float32)
    pk[:, :, 0:32] = f("cmp_pos_k").transpose(0, 2, 1)
    pv[:, :, 0:32] = f("cmp_pos_v").transpose(0, 2, 1)
    m["posk"] = pk
    m["posv"] = pv
    m["w_in"] = f("w_in")
    m["w_out"] = f("w_out")
    m["wq"] = f("peer_wq")
    m["skT"] = np.ascontiguousarray(f("peer_subkeys").transpose(0, 3, 1, 2))
    m["uT"] = np.ascontiguousarray(f("peer_u").transpose(0, 2, 1))
    m["vtab"] = f("peer_v")
    m.update(_consts())
    return m


def kernel(**inputs):
    x = np.asarray(inputs["x"], dtype=np.float32)
    shared = _pack(inputs)
    nc = build()
    in_maps = []
    for b in range(8):
        d = dict(shared)
        d["x"] = np.ascontiguousarray(x[b])
        in_maps.append(d)
    res = run_bass_kernel_spmd(nc, in_maps, core_ids=list(range(8)))
    return np.stack([np.asarray(r["y"], dtype=np.float32) for r in res.results], axis=0)
```
